# Optimizing a Trainium2 kernel written in Bass

```python
import jax, jax.numpy as jnp
from jax import lax
import numpy as np

D_MODEL = 1024
BATCH = 8
SEQ = 2048
DEPTH = 4
DEC_BATCH = 128
DEC_SEQ = 4
PAST_LEN = 16384
PAGE_SIZE = 128

N_META = 16
HGRN_HEADS = 8
HGRN_DK = 128
HGRN_DV = D_MODEL // HGRN_HEADS
QK_W = HGRN_HEADS * HGRN_DK
V_W = HGRN_HEADS * HGRN_DV
CHUNK = 64
PAD_FRONT = CHUNK - N_META
CONV_K = 3
CONV_W = 1024
D_FF = 2816
IN_COLS = 2 * QK_W + 2 * V_W + 3 * CONV_W + 2 * D_MODEL
ALPHA = (2 * DEPTH) ** 0.25
BETA = (8 * DEPTH) ** -0.25
LN_EPS = 1e-5
RMS_EPS = 1e-6

kernel_name = "hgrn2_shortconv_gated_hybrid_step"


def _layernorm(x, g, b):
    xf = x.astype(jnp.float32)
    mu = jnp.mean(xf, axis=-1, keepdims=True)
    var = jnp.mean(jnp.square(xf - mu), axis=-1, keepdims=True)
    y = (xf - mu) * lax.rsqrt(var + LN_EPS) * g.astype(jnp.float32) + b.astype(jnp.float32)
    return y.astype(x.dtype)


def _swiglu(x, w_gu, w_down):
    gu = jnp.einsum('btd,df->btf', x, w_gu)
    g, u = jnp.split(gu, 2, axis=-1)
    return jnp.einsum('btf,fd->btd', jax.nn.silu(g) * u, w_down)


def _gla_chunk(S, qkvg):
    q, k, v, g = qkvg
    C = q.shape[2]
    b = jnp.cumsum(g, axis=2)
    causal = jnp.tril(jnp.ones((C, C), dtype=bool))
    diff = b[:, :, :, None, :] - b[:, :, None, :, :]
    decay = jnp.exp(jnp.where(causal[None, None, :, :, None], diff, -jnp.inf))
    scores = jnp.einsum('bhtd,bhsd,bhtsd->bhts', q, k, decay)
    o = jnp.einsum('bhts,bhsv->bhtv', scores, v) + jnp.einsum('bhtd,bhdv->bhtv', q * jnp.exp(b), S)
    b_last = b[:, :, -1:, :]
    S_new = jnp.exp(b_last[:, :, 0, :, None]) * S + jnp.einsum('bhsd,bhsv->bhdv', k * jnp.exp(b_last - b), v)
    return S_new, o


def _hgrn_recurrence(q, k, v, log_f, S0, chunk):
    Bn, H, T, _ = q.shape
    nc = T // chunk

    def split(a):
        return jnp.moveaxis(a.reshape(Bn, H, nc, chunk, a.shape[-1]), 2, 0)

    S, o = lax.scan(_gla_chunk, S0, (split(q), split(k), split(v), split(log_f)))
    o = jnp.moveaxis(o, 0, 2).reshape(Bn, H, T, HGRN_DV)
    return o, S


def _heads(a, d, pad):
    a = a.reshape(a.shape[0], a.shape[1], HGRN_HEADS, d)
    a = jnp.pad(a, ((0, 0), (pad, 0), (0, 0), (0, 0)))
    return a.transpose(0, 2, 1, 3)


def _token_mixer(x, S0, conv_buf, lb, pad, chunk, w_in, norm_w, conv_w, w_a, w_b, w_o):
    Bn, T, _ = x.shape
    z = jnp.einsum('btd,dc->btc', x, w_in)
    idx = [int(i) for i in np.cumsum([QK_W, QK_W, V_W, V_W, CONV_W, CONV_W, CONV_W, D_MODEL])]
    q, f_pre, v, og, b_gate, c_gate, h, gate_a, gate_b = jnp.split(z, idx, axis=-1)

    lb = lb.astype(jnp.float32)
    log_f = jnp.logaddexp(jnp.log(lb), jnp.log1p(-lb) + jax.nn.log_sigmoid(f_pre.astype(jnp.float32)))
    k = -jnp.expm1(log_f)
    q = jax.nn.silu(q.astype(jnp.float32))
    o, S_new = _hgrn_recurrence(_heads(q, HGRN_DK, pad), _heads(k, HGRN_DK, pad),
                                _heads(v.astype(jnp.float32), HGRN_DV, pad), _heads(log_f, HGRN_DK, pad),
                                S0.astype(jnp.float32), chunk)
    o = o[:, :, pad:].transpose(0, 2, 1, 3)
    o = o * lax.rsqrt(jnp.mean(jnp.square(o), axis=-1, keepdims=True) + RMS_EPS) * norm_w.astype(jnp.float32)
    og = og.reshape(Bn, T, HGRN_HEADS, HGRN_DV).astype(jnp.float32)
    o = (o * jax.nn.silu(og)).reshape(Bn, T, V_W).astype(x.dtype)
    y_a = jnp.einsum('btc,cd->btd', o, w_a)

    u = c_gate * h
    uu = jnp.concatenate([conv_buf.astype(u.dtype), u], axis=1)
    conv = conv_w[0] * uu[:, 0:T]
    for j in range(1, CONV_K):
        conv = conv + conv_w[j] * uu[:, j:j + T]
    new_buf = uu[:, T:]
    y_b = jnp.einsum('btc,cd->btd', b_gate * conv, w_b)

    m = jax.nn.sigmoid(gate_a) * y_a + jax.nn.sigmoid(gate_b) * y_b
    out = jnp.einsum('btd,de->bte', m, w_o)
    return out, S_new.astype(x.dtype), new_buf


def _trunk(x, S_init, buf_init, pad, chunk, lb_all, p):
    new_S, new_buf = [], []
    for l in range(DEPTH):
        x = _layernorm(ALPHA * x + 0.5 * _swiglu(x, p['ffn1_w_gu'][l], p['ffn1_w_down'][l]),
                       p['ln_g'][l, 0], p['ln_b'][l, 0])
        m, S, buf = _token_mixer(x, S_init[l], buf_init[l], lb_all[l], pad, chunk,
                                 p['w_in'][l], p['hgrn_norm_w'][l], p['conv_w'][l],
                                 p['w_branch_a'][l], p['w_branch_b'][l], p['w_out'][l])
        x = _layernorm(ALPHA * x + m, p['ln_g'][l, 1], p['ln_b'][l, 1])
        x = _layernorm(ALPHA * x + 0.5 * _swiglu(x, p['ffn2_w_gu'][l], p['ffn2_w_down'][l]),
                       p['ln_g'][l, 2], p['ln_b'][l, 2])
        new_S.append(S)
        new_buf.append(buf)
    return x, jnp.stack(new_S), jnp.stack(new_buf)


def setup_inputs(seed: int = 0) -> dict:
    key = jax.random.key(seed)
    ks = jax.random.split(key, 20)
    nrm = jax.random.normal
    f32 = jnp.float32
    return {
        "x_prompt": nrm(ks[0], (BATCH, SEQ, D_MODEL), f32),
        "x_sample": nrm(ks[1], (DEC_BATCH, DEC_SEQ, D_MODEL), f32),
        "state_hgrn": 0.3 * nrm(ks[2], (DEPTH, DEC_BATCH, HGRN_HEADS, HGRN_DK, HGRN_DV), f32),
        "state_conv": nrm(ks[3], (DEPTH, DEC_BATCH, CONV_K - 1, CONV_W), f32),
        "meta_tokens": nrm(ks[4], (N_META, D_MODEL), f32),
        "w_in": nrm(ks[5], (DEPTH, D_MODEL, IN_COLS), f32) * D_MODEL ** -0.5,
        "hgrn_lb_logits": 0.1 * nrm(ks[6], (DEPTH, QK_W), f32),
        "hgrn_norm_w": 1.0 + 0.02 * nrm(ks[7], (DEPTH, HGRN_DV), f32),
        "conv_w": nrm(ks[8], (DEPTH, CONV_K, CONV_W), f32) * CONV_K ** -0.5,
        "w_branch_a": nrm(ks[9], (DEPTH, V_W, D_MODEL), f32) * (V_W ** -0.5 * BETA),
        "w_branch_b": nrm(ks[10], (DEPTH, CONV_W, D_MODEL), f32) * (CONV_W ** -0.5 * BETA),
        "w_out": nrm(ks[11], (DEPTH, D_MODEL, D_MODEL), f32) * (D_MODEL ** -0.5 * BETA),
        "ffn1_w_gu": nrm(ks[12], (DEPTH, D_MODEL, 2 * D_FF), f32) * D_MODEL ** -0.5,
        "ffn1_w_down": nrm(ks[13], (DEPTH, D_FF, D_MODEL), f32) * (D_FF ** -0.5 * BETA),
        "ffn2_w_gu": nrm(ks[14], (DEPTH, D_MODEL, 2 * D_FF), f32) * D_MODEL ** -0.5,
        "ffn2_w_down": nrm(ks[15], (DEPTH, D_FF, D_MODEL), f32) * (D_FF ** -0.5 * BETA),
        "ln_g": 1.0 + 0.02 * nrm(ks[16], (DEPTH, 3, D_MODEL), f32),
        "ln_b": 0.02 * nrm(ks[17], (DEPTH, 3, D_MODEL), f32),
    }


def reference(x_prompt, x_sample, state_hgrn, state_conv, meta_tokens, w_in, hgrn_lb_logits,
              hgrn_norm_w, conv_w, w_branch_a, w_branch_b, w_out, ffn1_w_gu, ffn1_w_down,
              ffn2_w_gu, ffn2_w_down, ln_g, ln_b):
    p = dict(w_in=w_in, hgrn_norm_w=hgrn_norm_w, conv_w=conv_w, w_branch_a=w_branch_a,
             w_branch_b=w_branch_b, w_out=w_out, ffn1_w_gu=ffn1_w_gu, ffn1_w_down=ffn1_w_down,
             ffn2_w_gu=ffn2_w_gu, ffn2_w_down=ffn2_w_down, ln_g=ln_g, ln_b=ln_b)
    lb_all = jnp.cumsum(jax.nn.softmax(hgrn_lb_logits.astype(jnp.float32), axis=0), axis=0)
    lb_all = lb_all - lb_all[0:1]

    bp = x_prompt.shape[0]
    meta = jnp.broadcast_to(meta_tokens[None].astype(x_prompt.dtype), (bp, N_META, D_MODEL))
    xp = jnp.concatenate([meta, x_prompt], axis=1)
    S0p = jnp.zeros((DEPTH, bp, HGRN_HEADS, HGRN_DK, HGRN_DV), x_prompt.dtype)
    buf0p = jnp.zeros((DEPTH, bp, CONV_K - 1, CONV_W), x_prompt.dtype)
    yp, new_hgrn_prompt, new_conv_prompt = _trunk(xp, S0p, buf0p, PAD_FRONT, CHUNK, lb_all, p)
    y_prompt = yp[:, N_META:]

    y_sample, new_hgrn_sample, new_conv_sample = _trunk(x_sample, state_hgrn, state_conv, 0,
                                                        x_sample.shape[1], lb_all, p)
    return (y_prompt, y_sample, new_hgrn_prompt, new_conv_prompt, new_hgrn_sample, new_conv_sample)
```

```python
import numpy as np
import concourse.bass as bass
import concourse.mybir as mybir
from concourse.bass_utils import run_bass_kernel_spmd

F32 = mybir.dt.float32
BF16 = mybir.dt.bfloat16
F32R = mybir.dt.float32r
AF = mybir.ActivationFunctionType
ALU = mybir.AluOpType

D = 1024
DFF = 2816
DEPTH = 4
NCK = 8
NJ = 22
NTOK = 2128
GROUPS = [(0, 720), (720, 704), (1424, 704)]
GT = 720
ALPHA = (2 * DEPTH) ** 0.25
CR = 0.5 / ALPHA
LN_EPS_P = 1e-5 / (ALPHA * ALPHA)
RMS_EPS = 1e-6
SLOT = 4096
NSLOT = 3
UBYTES = 90112

DEBUG = {"layers": DEPTH, "groups": 3, "stop": None}


class Tr:
    ENGS = ("pe", "act", "dve", "pool", "sp")

    def __init__(self):
        self.ops = {n: [] for n in self.ENGS}
        self.waited = {n: {} for n in self.ENGS}
        self.cells = {}
        self.dval = {}
        self.live = []
        self.ninst = 0

    def cell(self, key):
        c = self.cells.get(key)
        if c is None:
            pend = {}
            if isinstance(key, tuple) and isinstance(key[0], UInst):
                assert key[0].alive, ("access to retired buffer", key[0].name)
                pend = dict(key[0].pending)
            c = [{}, pend]
            self.cells[key] = c
        return c

    def op(self, eng, fn, reads=(), writes=(), dma_sem=None, ndma=1):
        deps = {}

        def merge(d):
            for k, v in d.items():
                if deps.get(k, -1) < v:
                    deps[k] = v
        for c in reads:
            merge(self.cell(c)[0])
        for c in writes:
            cc = self.cell(c)
            merge(cc[0])
            merge(cc[1])
        if dma_sem is not None:
            prev = self.dval.get(dma_sem, 0)
            if prev > 0:
                deps[("d", dma_sem)] = max(deps.get(("d", dma_sem), -1), prev)
        waits = []
        wd = self.waited[eng]
        for k, v in deps.items():
            if k == ("e", "pe") and eng == "pe":
                continue
            if wd.get(k, -1) >= v:
                continue
            wd[k] = v
            waits.append((k, v))
            if k[0] == "e":
                self.ops[k[1]][v]["signal"] = True
        idx = len(self.ops[eng])
        if dma_sem is not None:
            self.dval[dma_sem] = self.dval.get(dma_sem, 0) + 16 * ndma
            ev = (("d", dma_sem), self.dval[dma_sem])
        else:
            ev = (("e", eng), idx)
        self.ops[eng].append({"fn": fn, "waits": waits, "signal": False, "dma": dma_sem})
        for c in reads:
            r = self.cell(c)[1]
            if r.get(ev[0], -1) < ev[1]:
                r[ev[0]] = ev[1]
        for c in writes:
            self.cells[c] = [{ev[0]: ev[1]}, {}]
        return ev


class UInst:
    def __init__(self, tr, name, off, nbytes):
        self.name, self.off, self.nbytes = name, off, nbytes
        self.alive = True
        self.pending = {}
        keep = []
        for o in tr.live:
            if o.off < off + nbytes and off < o.off + o.nbytes:
                if o.alive:
                    o.alive = False
                    for k, c in list(tr.cells.items()):
                        if isinstance(k, tuple) and k[0] is o:
                            for d in (c[0], c[1]):
                                for kk, vv in d.items():
                                    if o.pending.get(kk, -1) < vv:
                                        o.pending[kk] = vv
                            del tr.cells[k]
                for kk, vv in o.pending.items():
                    if self.pending.get(kk, -1) < vv:
                        self.pending[kk] = vv
                if not (off <= o.off and o.off + o.nbytes <= off + nbytes):
                    keep.append(o)
            else:
                keep.append(o)
        keep.append(self)
        tr.live = keep


def group_chunks(g):
    if g == 0:
        return [(0, 16)] + [(16 + 64 * i, 64) for i in range(11)]
    if g == 1:
        return [(64 * i, 64) for i in range(11)]
    return [(64 * i, 64) for i in range(10)]


def build_program():
    nc = bass.Bass("TRN2", target_bir_lowering=False)
    NL = DEBUG["layers"]
    NG = DEBUG["groups"]

    def din(name, shape):
        return nc.dram_tensor(name, shape, F32, kind="ExternalInput").ap()

    def dout(name, shape):
        return nc.dram_tensor(name, shape, F32, kind="ExternalOutput").ap()

    xT = din("xT", [128, 8, NTOK])
    s0 = din("s0", [4, 16, 8, 128, 128])
    cb = din("cb", [128, 4, 8, 16, 2])
    w_in = din("w_in", [4, D, 9 * D])
    w_a = din("w_a", [4, D, D])
    w_b = din("w_b", [4, D, D])
    w_o = din("w_o", [4, D, D])
    fgu = [din("f1gu", [4, D, 2 * DFF]), din("f2gu", [4, D, 2 * DFF])]
    fdn = [din("f1d", [4, 8, 128, NJ, 128]), din("f2d", [4, 8, 128, NJ, 128])]
    plng = din("plng", [128, 4, 3, 8])
    plnb = din("plnb", [128, 4, 3, 8])
    plog = din("plog", [128, 4, 8])
    pconv = din("pconv", [128, 4, 3, 8])
    pnorm = din("pnorm", [128, 4])
    cident = din("cident", [128, 128])
    cmask = din("cmask", [64, 2, 512])
    crow = din("crow", [64, 16])
    cE = din("cE", [128, 3, GT])

    yT = dout("yT", [128, 8, NTOK - 16])
    hs = dout("hs", [4, 16, 8, 128, 128])
    hp = dout("hp", [4, 8, 128, 128])
    cso = dout("cso", [128, 4, 8, 16, 2])
    cpo = dout("cpo", [128, 4, 8, 2])

    tr = Tr()
    sb = {}
    ctxs = []

    def salloc(name, shape, dt):
        cm = nc.sbuf_tensor(name, shape, dt)
        t = cm.__enter__()
        ctxs.append(cm)
        sb[name] = t
        return t

    xf = salloc("xf", [128, 8, GT], F32)
    xb = salloc("xb", [128, 8, GT], BF16)
    S = salloc("S", [128, 8, 128], F32)
    Sb = salloc("Sb", [128, 8, 128], BF16)
    W = salloc("W", [128, NSLOT, SLOT], BF16)
    U = salloc("U", [128, UBYTES // 4], F32)
    identf = salloc("identf", [128, 128], F32)
    identb = salloc("identb", [128, 128], BF16)
    ones1k = salloc("ones1k", [128, 128], BF16)
    ones128 = salloc("ones128", [128, 128], BF16)
    zerob = salloc("zerob", [128, 128], BF16)
    lng = salloc("lng", [128, 4, 3, 8], F32)
    lnb = salloc("lnb", [128, 4, 3, 8], F32)
    lgt = salloc("lgt", [128, 4, 8], F32)
    lbt = salloc("lbt", [128, 4, 8], F32)
    c0t = salloc("c0t", [128, 4, 8], F32)
    c1t = salloc("c1t", [128, 4, 8], F32)
    sm8 = salloc("sm8", [128, 3, 8], F32)
    convw = salloc("convw", [128, 4, 3, 8], F32)
    normw = salloc("normw", [128, 4], F32)
    maskc = salloc("maskc", [64, 2, 512], F32)
    rowm = salloc("rowm", [64, 16], F32)
    Et = salloc("Et", [128, GT], F32)
    nEt = salloc("nEt", [128, GT], F32)
    ubuf = salloc("ubuf", [128, 4, 8, 2], F32)
    DEC = salloc("DEC", [128, 8, 12], F32)
    DECS = salloc("DECS", [128, 8, 16], F32)
    CB = salloc("CB", [128, 4, 8, 16, 2], F32)
    US = salloc("US", [128, 8, 16, 6], F32)
    CVS = salloc("CVS", [128, 8, 16, 4], F32)
    CVS2 = salloc("CVS2", [128, 8, 16, 4], F32)
    QTF = salloc("QTF", [128, 8, 64], F32)
    epsl = salloc("epsl", [128, 1], F32)
    epsr = salloc("epsr", [128, 1], F32)
    CSO = salloc("CSO", [128, 4, 8, 16, 2], F32)
    CPO = salloc("CPO", [128, 4, 8, 2], F32)

    pst = []
    for i in range(4):
        cm = nc.psum_tensor("ps%d" % i, [128, 1024], F32)
        pst.append(cm.__enter__())
        ctxs.append(cm)

    def bank(b, n=512):
        return pst[b // 2][:, (b % 2) * 512:(b % 2) * 512 + n]

    def bank_bf(b):
        return pst[b // 2][:, (b % 2) * 512:(b % 2) * 512 + 512].bitcast(BF16)

    bank_rr = [0]
    held = set()

    def alloc_bank():
        while True:
            b = bank_rr[0]
            bank_rr[0] = (b + 1) % 8
            if b not in held:
                return b

    def alloc_bank2():
        while True:
            if bank_rr[0] % 2:
                bank_rr[0] = (bank_rr[0] + 1) % 8
            b = bank_rr[0]
            bank_rr[0] = (b + 2) % 8
            if b not in held and (b + 1) not in held:
                return b

    def uview(off, dt, shape):
        esz = 2 if dt == BF16 else 4
        n = int(np.prod(shape))
        assert off % 4 == 0 and off + n * esz <= UBYTES, (off, shape)
        v = U[:, off // 4: off // 4 + (n * esz) // 4]
        if dt != F32:
            v = v.bitcast(dt)
        if len(shape) == 2:
            return v.rearrange("p (a b) -> p a b", a=shape[0], b=shape[1])
        if len(shape) == 3:
            return v.rearrange("p (a b c) -> p a b c", a=shape[0], b=shape[1], c=shape[2])
        return v

    class UB:
        def __init__(self, name, off, dt, shape):
            esz = 2 if dt == BF16 else 4
            self.inst = UInst(tr, name, off, int(np.prod(shape)) * esz)
            self.v = uview(off, dt, shape)

        def c(self, *key):
            return (self.inst,) + key

    wstate = {"n": 0}

    def load_block(parts, nk):
        i = wstate["n"]
        wstate["n"] += 1
        slot = i % NSLOT
        ncols = sum(p[2] for p in parts)
        assert nk * ncols <= SLOT
        wv = W[:, slot, 0:nk * ncols].rearrange("p (k c) -> p k c", k=nk, c=ncols)

        def fn(e, parts=parts, wv=wv):
            out = []
            for (src, co, ncp) in parts:
                sv = src if len(src.shape) == 3 else src.rearrange("(k p) c -> p k c", p=128)
                out.append(e.dma_start(out=wv[:, :, co:co + ncp], in_=sv))
            return out
        tr.op("pool", fn, reads=(), writes=[("w", slot)], dma_sem="w%d" % slot, ndma=len(parts))
        return slot, wv

    pending_blocks = []

    def block_sched2():
        for g in range(NG):
            for l in range(NL):
                for phase in ("ffn1", "mixer", "ffn2"):
                    if phase == "mixer":
                        wl = w_in[l]

                        def seg(sg, hh):
                            return [(wl[:, 1024 * sg + 512 * hh: 1024 * sg + 512 * hh + 512], 0, 512)]
                        for hh in (0, 1):
                            yield (seg(1, hh), 8)
                            yield (seg(2, hh), 8)
                            yield (seg(0, hh), 8)
                        for hh in (0, 1):
                            yield (seg(3, hh), 8)
                        for hh in (0, 1):
                            yield (seg(7, hh), 8)
                        for hh in (0, 1):
                            yield ([(w_a[l][:, 512 * hh: 512 * hh + 512], 0, 512)], 8)
                        for j in range(4):
                            yield ([(wl[:, 5120 + 256 * j: 5120 + 256 * j + 256], 0, 256),
                                    (wl[:, 6144 + 256 * j: 6144 + 256 * j + 256], 256, 256)], 8)
                        for hh in (0, 1):
                            yield (seg(8, hh), 8)
                        for hh in (0, 1):
                            yield (seg(4, hh), 8)
                        for hh in (0, 1):
                            yield ([(w_b[l][:, 512 * hh: 512 * hh + 512], 0, 512)], 8)
                        for hh in (0, 1):
                            yield ([(w_o[l][:, 512 * hh: 512 * hh + 512], 0, 512)], 8)
                    else:
                        fi = 0 if phase == "ffn1" else 1
                        gu = fgu[fi][l]
                        dn = fdn[fi][l]
                        for j in range(11):
                            yield ([(gu[:, 256 * j: 256 * j + 256], 0, 256),
                                    (gu[:, DFF + 256 * j: DFF + 256 * j + 256], 256, 256)], 8)
                        for rep in range(2):
                            for c in range(8):
                                yield ([(dn[c], 0, 128)], NJ)

    sched = block_sched2()
    prefetched = []

    def prefetch(maxn=99):
        k = 0
        while len(prefetched) < NSLOT and k < maxn:
            k += 1
            try:
                parts, nk = next(sched)
            except StopIteration:
                return
            prefetched.append(load_block(parts, nk))

    def next_block():
        if not prefetched:
            prefetch()
        return prefetched.pop(0)

    def mm_group(out_ap, pairs, reads, writes, first_start=True):
        def fn(e, out_ap=out_ap, pairs=pairs, first_start=first_start):
            ins = None
            n = len(pairs)
            for i, (l_, r_) in enumerate(pairs):
                ins = e.matmul(out_ap, lhsT=l_, rhs=r_, start=(first_start and i == 0), stop=(i == n - 1))
            return [ins]
        tr.op("pe", fn, reads=reads, writes=writes)

    def act(out, in_, func, reads, writes, bias=None, scale=None):
        kw = {}
        if bias is not None:
            kw["bias"] = bias
        if scale is not None:
            kw["scale"] = scale

        def fn(e):
            return [e.activation(out=out, in_=in_, func=func, **kw)]
        tr.op("act", fn, reads=reads, writes=writes)

    def dve(kind, reads, writes, eng="dve", **kw):
        def fn(e):
            return [getattr(e, kind)(**kw)]
        tr.op(eng, fn, reads=reads, writes=writes)

    def dma(eng, out, in_, reads, writes, sem):
        def fn(e):
            return [e.dma_start(out=out, in_=in_)]
        tr.op(eng, fn, reads=reads, writes=writes, dma_sem=sem)

    spsem = [0]

    def sp_dma(out, in_, reads, writes, nsem=4, base="io"):
        s = "%s%d" % (base, spsem[0] % nsem)
        spsem[0] += 1
        dma("sp", out, in_, reads, writes, s)

    def load_const(t, src, key):
        sp_dma(t[:] if not isinstance(t, bass.AP) else t, src, [], [key])

    load_const(identf, cident, "identf")
    load_const(lng, plng, "lng")
    load_const(lnb, plnb, "lnb")
    load_const(lgt, plog, "lgt")
    load_const(convw, pconv, "convw")
    load_const(normw, pnorm, "normw")
    load_const(maskc, cmask, "maskc")
    load_const(rowm, crow, "rowm")
    load_const(CB, cb, "CB")
    dve("tensor_copy", ["identf"], ["identb"], out=identb[:], in_=identf[:])
    dve("memset", [], ["ones1k"], ap=ones1k[:], constant=1.0 / 1024.0)
    dve("memset", [], ["ones128"], ap=ones128[:], constant=1.0 / 128.0)
    dve("memset", [], ["zerob"], ap=zerob[:], constant=0.0)
    dve("memset", [], ["epsl"], ap=epsl[:], constant=LN_EPS_P)
    dve("memset", [], ["epsr"], ap=epsr[:], constant=RMS_EPS)
    dve("memset", [], ["ubuf"], ap=ubuf[:], constant=0.0)
    dve("tensor_tensor", ["lgt"], ["sm8"], out=sm8[:, 0, :], in0=lgt[:, 0, :], in1=lgt[:, 1, :], op=ALU.max)
    dve("tensor_tensor", ["lgt", "sm8"], ["sm8"], out=sm8[:, 0, :], in0=sm8[:, 0, :], in1=lgt[:, 2, :], op=ALU.max)
    dve("tensor_tensor", ["lgt", "sm8"], ["sm8"], out=sm8[:, 0, :], in0=sm8[:, 0, :], in1=lgt[:, 3, :], op=ALU.max)
    dve("tensor_tensor", ["lgt", "sm8"], ["lbt"], out=lbt[:], in0=lgt[:],
        in1=sm8[:, 0:1, :].broadcast_to([128, 4, 8]), op=ALU.subtract)
    act(lbt[:], lbt[:], AF.Exp, ["lbt"], ["lbt"])
    dve("tensor_tensor", ["lbt"], ["sm8"], out=sm8[:, 1, :], in0=lbt[:, 0, :], in1=lbt[:, 1, :], op=ALU.add)
    dve("tensor_tensor", ["lbt", "sm8"], ["sm8"], out=sm8[:, 1, :], in0=sm8[:, 1, :], in1=lbt[:, 2, :], op=ALU.add)
    dve("tensor_tensor", ["lbt", "sm8"], ["sm8"], out=sm8[:, 1, :], in0=sm8[:, 1, :], in1=lbt[:, 3, :], op=ALU.add)
    dve("reciprocal", ["sm8"], ["sm8"], out=sm8[:, 2, :], in_=sm8[:, 1, :])
    dve("tensor_tensor", ["lbt", "sm8"], ["lgt"], out=lgt[:], in0=lbt[:],
        in1=sm8[:, 2:3, :].broadcast_to([128, 4, 8]), op=ALU.mult)
    dve("memset", [], ["lbt"], ap=lbt[:, 0, :], constant=0.0)
    dve("tensor_copy", ["lgt", "lbt"], ["lbt"], out=lbt[:, 1, :], in_=lgt[:, 1, :])
    dve("tensor_tensor", ["lgt", "lbt"], ["lbt"], out=lbt[:, 2, :], in0=lbt[:, 1, :], in1=lgt[:, 2, :], op=ALU.add)
    dve("tensor_tensor", ["lgt", "lbt"], ["lbt"], out=lbt[:, 3, :], in0=lbt[:, 2, :], in1=lgt[:, 3, :], op=ALU.add)
    dve("tensor_scalar", ["lbt"], ["c1t"], out=c1t[:], in0=lbt[:], scalar1=-0.5, scalar2=0.5,
        op0=ALU.mult, op1=ALU.add)
    dve("tensor_tensor", ["lbt", "c1t"], ["c0t"], out=c0t[:], in0=lbt[:], in1=c1t[:], op=ALU.add)

    prefetch()

    def xcells(name, s):
        return [(name, c, s) for c in range(8)]

    def ln_sub(l, k, subs, s, off_sq, off_yt, off_st):
        nmax = max(n for _, n in subs)
        t0, n = subs[s]
        SQ = UB("SQ", off_sq, BF16, [16, nmax])
        YT = UB("YT", off_yt, F32, [8, nmax])
        ST = UB("ST", off_st, F32, [4, nmax])
        dve("tensor_copy", xcells("xf", s), [SQ.c(0)], out=SQ.v[:, 0:8, 0:n], in_=xf[:, :, t0:t0 + n])
        act(SQ.v[:, 8:16, 0:n], xf[:, :, t0:t0 + n], AF.Square, xcells("xf", s), [SQ.c(1)])
        bm = alloc_bank()
        be = alloc_bank()
        mm_group(bank(bm, n), [(ones1k[:], SQ.v[:, c, 0:n]) for c in range(8)],
                 ["ones1k", SQ.c(0)], [("ps", bm)])
        mm_group(bank(be, n), [(ones1k[:], SQ.v[:, 8 + c, 0:n]) for c in range(8)],
                 ["ones1k", SQ.c(1)], [("ps", be)])
        act(ST.v[:, 0, 0:n], bank(bm, n), AF.Identity, [("ps", bm)], [ST.c(0)])
        act(ST.v[:, 1, 0:n], bank(bm, n), AF.Square, [("ps", bm)], [ST.c(1)])
        dve("tensor_tensor", [("ps", be), ST.c(1)], [ST.c(2)], out=ST.v[:, 2, 0:n], in0=bank(be, n),
            in1=ST.v[:, 1, 0:n], op=ALU.subtract)
        act(ST.v[:, 2, 0:n], ST.v[:, 2, 0:n], AF.Sqrt, [ST.c(2), "epsl"], [ST.c(2)], bias=epsl[:], scale=1.0)
        dve("tensor_tensor", xcells("xf", s) + [ST.c(0)], [YT.c(0)], out=YT.v[:, :, 0:n],
            in0=xf[:, :, t0:t0 + n], in1=ST.v[:, 0:1, 0:n].broadcast_to([128, 8, n]), op=ALU.subtract)
        dve("reciprocal", [ST.c(2)], [ST.c(3)], out=ST.v[:, 3, 0:n], in_=ST.v[:, 2, 0:n])
        dve("tensor_tensor", [YT.c(0), ST.c(3)], [YT.c(0)], out=YT.v[:, :, 0:n],
            in0=YT.v[:, :, 0:n], in1=ST.v[:, 3:4, 0:n].broadcast_to([128, 8, n]), op=ALU.mult)
        for c in range(8):
            act(xb[:, c, t0:t0 + n], YT.v[:, c, 0:n], AF.Identity, [YT.c(0), "lng", "lnb"], [("xb", c, s)],
                scale=lng[:, l, k, c:c + 1], bias=lnb[:, l, k, c:c + 1])
            dve("tensor_scalar", [YT.c(0), "lng", "lnb"], [("xf", c, s)], out=xf[:, c, t0:t0 + n],
                in0=YT.v[:, c, 0:n], scalar1=lng[:, l, k, c:c + 1], scalar2=lnb[:, l, k, c:c + 1],
                op0=ALU.mult, op1=ALU.add)

    def ffn(l, fi, subs):
        H = UB("H", 0, BF16, [NJ, GT])
        SG = UB("SG", 60480, BF16, [4, 360])
        sgi = [0]

        def gu_groups(j, slot, wv, s):
            t0, n = subs[s]
            for jj in range(2):
                bg = alloc_bank()
                bu = alloc_bank()
                mm_group(bank(bg, n), [(wv[:, k, jj * 128:(jj + 1) * 128], xb[:, k, t0:t0 + n]) for k in range(8)],
                         [("w", slot)] + xcells("xb", s), [("ps", bg)])
                mm_group(bank(bu, n), [(wv[:, k, 256 + jj * 128:256 + (jj + 1) * 128], xb[:, k, t0:t0 + n])
                                       for k in range(8)],
                         [("w", slot)] + xcells("xb", s), [("ps", bu)])
                sgv = SG.v[:, sgi[0] % 4, 0:n]
                sgc = SG.c(sgi[0] % 4)
                sgi[0] += 1
                act(sgv, bank(bg, n), AF.Silu, [("ps", bg)], [sgc])
                dve("tensor_tensor", [sgc, ("ps", bu)], [H.c(2 * j + jj, s)], out=H.v[:, 2 * j + jj, t0:t0 + n],
                    in0=bank(bu, n), in1=sgv, op=ALU.mult)
        sl0, wv0 = next_block()
        sl1, wv1 = next_block()
        gu_groups(0, sl0, wv0, 0)
        gu_groups(1, sl1, wv1, 0)
        gu_groups(0, sl0, wv0, 1)
        prefetch(1)
        gu_groups(1, sl1, wv1, 1)
        prefetch(1)
        for j in range(2, 11):
            slot, wv = next_block()
            for s in range(2):
                gu_groups(j, slot, wv, s)
            prefetch()
        kk = 0 if fi == 0 else 2
        for s, (t0, n) in enumerate(subs):
            for c in range(8):
                slot, wv = next_block()
                b = alloc_bank()
                mm_group(bank(b, n), [(wv[:, k, 0:128], H.v[:, k, t0:t0 + n]) for k in range(NJ)],
                         [("w", slot)] + [H.c(k, s) for k in range(NJ)], [("ps", b)])
                dve("scalar_tensor_tensor", [("ps", b), ("xf", c, s)], [("xf", c, s)], out=xf[:, c, t0:t0 + n],
                    in0=bank(b, n), scalar=CR, in1=xf[:, c, t0:t0 + n], op0=ALU.mult, op1=ALU.add)
                prefetch()
                if s == 1 and c == 1:
                    ln_sub(l, kk, subs, 0, 31680, 43200, 54720)
        ln_sub(l, kk, subs, 1, 31680, 43200, 54720)

    def dense8(out_chunks, subs, consumer):
        slot, wv = next_block()
        for s, (t0, n) in enumerate(subs):
            for i, co in enumerate(out_chunks):
                b = alloc_bank()
                mm_group(bank(b, n), [(wv[:, k, co:co + 128], xb[:, k, t0:t0 + n]) for k in range(8)],
                         [("w", slot)] + xcells("xb", s), [("ps", b)])
                consumer(i, s, t0, n, b)
        prefetch()

    def dense8_pair(subs, consA, consB):
        slA, wvA = next_block()
        slB, wvB = next_block()
        for s, (sl, wv, cons) in ((0, (slA, wvA, consA)), (0, (slB, wvB, consB)), (1, (slA, wvA, consA)),
                                  (1, (slB, wvB, consB))):
            t0, n = subs[s]
            for i in range(4):
                b = alloc_bank()
                mm_group(bank(b, n), [(wv[:, k, i * 128:(i + 1) * 128], xb[:, k, t0:t0 + n]) for k in range(8)],
                         [("w", sl)] + xcells("xb", s), [("ps", b)])
                cons(i, s, t0, n, b)
            if s == 1:
                prefetch(1)

    def dense8_from(src, srccells, subs, consumer):
        slot, wv = next_block()
        for s, (t0, n) in enumerate(subs):
            for i in range(4):
                b = alloc_bank()
                mm_group(bank(b, n), [(wv[:, k, i * 128:(i + 1) * 128], src.v[:, k, t0:t0 + n]) for k in range(8)],
                         [("w", slot)] + [srccells(k, s) for k in range(8)], [("ps", b)])
                consumer(i, s, t0, n, b)
        prefetch()

    def mixer(g, l, subs):
        G = GROUPS[g][1]
        chunks = group_chunks(g)
        has_sample = (g == 2)
        nch = len(chunks)
        Fb = UB("F", 0, F32, [4, GT])
        Pb = UB("P", 11520, F32, [4, GT])
        QT = UB("QT", 23040, BF16, [8, GT])
        NK = UB("NK", 34560, BF16, [8, GT])
        VT = UB("VT", 46080, BF16, [8, GT])
        A_ = UB("A", 57600, F32, [2, GT])
        B_ = UB("B", 63360, F32, [2, GT])
        R_ = UB("R", 69120, F32, [2, GT])
        QS = UB("QS", 74880, F32, [2, 360])
        if g == 0:
            dve("memset", [], ["S"], ap=S[:], constant=0.0)
        else:
            sp_dma(S[:], hp[l].rearrange("h d v -> d h v"), [("hp", l)], ["S"], base="st", nsem=2)
        act(Sb[:], S[:], AF.Copy, ["S"], ["Sb"])
        qsi = [0]
        vtog = [0]
        for hh in (0, 1):
            def cons_f(i, s, t0, n, b):
                act(Fb.v[:, i, t0:t0 + n], bank(b, n), AF.Tanh, [("ps", b)], [Fb.c(i, s)], scale=0.5)

            def cons_v(i, s, t0, n, b, hh=hh):
                h = 4 * hh + i
                act(VT.v[:, h, t0:t0 + n], bank(b, n), AF.Copy, [("ps", b)], [VT.c(h, s)])
            if hh == 0:
                dense8_pair(subs, cons_f, cons_v)
            else:
                dense8([0, 128, 256, 384], subs, cons_f)
            for i0 in (0, 2):
                pair = (i0, i0 + 1)

                def fcs(i):
                    return [Fb.c(i, 0), Fb.c(i, 1)]
                for i in pair:
                    h = 4 * hh + i
                    dve("tensor_scalar", fcs(i) + ["c0t", "c1t"], fcs(i), out=Fb.v[:, i, 0:G], in0=Fb.v[:, i, 0:G],
                        scalar1=c1t[:, l, h:h + 1], scalar2=c0t[:, l, h:h + 1], op0=ALU.mult, op1=ALU.add)
                for i in pair:
                    dve("tensor_tensor", fcs(i) + ["nEt"], [A_.c(i % 2)], out=A_.v[:, i % 2, 0:G], in0=Fb.v[:, i, 0:G],
                        in1=nEt[:, 0:G], op=ALU.mult)
                    dve("tensor_tensor", fcs(i) + ["Et"], [B_.c(i % 2)], out=B_.v[:, i % 2, 0:G], in0=Fb.v[:, i, 0:G],
                        in1=Et[:, 0:G], op=ALU.mult)
                for i in pair:
                    dve("tensor_tensor_scan", [A_.c(i % 2), B_.c(i % 2)], [Pb.c(i)], out=Pb.v[:, i, 0:G],
                        data0=A_.v[:, i % 2, 0:G], data1=B_.v[:, i % 2, 0:G], initial=0.0, op0=ALU.mult, op1=ALU.add)
                for i in pair:
                    dve("reciprocal", [Pb.c(i)], [R_.c(i % 2)], out=R_.v[:, i % 2, 0:G], in_=Pb.v[:, i, 0:G])
                for i in pair:
                    h = 4 * hh + i
                    dve("scalar_tensor_tensor", fcs(i) + [R_.c(i % 2)], [NK.c(h)], out=NK.v[:, h, 0:G],
                        in0=Fb.v[:, i, 0:G], scalar=1.0, in1=R_.v[:, i % 2, 0:G], op0=ALU.subtract, op1=ALU.mult)
                for i in pair:
                    h = 4 * hh + i
                    e0 = chunks[0][1] - 1
                    act(DEC[:, h, 0:nch], Pb.v[:, i, e0:e0 + 64 * (nch - 1) + 1:64], AF.Copy, [Pb.c(i)], [("DEC", h)])
                    if has_sample:
                        act(DECS[:, h, :], Pb.v[:, i, 643:704:4], AF.Copy, [Pb.c(i)], [("DECS", h)])
            if hh == 1:
                dense8([0, 128, 256, 384], subs, cons_v)

            def cons_q(i, s, t0, n, b, hh=hh):
                h = 4 * hh + i
                qv = QS.v[:, qsi[0] % 2, 0:n]
                qc = QS.c(qsi[0] % 2)
                qsi[0] += 1
                act(qv, bank(b, n), AF.Silu, [("ps", b)], [qc])
                dve("tensor_tensor", [qc, Pb.c(i)], [QT.c(h, s)], out=QT.v[:, h, t0:t0 + n], in0=qv,
                    in1=Pb.v[:, i, t0:t0 + n], op=ALU.mult)
                if has_sample and s == 1:
                    dve("tensor_tensor", [qc, Pb.c(i)], [("QTF", h)], out=QTF[:, h, :], in0=qv[:, n - 64:n],
                        in1=Pb.v[:, i, 640:704], op=ALU.mult)
            dense8([0, 128, 256, 384], subs, cons_q)
        OT = UB("OT", 0, F32, [8, GT])
        SC = UB("SC", 57600, BF16, [2, 512])
        KT = UB("KT", 59648, BF16, [2, 1024])
        VTT = UB("VTT", 63744, BF16, [2, 1024])
        KH = UB("KH", 67840, BF16, [2, 512])
        n2 = subs[0][1]

        def scell(buf, t0, L):
            ss = set()
            for (s, (a, n)) in enumerate(subs):
                if t0 < a + n and a < t0 + L:
                    ss.add(s)
            return [buf.c(h, s) for h in range(8) for s in ss]

        def chunk_common(ci, t0, L, mask_idx, dec_ap, dec_cells, par):
            qc = scell(QT, t0, L)
            kc = [NK.c(h) for h in range(8)]
            vc = scell(VT, t0, L)
            bs = alloc_bank()
            def fn_sc(e):
                ins = None
                for h in range(8):
                    ins = e.matmul(bank(bs)[0:L, h * 64:h * 64 + L], lhsT=NK.v[:, h, t0:t0 + L],
                                   rhs=QT.v[:, h, t0:t0 + L], start=True, stop=True)
                return [ins]
            tr.op("pe", fn_sc, reads=qc + kc, writes=[("ps", bs)])
            scv = SC.v[0:L, par, :].rearrange("p (h t) -> p h t", h=8, t=64)[:, :, 0:L]
            dve("tensor_tensor", [("ps", bs), "maskc"], [SC.c(par)], out=scv,
                in0=bank(bs)[0:L, :].rearrange("p (h t) -> p h t", h=8, t=64)[:, :, 0:L],
                in1=maskc[0:L, mask_idx, :].rearrange("p (h t) -> p h t", h=8, t=64)[:, :, 0:L], op=ALU.mult)
            khv = KH.v[:, par, :].rearrange("p (h t) -> p h t", h=8, t=64)[:, :, 0:L]
            dve("tensor_tensor", kc + dec_cells, [KH.c(par)], out=khv, in0=NK.v[:, :, t0:t0 + L], in1=dec_ap,
                op=ALU.mult)
            bk = alloc_bank()
            bv = alloc_bank()

            def fn_tk(e):
                ins = None
                for h in range(8):
                    ins = e.transpose(out=bank_bf(bk)[0:L, h * 128:(h + 1) * 128],
                                      in_=KH.v[:, par, h * 64:h * 64 + L], identity=identb[:])
                return [ins]
            tr.op("pe", fn_tk, reads=[KH.c(par), "identb"], writes=[("ps", bk)])

            def fn_tv(e):
                ins = None
                for h in range(8):
                    ins = e.transpose(out=bank_bf(bv)[0:L, h * 128:(h + 1) * 128],
                                      in_=VT.v[:, h, t0:t0 + L], identity=identb[:])
                return [ins]
            tr.op("pe", fn_tv, reads=vc + ["identb"], writes=[("ps", bv)])
            act(KT.v[0:L, par, :], bank_bf(bk)[0:L, :], AF.Copy, [("ps", bk)], [KT.c(par)])
            act(VTT.v[0:L, par, :], bank_bf(bv)[0:L, :], AF.Copy, [("ps", bv)], [VTT.c(par)])
            return qc

        def part_a(ci):
            t0, L = chunks[ci]
            dec_ap = DEC[:, :, ci:ci + 1].broadcast_to([128, 8, L])
            return chunk_common(ci, t0, L, 0, dec_ap, [("DEC", h) for h in range(8)], ci % 2)

        def part_b(ci, qc):
            t0, L = chunks[ci]
            par = ci % 2
            bsu = alloc_bank2()

            def fn_su(e, L=L, par=par, bsu=bsu):
                ins = None
                for h in range(8):
                    ins = e.matmul(pst[bsu // 2][:, h * 128:(h + 1) * 128], lhsT=KT.v[0:L, par, h * 128:(h + 1) * 128],
                                   rhs=VTT.v[0:L, par, h * 128:(h + 1) * 128], start=True, stop=True)
                return [ins]
            tr.op("pe", fn_su, reads=[KT.c(par), VTT.c(par)], writes=[("ps", bsu), ("ps", bsu + 1)])
            bo = alloc_bank()

            def fn_o(e, t0=t0, L=L, par=par, bo=bo):
                ins = None
                for h in range(8):
                    e.matmul(bank(bo)[:, h * 64:h * 64 + L], lhsT=Sb[:, h, :], rhs=QT.v[:, h, t0:t0 + L],
                             start=True, stop=False)
                    ins = e.matmul(bank(bo)[:, h * 64:h * 64 + L], lhsT=VTT.v[0:L, par, h * 128:(h + 1) * 128],
                                   rhs=SC.v[0:L, par, h * 64:h * 64 + L], start=False, stop=True)
                return [ins]
            tr.op("pe", fn_o, reads=["Sb", VTT.c(par), SC.c(par)] + qc, writes=[("ps", bo)])
            dve("tensor_tensor", ["S"] + [("DEC", h) for h in range(8)], ["S"], eng="pool", out=S[:], in0=S[:],
                in1=DEC[:, :, ci:ci + 1].broadcast_to([128, 8, 128]), op=ALU.mult)
            dve("tensor_tensor", ["S", ("ps", bsu), ("ps", bsu + 1)], ["S"], out=S[:], in0=S[:],
                in1=pst[bsu // 2][:, :].rearrange("p (h v) -> p h v", h=8, v=128), op=ALU.subtract)
            act(OT.v[:, :, t0:t0 + L], bank(bo).rearrange("p (h t) -> p h t", h=8, t=64)[:, :, 0:L], AF.Copy,
                [("ps", bo)], scell(OT, t0, L))
            if ci < nch - 1:
                dve("tensor_copy", ["S"], ["Sb"], eng="pool", out=Sb[:], in_=S[:])

        qcs = {0: part_a(0)}
        for ci in range(nch):
            if ci + 1 < nch:
                qcs[ci + 1] = part_a(ci + 1)
            part_b(ci, qcs[ci])
        sp_dma(hp[l].rearrange("h d v -> d h v"), S[:], ["S"], [("hp", l)], base="st", nsem=2)

        if has_sample:
            S0b = UB("S0b", 69888, F32, [2, 1024])
            SOb = UB("SOb", 78080, F32, [2, 1024])
            KTM = UB("KTM", 86272, BF16, [1, 1024])
            tS = 640
            parS = nch % 2
            dec_ap = DECS[:, :, :].unsqueeze(3).broadcast_to([128, 8, 16, 4])
            qcells = scell(QT, tS, 64)
            kc = [NK.c(h) for h in range(8)]
            vc = scell(VT, tS, 64)
            bs = alloc_bank()

            def fn_sc(e):
                ins = None
                for h in range(8):
                    ins = e.matmul(bank(bs)[0:64, h * 64:(h + 1) * 64], lhsT=NK.v[:, h, tS:tS + 64],
                                   rhs=QT.v[:, h, tS:tS + 64], start=True, stop=True)
                return [ins]
            tr.op("pe", fn_sc, reads=qcells + kc, writes=[("ps", bs)])
            dve("tensor_tensor", [("ps", bs), "maskc"], [SC.c(parS)], out=SC.v[0:64, parS, :], in0=bank(bs)[0:64, :],
                in1=maskc[:, 1, :], op=ALU.mult)
            dve("tensor_tensor", kc + [("DECS", h) for h in range(8)], [KH.c(parS)],
                out=KH.v[:, parS, :].rearrange("p (h q t) -> p h q t", h=8, q=16, t=4),
                in0=NK.v[:, :, tS:tS + 64].rearrange("p h (q t) -> p h q t", q=16, t=4), in1=dec_ap, op=ALU.mult)
            bk = alloc_bank()
            bv = alloc_bank()

            def fn_tk(e):
                ins = None
                for h in range(8):
                    ins = e.transpose(out=bank_bf(bk)[0:64, h * 128:(h + 1) * 128],
                                      in_=KH.v[:, parS, h * 64:(h + 1) * 64], identity=identb[:])
                return [ins]
            tr.op("pe", fn_tk, reads=[KH.c(parS), "identb"], writes=[("ps", bk)])

            def fn_tv(e):
                ins = None
                for h in range(8):
                    ins = e.transpose(out=bank_bf(bv)[0:64, h * 128:(h + 1) * 128],
                                      in_=VT.v[:, h, tS:tS + 64], identity=identb[:])
                return [ins]
            tr.op("pe", fn_tv, reads=vc + ["identb"], writes=[("ps", bv)])
            act(KT.v[0:64, parS, :], bank_bf(bk)[0:64, :], AF.Copy, [("ps", bk)], [KT.c(parS)])
            act(VTT.v[0:64, parS, :], bank_bf(bv)[0:64, :], AF.Copy, [("ps", bv)], [VTT.c(parS)])
            bo = alloc_bank()
            held.add(bo)

            def fn_o0(e):
                e.matmul(bank(bo), lhsT=zerob[:], rhs=xb[:, 0, 0:512], start=True, stop=False)
                ins = None
                for h in range(8):
                    ins = e.matmul(bank(bo)[:, h * 64:(h + 1) * 64], lhsT=VTT.v[0:64, parS, h * 128:(h + 1) * 128],
                                   rhs=SC.v[0:64, parS, h * 64:(h + 1) * 64], start=False, stop=False)
                return [ins]
            tr.op("pe", fn_o0, reads=["zerob", VTT.c(parS), SC.c(parS)] + xcells("xb", 0) + xcells("xb", 1),
                  writes=[("ps", bo)])
            for q in range(16):
                sp_dma(S0b.v[:, q % 2, :].rearrange("p (h v) -> p h v", h=8, v=128),
                       s0[l, q].rearrange("h d v -> d h v"), [], [S0b.c(q % 2)], base="si", nsem=2)

                def fn_oq(e, q=q):
                    ins = None
                    for h in range(8):
                        ins = e.matmul(bank(bo)[:, h * 64 + q * 4:h * 64 + q * 4 + 4],
                                       lhsT=S0b.v[:, q % 2, h * 128:(h + 1) * 128], rhs=QTF[:, h, q * 4:q * 4 + 4],
                                       start=False, stop=(q == 15 and h == 7))
                    return [ins]
                tr.op("pe", fn_oq, reads=[S0b.c(q % 2)] + [("QTF", h) for h in range(8)], writes=[("ps", bo)])
                dve("tensor_scalar", [KT.c(parS), "rowm"], [KTM.c(0)], out=KTM.v[0:64, 0, :], in0=KT.v[0:64, parS, :],
                    scalar1=rowm[:, q:q + 1], scalar2=None, op0=ALU.mult)
                bsu = alloc_bank2()

                def fn_su(e, bsu=bsu):
                    ins = None
                    for h in range(8):
                        ins = e.matmul(pst[bsu // 2][:, h * 128:(h + 1) * 128],
                                       lhsT=KTM.v[0:64, 0, h * 128:(h + 1) * 128],
                                       rhs=VTT.v[0:64, parS, h * 128:(h + 1) * 128], start=True, stop=True)
                    return [ins]
                tr.op("pe", fn_su, reads=[KTM.c(0), VTT.c(parS)], writes=[("ps", bsu), ("ps", bsu + 1)])
                sov = SOb.v[:, q % 2, :].rearrange("p (h v) -> p h v", h=8, v=128)
                dve("tensor_tensor", [S0b.c(q % 2)] + [("DECS", h) for h in range(8)], [SOb.c(q % 2)], out=sov,
                    in0=S0b.v[:, q % 2, :].rearrange("p (h v) -> p h v", h=8, v=128),
                    in1=DECS[:, :, q:q + 1].broadcast_to([128, 8, 128]), op=ALU.mult)
                dve("tensor_tensor", [SOb.c(q % 2), ("ps", bsu), ("ps", bsu + 1)], [SOb.c(q % 2)], out=sov, in0=sov,
                    in1=pst[bsu // 2][:, :].rearrange("p (h v) -> p h v", h=8, v=128), op=ALU.subtract)
                sp_dma(hs[l, q].rearrange("h d v -> d h v"), sov, [SOb.c(q % 2)], [], base="so", nsem=2)
            act(OT.v[:, :, tS:tS + 64], bank(bo).rearrange("p (h t) -> p h t", h=8, t=64), AF.Copy,
                [("ps", bo)], scell(OT, tS, 64))
            held.discard(bo)

        SQ2 = UB("SQ2", 23040, BF16, [4, 360])
        RS = UB("RS", 25920, F32, [4, 360])
        NRB = 4
        O2 = UB("O2", 34560, BF16, [8, GT])
        SGO = UB("SGO", 31680, F32, [2, 360])
        items = [(h, s) for h in range(8) for s in range(2)]

        def rms_sq(ii):
            h, s = items[ii]
            t0, n = subs[s]
            act(SQ2.v[:, ii % 4, 0:n], OT.v[:, h, t0:t0 + n], AF.Square, [OT.c(h, s)], [SQ2.c(ii % 4)])
        rms_sq(0)
        rms_sq(1)
        for ii, (h, s) in enumerate(items):
            t0, n = subs[s]
            p2 = ii % 4
            if ii + 2 < len(items):
                rms_sq(ii + 2)
            b = alloc_bank()
            mm_group(bank(b, n), [(ones128[:], SQ2.v[:, p2, 0:n])], ["ones128", SQ2.c(p2)], [("ps", b)])
            act(RS.v[:, p2, 0:n], bank(b, n), AF.Sqrt, [("ps", b), "epsr"], [RS.c(p2)], bias=epsr[:], scale=1.0)
            dve("reciprocal", [RS.c(p2)], [RS.c(p2)], out=RS.v[:, p2, 0:n], in_=RS.v[:, p2, 0:n])
            dve("tensor_tensor", [RS.c(p2), OT.c(h, s)], [OT.c(h, s)], out=OT.v[:, h, t0:t0 + n],
                in0=OT.v[:, h, t0:t0 + n], in1=RS.v[:, p2, 0:n], op=ALU.mult)
        sgi = [0]
        for hh in (0, 1):
            def cons_og(i, s, t0, n, b, hh=hh):
                h = 4 * hh + i
                p2 = sgi[0] % 2
                sgi[0] += 1
                act(SGO.v[:, p2, 0:n], bank(b, n), AF.Silu, [("ps", b)], [SGO.c(p2)])
                dve("scalar_tensor_tensor", [SGO.c(p2), OT.c(h, s), "normw"], [O2.c(h, s)],
                    out=O2.v[:, h, t0:t0 + n], in0=OT.v[:, h, t0:t0 + n], scalar=normw[:, l:l + 1],
                    in1=SGO.v[:, p2, 0:n], op0=ALU.mult, op1=ALU.mult)
            dense8([0, 128, 256, 384], subs, cons_og)

        TGA = UB("TGA", 46080, BF16, [8, GT])
        for hh in (0, 1):
            def cons_ga(i, s, t0, n, b, hh=hh):
                c = 4 * hh + i
                act(TGA.v[:, c, t0:t0 + n], bank(b, n), AF.Tanh, [("ps", b)], [TGA.c(c, s)], scale=0.5)
            dense8([0, 128, 256, 384], subs, cons_ga)
        MA = UB("MA", 0, F32, [8, GT])
        for hh in (0, 1):
            def cons_wa(i, s, t0, n, b, hh=hh):
                c = 4 * hh + i
                dve("scalar_tensor_tensor", [("ps", b), TGA.c(c, s)], [MA.c(c, s)], out=MA.v[:, c, t0:t0 + n],
                    in0=TGA.v[:, c, t0:t0 + n], scalar=1.0, in1=bank(b, n), op0=ALU.add, op1=ALU.mult)
            dense8_from(O2, lambda k, s: O2.c(k, s), subs, cons_wa)

        U2 = UB("U2", 23040, F32, [8, GT + 2])
        CV = UB("CV", 57664, F32, [8, GT])
        CGT = UB("CGT", 80704, F32, [2, 360])
        Gp = 640 if has_sample else G
        for c in range(8):
            pass
        dve("tensor_copy", ["ubuf"], [U2.c("halo")], out=U2.v[:, :, 0:2], in_=ubuf[:, l, :, :])
        cgi = [0]
        for j in range(4):
            def cons_ch(i, s, t0, n, b, j=j):
                c = 2 * j + (i % 2)
                if i < 2:
                    act(CGT.v[:, i, 0:n], bank(b, n), AF.Copy, [("ps", b)], [CGT.c(i)])
                else:
                    dve("tensor_tensor", [("ps", b), CGT.c(i - 2)], [U2.c(c, s)], out=U2.v[:, c, 2 + t0:2 + t0 + n],
                        in0=bank(b, n), in1=CGT.v[:, i - 2, 0:n], op=ALU.mult)
            dense8([0, 128, 256, 384], subs, cons_ch)
        for c in range(8):
            uc = [U2.c(c, 0), U2.c(c, 1), U2.c("halo")]
            dve("tensor_scalar", uc + ["convw"], [CV.c(c)], out=CV.v[:, c, 0:Gp], in0=U2.v[:, c, 0:Gp],
                scalar1=convw[:, l, 0, c:c + 1], scalar2=None, op0=ALU.mult)
            dve("scalar_tensor_tensor", uc + ["convw", CV.c(c)], [CV.c(c)], out=CV.v[:, c, 0:Gp],
                in0=U2.v[:, c, 1:Gp + 1], scalar=convw[:, l, 1, c:c + 1], in1=CV.v[:, c, 0:Gp],
                op0=ALU.mult, op1=ALU.add)
            dve("scalar_tensor_tensor", uc + ["convw", CV.c(c)], [CV.c(c)], out=CV.v[:, c, 0:Gp],
                in0=U2.v[:, c, 2:Gp + 2], scalar=convw[:, l, 2, c:c + 1], in1=CV.v[:, c, 0:Gp],
                op0=ALU.mult, op1=ALU.add)
        allu = [U2.c(c, s) for c in range(8) for s in range(2)] + [U2.c("halo")]
        allcv = [CV.c(c) for c in range(8)]
        dve("tensor_copy", allu, ["ubuf"], out=ubuf[:, l, :, :], in_=U2.v[:, :, Gp:Gp + 2])
        if has_sample:
            act(CPO[:, l, :, :], U2.v[:, :, Gp:Gp + 2], AF.Copy, allu, [("CPO", l)])
            dve("tensor_copy", ["CB"], ["US"], out=US[:, :, :, 0:2], in_=CB[:, l, :, :, :])
            dve("tensor_copy", allu + ["US"], ["US"], out=US[:, :, :, 2:6],
                in_=U2.v[:, :, 2 + 640:2 + 704].rearrange("p c (q t) -> p c q t", q=16, t=4))
            act(CSO[:, l, :, :, :], US[:, :, :, 4:6], AF.Copy, ["US"], [("CSO", l)])
            for jx in range(3):
                wbc = convw[:, l, jx, :].unsqueeze(2).unsqueeze(3).broadcast_to([128, 8, 16, 4])
                if jx == 0:
                    dve("tensor_tensor", ["US", "convw"], ["CVS"], out=CVS[:], in0=US[:, :, :, 0:4], in1=wbc,
                        op=ALU.mult)
                else:
                    dve("tensor_tensor", ["US", "convw"], ["CVS2"], out=CVS2[:], in0=US[:, :, :, jx:jx + 4], in1=wbc,
                        op=ALU.mult)
                    dve("tensor_tensor", ["CVS", "CVS2"], ["CVS"], out=CVS[:], in0=CVS[:], in1=CVS2[:], op=ALU.add)
            dve("tensor_copy", ["CVS"] + allcv, allcv, out=CV.v[:, :, 640:704],
                in_=CVS[:].rearrange("p c q t -> p c (q t)"))
        TGB = UB("TGB", 46144, BF16, [8, GT])
        for hh in (0, 1):
            def cons_gb(i, s, t0, n, b, hh=hh):
                c = 4 * hh + i
                act(TGB.v[:, c, t0:t0 + n], bank(b, n), AF.Tanh, [("ps", b)], [TGB.c(c, s)], scale=0.5)
            dense8([0, 128, 256, 384], subs, cons_gb)
        BC = UB("BC", 23040, BF16, [8, GT])
        for hh in (0, 1):
            def cons_bg(i, s, t0, n, b, hh=hh):
                c = 4 * hh + i
                dve("tensor_tensor", [("ps", b)] + allcv, [BC.c(c, s)], out=BC.v[:, c, t0:t0 + n], in0=bank(b, n),
                    in1=CV.v[:, c, t0:t0 + n], op=ALU.mult)
            dense8([0, 128, 256, 384], subs, cons_bg)
        M = UB("M", 34560, BF16, [8, GT])
        YTMP = UB("YTMP", 57664, F32, [2, 360])
        yi = [0]
        for hh in (0, 1):
            def cons_wb(i, s, t0, n, b, hh=hh):
                c = 4 * hh + i
                p2 = yi[0] % 2
                yi[0] += 1
                dve("scalar_tensor_tensor", [("ps", b), TGB.c(c, s)], [YTMP.c(p2)], out=YTMP.v[:, p2, 0:n],
                    in0=TGB.v[:, c, t0:t0 + n], scalar=1.0, in1=bank(b, n), op0=ALU.add, op1=ALU.mult)
                dve("tensor_tensor", [YTMP.c(p2), MA.c(c, s)], [M.c(c, s)], out=M.v[:, c, t0:t0 + n],
                    in0=YTMP.v[:, p2, 0:n], in1=MA.v[:, c, t0:t0 + n], op=ALU.add)
            dense8_from(BC, lambda k, s: BC.c(k, s), subs, cons_wb)
        def cons_wo(c, s, t0, n, b):
            dve("scalar_tensor_tensor", [("ps", b), ("xf", c, s)], [("xf", c, s)], out=xf[:, c, t0:t0 + n],
                in0=bank(b, n), scalar=CR, in1=xf[:, c, t0:t0 + n], op0=ALU.mult, op1=ALU.add)
        slA, wvA = next_block()
        slB, wvB = next_block()
        for s, (t0, n) in enumerate(subs):
            for hh, (sl, wv) in enumerate(((slA, wvA), (slB, wvB))):
                for i in range(4):
                    b = alloc_bank()
                    mm_group(bank(b, n), [(wv[:, k, i * 128:(i + 1) * 128], M.v[:, k, t0:t0 + n]) for k in range(8)],
                             [("w", sl)] + [M.c(k, s) for k in range(8)], [("ps", b)])
                    cons_wo(4 * hh + i, s, t0, n, b)
                if s == 1 and hh == 0:
                    ln_sub(l, 1, subs, 0, 46080, 57600, 69120)
        prefetch()
        ln_sub(l, 1, subs, 1, 46080, 57600, 69120)

    for g in range(NG):
        g0, G = GROUPS[g]
        n = G // 2
        subs = [(0, n), (n, G - n)]
        sp_dma(xf[:, :, 0:G], xT[:, :, g0:g0 + G], [], [("xf", c, s) for c in range(8) for s in range(2)])
        sp_dma(Et[:], cE[:, g, :], [], ["Et"])
        dve("tensor_scalar", ["Et"], ["nEt"], out=nEt[:], in0=Et[:], scalar1=-1.0, scalar2=1.0, op0=ALU.mult,
            op1=ALU.add)
        for s, (t0, nn) in enumerate(subs):
            act(xb[:, :, t0:t0 + nn], xf[:, :, t0:t0 + nn], AF.Copy, xcells("xf", s), xcells("xb", s))
        for l in range(NL):
            ffn(l, 0, subs)
            if DEBUG["stop"] == "ffn1":
                break
            mixer(g, l, subs)
            if DEBUG["stop"] == "mixer":
                break
            ffn(l, 1, subs)
        allxf = [("xf", c, s) for c in range(8) for s in range(2)]
        if g == 0:
            sp_dma(yT[:, :, 0:G - 16], xf[:, :, 16:G], allxf, [], base="yo", nsem=1)
        else:
            sp_dma(yT[:, :, g0 - 16:g0 - 16 + G], xf[:, :, 0:G], allxf, [], base="yo", nsem=1)
    if DEBUG["stop"] is None:
      sp_dma(cso[:, 0:NL], CSO[:, 0:NL], [("CSO", l) for l in range(NL)], [], base="yo", nsem=1)
      sp_dma(cpo[:, 0:NL], CPO[:, 0:NL], [("CPO", l) for l in range(NL)], [], base="yo", nsem=1)

    cum = {}
    for en in Tr.ENGS:
        c = 0
        arr = []
        for o in tr.ops[en]:
            if o["signal"]:
                c += 1
            arr.append(c)
        cum[en] = arr
    dsems = sorted(tr.dval.keys())
    semctx = {}
    for en in Tr.ENGS:
        cm = nc.semaphore("e_" + en)
        semctx[("e", en)] = cm.__enter__()
        ctxs.append(cm)
    for ds in dsems:
        cm = nc.semaphore("d_" + ds)
        semctx[("d", ds)] = cm.__enter__()
        ctxs.append(cm)

    def replay(en, e):
        my = semctx[("e", en)]
        for o in tr.ops[en]:
            for (k, v) in o["waits"]:
                if k[0] == "e":
                    e.wait_ge(semctx[k], cum[k[1]][v])
                else:
                    e.wait_ge(semctx[k], v)
            inss = o["fn"](e)
            if o["dma"] is not None:
                for ins in inss:
                    ins.then_inc(semctx[("d", o["dma"])], 16)
            elif o["signal"]:
                inss[-1].then_inc(my, 1)
        if en == "sp":
            for ds in dsems:
                e.wait_ge(semctx[("d", ds)], tr.dval[ds])

    with nc.Block() as block:
        @block.tensor
        def _(e):
            replay("pe", e)

        @block.scalar
        def _(e):
            replay("act", e)

        @block.vector
        def _(e):
            replay("dve", e)

        @block.gpsimd
        def _(e):
            replay("pool", e)

        @block.sync
        def _(e):
            replay("sp", e)
    for cm in reversed(ctxs):
        cm.__exit__(None, None, None)
    stats = {en: len(tr.ops[en]) for en in Tr.ENGS}
    return nc, stats


def host_consts():
    ident = np.eye(128, dtype=np.float32)
    s_ = np.arange(64)[:, None]
    t_ = np.arange(64)[None, :]
    mc = np.where(s_ <= t_, -1.0, 0.0).astype(np.float32)
    ms = np.where((s_ <= t_) & (s_ // 4 == t_ // 4), -1.0, 0.0).astype(np.float32)
    cmask = np.zeros((64, 2, 512), np.float32)
    cmask[:, 0, :] = np.tile(mc, (1, 8))
    cmask[:, 1, :] = np.tile(ms, (1, 8))
    crow = (np.arange(64)[:, None] // 4 == np.arange(16)[None, :]).astype(np.float32)
    E = np.zeros((3, GT), np.float32)
    E[0, 0] = 1.0
    E[0, 16::64] = 1.0
    E[1, 0:704:64] = 1.0
    E[2, 0:640:64] = 1.0
    E[2, 640:704:4] = 1.0
    cE = np.ascontiguousarray(np.broadcast_to(E[None], (128, 3, GT)))
    return ident, cmask, crow, cE


_CACHE = {}


def kernel(x_prompt, x_sample, state_hgrn, state_conv, meta_tokens, w_in, hgrn_lb_logits,
           hgrn_norm_w, conv_w, w_branch_a, w_branch_b, w_out, ffn1_w_gu, ffn1_w_down,
           ffn2_w_gu, ffn2_w_down, ln_g, ln_b):
    f32 = np.float32
    x_prompt = np.asarray(x_prompt, f32)
    x_sample = np.asarray(x_sample, f32)
    state_hgrn = np.asarray(state_hgrn, f32)
    state_conv = np.asarray(state_conv, f32)
    meta = np.asarray(meta_tokens, f32)
    nc, stats = build_program()
    ident, cmask, crow, cE = host_consts()

    def fm(a):
        a = np.asarray(a, f32)
        sh = a.shape[:-1]
        return np.ascontiguousarray(np.moveaxis(a.reshape(sh + (8, 128)), -1, 0))

    def dn_layout(w):
        w = np.asarray(w, f32).reshape(4, NJ, 128, 8, 128)
        return np.ascontiguousarray(w.transpose(0, 3, 2, 1, 4))

    shared = {
        "w_in": np.ascontiguousarray(np.asarray(w_in, f32)),
        "w_a": np.ascontiguousarray(np.asarray(w_branch_a, f32)),
        "w_b": np.ascontiguousarray(np.asarray(w_branch_b, f32)),
        "w_o": np.ascontiguousarray(np.asarray(w_out, f32)),
        "f1gu": np.ascontiguousarray(np.asarray(ffn1_w_gu, f32)),
        "f2gu": np.ascontiguousarray(np.asarray(ffn2_w_gu, f32)),
        "f1d": dn_layout(ffn1_w_down),
        "f2d": dn_layout(ffn2_w_down),
        "plng": fm(ln_g), "plnb": fm(ln_b), "plog": fm(hgrn_lb_logits), "pconv": fm(conv_w),
        "pnorm": np.ascontiguousarray(np.asarray(hgrn_norm_w, f32).T),
        "cident": ident, "cmask": cmask, "crow": crow, "cE": cE,
    }
    in_maps = []
    for c in range(8):
        toks = np.concatenate([meta, x_prompt[c], x_sample[16 * c:16 * c + 16].reshape(64, D)], axis=0)
        xT = np.ascontiguousarray(toks.reshape(NTOK, 8, 128).transpose(2, 1, 0))
        m = dict(shared)
        m["xT"] = xT
        m["s0"] = np.ascontiguousarray(state_hgrn[:, 16 * c:16 * c + 16])
        sc = state_conv[:, 16 * c:16 * c + 16]
        m["cb"] = np.ascontiguousarray(sc.reshape(4, 16, 2, 8, 128).transpose(4, 0, 3, 1, 2))
        in_maps.append(m)
    res = run_bass_kernel_spmd(nc, in_maps, core_ids=list(range(8)))
    y_prompt = np.zeros((8, 2048, D), f32)
    y_sample = np.zeros((128, 4, D), f32)
    nhp = np.zeros((4, 8, 8, 128, 128), f32)
    ncp = np.zeros((4, 8, 2, 1024), f32)
    nhs = np.zeros((4, 128, 8, 128, 128), f32)
    ncs = np.zeros((4, 128, 2, 1024), f32)
    for c in range(8):
        r = res.results[c]
        y = np.asarray(r["yT"]).transpose(2, 1, 0).reshape(NTOK - 16, D)
        y_prompt[c] = y[0:2048]
        y_sample[16 * c:16 * c + 16] = y[2048:].reshape(16, 4, D)
        nhp[:, c] = np.asarray(r["hp"])
        nhs[:, 16 * c:16 * c + 16] = np.asarray(r["hs"])
        cso = np.asarray(r["cso"])
        ncs[:, 16 * c:16 * c + 16] = cso.transpose(1, 3, 4, 2, 0).reshape(4, 16, 2, 1024)
        cpo = np.asarray(r["cpo"])
        ncp[:, c] = cpo.transpose(1, 3, 2, 0).reshape(4, 2, 1024)
    return (y_prompt, y_sample, nhp, ncp, nhs, ncs)
```

```python
import numpy as np
import concourse.bass as bass
import concourse.mybir as mybir
from concourse.bass_utils import run_bass_kernel_spmd

F32 = mybir.dt.float32
BF16 = mybir.dt.bfloat16
F32R = mybir.dt.float32r
AF = mybir.ActivationFunctionType
ALU = mybir.AluOpType

D = 1024
DFF = 2816
DEPTH = 4
NCK = 8
NJ = 22
NTOK = 2128
GROUPS = [(0, 720), (720, 704), (1424, 704)]
GT = 720
ALPHA = (2 * DEPTH) ** 0.25
CR = 0.5 / ALPHA
LN_EPS_P = 1e-5 / (ALPHA * ALPHA)
RMS_EPS = 1e-6
SLOT = 4096
NSLOT = 3
UBYTES = 90112

DEBUG = {"layers": DEPTH, "groups": 3, "stop": None}


class Tr:
    ENGS = ("pe", "act", "dve", "pool", "sp")

    def __init__(self):
        self.ops = {n: [] for n in self.ENGS}
        self.waited = {n: {} for n in self.ENGS}
        self.cells = {}
        self.dval = {}
        self.live = []
        self.ninst = 0

    def cell(self, key):
        c = self.cells.get(key)
        if c is None:
            pend = {}
            if isinstance(key, tuple) and isinstance(key[0], UInst):
                assert key[0].alive, ("access to retired buffer", key[0].name)
                pend = dict(key[0].pending)
            c = [{}, pend]
            self.cells[key] = c
        return c

    def op(self, eng, fn, reads=(), writes=(), dma_sem=None, ndma=1):
        deps = {}

        def merge(d):
            for k, v in d.items():
                if deps.get(k, -1) < v:
                    deps[k] = v
        for c in reads:
            merge(self.cell(c)[0])
        for c in writes:
            cc = self.cell(c)
            merge(cc[0])
            merge(cc[1])
        if dma_sem is not None:
            prev = self.dval.get(dma_sem, 0)
            if prev > 0:
                deps[("d", dma_sem)] = max(deps.get(("d", dma_sem), -1), prev)
        waits = []
        wd = self.waited[eng]
        for k, v in deps.items():
            if k == ("e", "pe") and eng == "pe":
                continue
            if wd.get(k, -1) >= v:
                continue
            wd[k] = v
            waits.append((k, v))
            if k[0] == "e":
                self.ops[k[1]][v]["signal"] = True
        idx = len(self.ops[eng])
        if dma_sem is not None:
            self.dval[dma_sem] = self.dval.get(dma_sem, 0) + 16 * ndma
            ev = (("d", dma_sem), self.dval[dma_sem])
        else:
            ev = (("e", eng), idx)
        self.ops[eng].append({"fn": fn, "waits": waits, "signal": False, "dma": dma_sem})
        for c in reads:
            r = self.cell(c)[1]
            if r.get(ev[0], -1) < ev[1]:
                r[ev[0]] = ev[1]
        for c in writes:
            self.cells[c] = [{ev[0]: ev[1]}, {}]
        return ev


class UInst:
    def __init__(self, tr, name, off, nbytes):
        self.name, self.off, self.nbytes = name, off, nbytes
        self.alive = True
        self.pending = {}
        keep = []
        for o in tr.live:
            if o.off < off + nbytes and off < o.off + o.nbytes:
                if o.alive:
                    o.alive = False
                    for k, c in list(tr.cells.items()):
                        if isinstance(k, tuple) and k[0] is o:
                            for d in (c[0], c[1]):
                                for kk, vv in d.items():
                                    if o.pending.get(kk, -1) < vv:
                                        o.pending[kk] = vv
                            del tr.cells[k]
                for kk, vv in o.pending.items():
                    if self.pending.get(kk, -1) < vv:
                        self.pending[kk] = vv
                if not (off <= o.off and o.off + o.nbytes <= off + nbytes):
                    keep.append(o)
            else:
                keep.append(o)
        keep.append(self)
        tr.live = keep


def group_chunks(g):
    if g == 0:
        return [(0, 16)] + [(16 + 64 * i, 64) for i in range(11)]
    if g == 1:
        return [(64 * i, 64) for i in range(11)]
    return [(64 * i, 64) for i in range(10)]


def build_program():
    nc = bass.Bass("TRN2", target_bir_lowering=False)
    NL = DEBUG["layers"]
    NG = DEBUG["groups"]

    def din(name, shape):
        return nc.dram_tensor(name, shape, F32, kind="ExternalInput").ap()

    def dout(name, shape):
        return nc.dram_tensor(name, shape, F32, kind="ExternalOutput").ap()

    xT = din("xT", [128, 8, NTOK])
    s0 = din("s0", [4, 16, 8, 128, 128])
    cb = din("cb", [128, 4, 8, 16, 2])
    w_in = din("w_in", [4, D, 9 * D])
    w_a = din("w_a", [4, D, D])
    w_b = din("w_b", [4, D, D])
    w_o = din("w_o", [4, D, D])
    fgu = [din("f1gu", [4, D, 2 * DFF]), din("f2gu", [4, D, 2 * DFF])]
    fdn = [din("f1d", [4, 8, 128, NJ, 128]), din("f2d", [4, 8, 128, NJ, 128])]
    plng = din("plng", [128, 4, 3, 8])
    plnb = din("plnb", [128, 4, 3, 8])
    plog = din("plog", [128, 4, 8])
    pconv = din("pconv", [128, 4, 3, 8])
    pnorm = din("pnorm", [128, 4])
    cident = din("cident", [128, 128])
    cmask = din("cmask", [64, 2, 512])
    crow = din("crow", [64, 16])
    cE = din("cE", [128, 3, GT])

    yT = dout("yT", [128, 8, NTOK - 16])
    hs = dout("hs", [4, 16, 8, 128, 128])
    hp = dout("hp", [4, 8, 128, 128])
    cso = dout("cso", [128, 4, 8, 16, 2])
    cpo = dout("cpo", [128, 4, 8, 2])

    tr = Tr()
    sb = {}
    ctxs = []

    def salloc(name, shape, dt):
        cm = nc.sbuf_tensor(name, shape, dt)
        t = cm.__enter__()
        ctxs.append(cm)
        sb[name] = t
        return t

    xf = salloc("xf", [128, 8, GT], F32)
    xb = salloc("xb", [128, 8, GT], BF16)
    S = salloc("S", [128, 8, 128], F32)
    Sb = salloc("Sb", [128, 8, 128], BF16)
    W = salloc("W", [128, NSLOT, SLOT], BF16)
    U = salloc("U", [128, UBYTES // 4], F32)
    identf = salloc("identf", [128, 128], F32)
    identb = salloc("identb", [128, 128], BF16)
    ones1k = salloc("ones1k", [128, 128], BF16)
    ones128 = salloc("ones128", [128, 128], BF16)
    zerob = salloc("zerob", [128, 128], BF16)
    lng = salloc("lng", [128, 4, 3, 8], F32)
    lnb = salloc("lnb", [128, 4, 3, 8], F32)
    lgt = salloc("lgt", [128, 4, 8], F32)
    lbt = salloc("lbt", [128, 4, 8], F32)
    c0t = salloc("c0t", [128, 4, 8], F32)
    c1t = salloc("c1t", [128, 4, 8], F32)
    sm8 = salloc("sm8", [128, 3, 8], F32)
    convw = salloc("convw", [128, 4, 3, 8], F32)
    normw = salloc("normw", [128, 4], F32)
    maskc = salloc("maskc", [64, 2, 512], F32)
    rowm = salloc("rowm", [64, 16], F32)
    Et = salloc("Et", [128, GT], F32)
    nEt = salloc("nEt", [128, GT], F32)
    ubuf = salloc("ubuf", [128, 4, 8, 2], F32)
    DEC = salloc("DEC", [128, 8, 12], F32)
    DECS = salloc("DECS", [128, 8, 16], F32)
    CB = salloc("CB", [128, 4, 8, 16, 2], F32)
    US = salloc("US", [128, 8, 16, 6], F32)
    CVS = salloc("CVS", [128, 8, 16, 4], F32)
    CVS2 = salloc("CVS2", [128, 8, 16, 4], F32)
    QTF = salloc("QTF", [128, 8, 64], F32)
    epsl = salloc("epsl", [128, 1], F32)
    epsr = salloc("epsr", [128, 1], F32)
    CSO = salloc("CSO", [128, 4, 8, 16, 2], F32)
    CPO = salloc("CPO", [128, 4, 8, 2], F32)

    pst = []
    for i in range(4):
        cm = nc.psum_tensor("ps%d" % i, [128, 1024], F32)
        pst.append(cm.__enter__())
        ctxs.append(cm)

    def bank(b, n=512):
        return pst[b // 2][:, (b % 2) * 512:(b % 2) * 512 + n]

    def bank_bf(b):
        return pst[b // 2][:, (b % 2) * 512:(b % 2) * 512 + 512].bitcast(BF16)

    bank_rr = [0]
    held = set()

    def alloc_bank():
        while True:
            b = bank_rr[0]
            bank_rr[0] = (b + 1) % 8
            if b not in held:
                return b

    def alloc_bank2():
        while True:
            if bank_rr[0] % 2:
                bank_rr[0] = (bank_rr[0] + 1) % 8
            b = bank_rr[0]
            bank_rr[0] = (b + 2) % 8
            if b not in held and (b + 1) not in held:
                return b

    def uview(off, dt, shape):
        esz = 2 if dt == BF16 else 4
        n = int(np.prod(shape))
        assert off % 4 == 0 and off + n * esz <= UBYTES, (off, shape)
        v = U[:, off // 4: off // 4 + (n * esz) // 4]
        if dt != F32:
            v = v.bitcast(dt)
        if len(shape) == 2:
            return v.rearrange("p (a b) -> p a b", a=shape[0], b=shape[1])
        if len(shape) == 3:
            return v.rearrange("p (a b c) -> p a b c", a=shape[0], b=shape[1], c=shape[2])
        return v

    class UB:
        def __init__(self, name, off, dt, shape):
            esz = 2 if dt == BF16 else 4
            self.inst = UInst(tr, name, off, int(np.prod(shape)) * esz)
            self.v = uview(off, dt, shape)

        def c(self, *key):
            return (self.inst,) + key

    wstate = {"n": 0}

    def load_block(parts, nk):
        i = wstate["n"]
        wstate["n"] += 1
        slot = i % NSLOT
        ncols = sum(p[2] for p in parts)
        assert nk * ncols <= SLOT
        wv = W[:, slot, 0:nk * ncols].rearrange("p (k c) -> p k c", k=nk, c=ncols)

        def fn(e, parts=parts, wv=wv):
            out = []
            for (src, co, ncp) in parts:
                sv = src if len(src.shape) == 3 else src.rearrange("(k p) c -> p k c", p=128)
                out.append(e.dma_start(out=wv[:, :, co:co + ncp], in_=sv))
            return out
        tr.op("pool", fn, reads=(), writes=[("w", slot)], dma_sem="w%d" % slot, ndma=len(parts))
        return slot, wv

    pending_blocks = []

    def block_sched2():
        for g in range(NG):
            for l in range(NL):
                for phase in ("ffn1", "mixer", "ffn2"):
                    if phase == "mixer":
                        wl = w_in[l]

                        def seg(sg, hh):
                            return [(wl[:, 1024 * sg + 512 * hh: 1024 * sg + 512 * hh + 512], 0, 512)]
                        for hh in (0, 1):
                            yield (seg(1, hh), 8)
                            yield (seg(2, hh), 8)
                            yield (seg(0, hh), 8)
                        for hh in (0, 1):
                            yield (seg(3, hh), 8)
                        for hh in (0, 1):
                            yield (seg(7, hh), 8)
                        for hh in (0, 1):
                            yield ([(w_a[l][:, 512 * hh: 512 * hh + 512], 0, 512)], 8)
                        for j in range(4):
                            yield ([(wl[:, 5120 + 256 * j: 5120 + 256 * j + 256], 0, 256),
                                    (wl[:, 6144 + 256 * j: 6144 + 256 * j + 256], 256, 256)], 8)
                        for hh in (0, 1):
                            yield (seg(8, hh), 8)
                        for hh in (0, 1):
                            yield (seg(4, hh), 8)
                        for hh in (0, 1):
                            yield ([(w_b[l][:, 512 * hh: 512 * hh + 512], 0, 512)], 8)
                        for hh in (0, 1):
                            yield ([(w_o[l][:, 512 * hh: 512 * hh + 512], 0, 512)], 8)
                    else:
                        fi = 0 if phase == "ffn1" else 1
                        gu = fgu[fi][l]
                        dn = fdn[fi][l]
                        for j in range(11):
                            yield ([(gu[:, 256 * j: 256 * j + 256], 0, 256),
                                    (gu[:, DFF + 256 * j: DFF + 256 * j + 256], 256, 256)], 8)
                        for rep in range(2):
                            for c in range(8):
                                yield ([(dn[c], 0, 128)], NJ)

    sched = block_sched2()
    prefetched = []

    def prefetch(maxn=99):
        k = 0
        while len(prefetched) < NSLOT and k < maxn:
            k += 1
            try:
                parts, nk = next(sched)
            except StopIteration:
                return
            prefetched.append(load_block(parts, nk))

    def next_block():
        if not prefetched:
            prefetch()
        return prefetched.pop(0)

    def mm_group(out_ap, pairs, reads, writes, first_start=True):
        def fn(e, out_ap=out_ap, pairs=pairs, first_start=first_start):
            ins = None
            n = len(pairs)
            for i, (l_, r_) in enumerate(pairs):
                ins = e.matmul(out_ap, lhsT=l_, rhs=r_, start=(first_start and i == 0), stop=(i == n - 1))
            return [ins]
        tr.op("pe", fn, reads=reads, writes=writes)

    def act(out, in_, func, reads, writes, bias=None, scale=None):
        kw = {}
        if bias is not None:
            kw["bias"] = bias
        if scale is not None:
            kw["scale"] = scale

        def fn(e):
            return [e.activation(out=out, in_=in_, func=func, **kw)]
        tr.op("act", fn, reads=reads, writes=writes)

    def dve(kind, reads, writes, eng="dve", **kw):
        def fn(e):
            return [getattr(e, kind)(**kw)]
        tr.op(eng, fn, reads=reads, writes=writes)

    def dma(eng, out, in_, reads, writes, sem):
        def fn(e):
            return [e.dma_start(out=out, in_=in_)]
        tr.op(eng, fn, reads=reads, writes=writes, dma_sem=sem)

    spsem = [0]

    def sp_dma(out, in_, reads, writes, nsem=4, base="io"):
        s = "%s%d" % (base, spsem[0] % nsem)
        spsem[0] += 1
        dma("sp", out, in_, reads, writes, s)

    def load_const(t, src, key):
        sp_dma(t[:] if not isinstance(t, bass.AP) else t, src, [], [key])

    load_const(identf, cident, "identf")
    load_const(lng, plng, "lng")
    load_const(lnb, plnb, "lnb")
    load_const(lgt, plog, "lgt")
    load_const(convw, pconv, "convw")
    load_const(normw, pnorm, "normw")
    load_const(maskc, cmask, "maskc")
    load_const(rowm, crow, "rowm")
    load_const(CB, cb, "CB")
    dve("tensor_copy", ["identf"], ["identb"], out=identb[:], in_=identf[:])
    dve("memset", [], ["ones1k"], ap=ones1k[:], constant=1.0 / 1024.0)
    dve("memset", [], ["ones128"], ap=ones128[:], constant=1.0 / 128.0)
    dve("memset", [], ["zerob"], ap=zerob[:], constant=0.0)
    dve("memset", [], ["epsl"], ap=epsl[:], constant=LN_EPS_P)
    dve("memset", [], ["epsr"], ap=epsr[:], constant=RMS_EPS)
    dve("memset", [], ["ubuf"], ap=ubuf[:], constant=0.0)
    dve("tensor_tensor", ["lgt"], ["sm8"], out=sm8[:, 0, :], in0=lgt[:, 0, :], in1=lgt[:, 1, :], op=ALU.max)
    dve("tensor_tensor", ["lgt", "sm8"], ["sm8"], out=sm8[:, 0, :], in0=sm8[:, 0, :], in1=lgt[:, 2, :], op=ALU.max)
    dve("tensor_tensor", ["lgt", "sm8"], ["sm8"], out=sm8[:, 0, :], in0=sm8[:, 0, :], in1=lgt[:, 3, :], op=ALU.max)
    dve("tensor_tensor", ["lgt", "sm8"], ["lbt"], out=lbt[:], in0=lgt[:],
        in1=sm8[:, 0:1, :].broadcast_to([128, 4, 8]), op=ALU.subtract)
    act(lbt[:], lbt[:], AF.Exp, ["lbt"], ["lbt"])
    dve("tensor_tensor", ["lbt"], ["sm8"], out=sm8[:, 1, :], in0=lbt[:, 0, :], in1=lbt[:, 1, :], op=ALU.add)
    dve("tensor_tensor", ["lbt", "sm8"], ["sm8"], out=sm8[:, 1, :], in0=sm8[:, 1, :], in1=lbt[:, 2, :], op=ALU.add)
    dve("tensor_tensor", ["lbt", "sm8"], ["sm8"], out=sm8[:, 1, :], in0=sm8[:, 1, :], in1=lbt[:, 3, :], op=ALU.add)
    dve("reciprocal", ["sm8"], ["sm8"], out=sm8[:, 2, :], in_=sm8[:, 1, :])
    dve("tensor_tensor", ["lbt", "sm8"], ["lgt"], out=lgt[:], in0=lbt[:],
        in1=sm8[:, 2:3, :].broadcast_to([128, 4, 8]), op=ALU.mult)
    dve("memset", [], ["lbt"], ap=lbt[:, 0, :], constant=0.0)
    dve("tensor_copy", ["lgt", "lbt"], ["lbt"], out=lbt[:, 1, :], in_=lgt[:, 1, :])
    dve("tensor_tensor", ["lgt", "lbt"], ["lbt"], out=lbt[:, 2, :], in0=lbt[:, 1, :], in1=lgt[:, 2, :], op=ALU.add)
    dve("tensor_tensor", ["lgt", "lbt"], ["lbt"], out=lbt[:, 3, :], in0=lbt[:, 2, :], in1=lgt[:, 3, :], op=ALU.add)
    dve("tensor_scalar", ["lbt"], ["c1t"], out=c1t[:], in0=lbt[:], scalar1=-0.5, scalar2=0.5,
        op0=ALU.mult, op1=ALU.add)
    dve("tensor_tensor", ["lbt", "c1t"], ["c0t"], out=c0t[:], in0=lbt[:], in1=c1t[:], op=ALU.add)

    prefetch()

    def xcells(name, s):
        return [(name, c, s) for c in range(8)]

    def ln_sub(l, k, subs, s, off_sq, off_yt, off_st):
        nmax = max(n for _, n in subs)
        t0, n = subs[s]
        SQ = UB("SQ", off_sq, BF16, [16, nmax])
        YT = UB("YT", off_yt, F32, [8, nmax])
        ST = UB("ST", off_st, F32, [4, nmax])
        dve("tensor_copy", xcells("xf", s), [SQ.c(0)], out=SQ.v[:, 0:8, 0:n], in_=xf[:, :, t0:t0 + n])
        act(SQ.v[:, 8:16, 0:n], xf[:, :, t0:t0 + n], AF.Square, xcells("xf", s), [SQ.c(1)])
        bm = alloc_bank()
        be = alloc_bank()
        mm_group(bank(bm, n), [(ones1k[:], SQ.v[:, c, 0:n]) for c in range(8)],
                 ["ones1k", SQ.c(0)], [("ps", bm)])
        mm_group(bank(be, n), [(ones1k[:], SQ.v[:, 8 + c, 0:n]) for c in range(8)],
                 ["ones1k", SQ.c(1)], [("ps", be)])
        act(ST.v[:, 0, 0:n], bank(bm, n), AF.Identity, [("ps", bm)], [ST.c(0)])
        act(ST.v[:, 1, 0:n], bank(bm, n), AF.Square, [("ps", bm)], [ST.c(1)])
        dve("tensor_tensor", [("ps", be), ST.c(1)], [ST.c(2)], out=ST.v[:, 2, 0:n], in0=bank(be, n),
            in1=ST.v[:, 1, 0:n], op=ALU.subtract)
        act(ST.v[:, 2, 0:n], ST.v[:, 2, 0:n], AF.Sqrt, [ST.c(2), "epsl"], [ST.c(2)], bias=epsl[:], scale=1.0)
        dve("tensor_tensor", xcells("xf", s) + [ST.c(0)], [YT.c(0)], out=YT.v[:, :, 0:n],
            in0=xf[:, :, t0:t0 + n], in1=ST.v[:, 0:1, 0:n].broadcast_to([128, 8, n]), op=ALU.subtract)
        dve("reciprocal", [ST.c(2)], [ST.c(3)], out=ST.v[:, 3, 0:n], in_=ST.v[:, 2, 0:n])
        dve("tensor_tensor", [YT.c(0), ST.c(3)], [YT.c(0)], out=YT.v[:, :, 0:n],
            in0=YT.v[:, :, 0:n], in1=ST.v[:, 3:4, 0:n].broadcast_to([128, 8, n]), op=ALU.mult)
        for c in range(8):
            act(xb[:, c, t0:t0 + n], YT.v[:, c, 0:n], AF.Identity, [YT.c(0), "lng", "lnb"], [("xb", c, s)],
                scale=lng[:, l, k, c:c + 1], bias=lnb[:, l, k, c:c + 1])
            dve("tensor_scalar", [YT.c(0), "lng", "lnb"], [("xf", c, s)], out=xf[:, c, t0:t0 + n],
                in0=YT.v[:, c, 0:n], scalar1=lng[:, l, k, c:c + 1], scalar2=lnb[:, l, k, c:c + 1],
                op0=ALU.mult, op1=ALU.add)

    def ffn(l, fi, subs):
        H = UB("H", 0, BF16, [NJ, GT])
        SG = UB("SG", 60480, BF16, [4, 360])
        sgi = [0]

        def gu_groups(j, slot, wv, s):
            t0, n = subs[s]
            for jj in range(2):
                bg = alloc_bank()
                bu = alloc_bank()
                mm_group(bank(bg, n), [(wv[:, k, jj * 128:(jj + 1) * 128], xb[:, k, t0:t0 + n]) for k in range(8)],
                         [("w", slot)] + xcells("xb", s), [("ps", bg)])
                mm_group(bank(bu, n), [(wv[:, k, 256 + jj * 128:256 + (jj + 1) * 128], xb[:, k, t0:t0 + n])
                                       for k in range(8)],
                         [("w", slot)] + xcells("xb", s), [("ps", bu)])
                sgv = SG.v[:, sgi[0] % 4, 0:n]
                sgc = SG.c(sgi[0] % 4)
                sgi[0] += 1
                act(sgv, bank(bg, n), AF.Silu, [("ps", bg)], [sgc])
                dve("tensor_tensor", [sgc, ("ps", bu)], [H.c(2 * j + jj, s)], out=H.v[:, 2 * j + jj, t0:t0 + n],
                    in0=bank(bu, n), in1=sgv, op=ALU.mult)
        sl0, wv0 = next_block()
        sl1, wv1 = next_block()
        gu_groups(0, sl0, wv0, 0)
        gu_groups(1, sl1, wv1, 0)
        gu_groups(0, sl0, wv0, 1)
        prefetch(1)
        gu_groups(1, sl1, wv1, 1)
        prefetch(1)
        for j in range(2, 11):
            slot, wv = next_block()
            for s in range(2):
                gu_groups(j, slot, wv, s)
            prefetch()
        kk = 0 if fi == 0 else 2
        for s, (t0, n) in enumerate(subs):
            for c in range(8):
                slot, wv = next_block()
                b = alloc_bank()
                mm_group(bank(b, n), [(wv[:, k, 0:128], H.v[:, k, t0:t0 + n]) for k in range(NJ)],
                         [("w", slot)] + [H.c(k, s) for k in range(NJ)], [("ps", b)])
                dve("scalar_tensor_tensor", [("ps", b), ("xf", c, s)], [("xf", c, s)], out=xf[:, c, t0:t0 + n],
                    in0=bank(b, n), scalar=CR, in1=xf[:, c, t0:t0 + n], op0=ALU.mult, op1=ALU.add)
                prefetch()
                if s == 1 and c == 1:
                    ln_sub(l, kk, subs, 0, 31680, 43200, 54720)
        ln_sub(l, kk, subs, 1, 31680, 43200, 54720)

    def dense8(out_chunks, subs, consumer):
        slot, wv = next_block()
        for s, (t0, n) in enumerate(subs):
            for i, co in enumerate(out_chunks):
                b = alloc_bank()
                mm_group(bank(b, n), [(wv[:, k, co:co + 128], xb[:, k, t0:t0 + n]) for k in range(8)],
                         [("w", slot)] + xcells("xb", s), [("ps", b)])
                consumer(i, s, t0, n, b)
        prefetch()

    def dense8_pair(subs, consA, consB):
        slA, wvA = next_block()
        slB, wvB = next_block()
        for s, (sl, wv, cons) in ((0, (slA, wvA, consA)), (0, (slB, wvB, consB)), (1, (slA, wvA, consA)),
                                  (1, (slB, wvB, consB))):
            t0, n = subs[s]
            for i in range(4):
                b = alloc_bank()
                mm_group(bank(b, n), [(wv[:, k, i * 128:(i + 1) * 128], xb[:, k, t0:t0 + n]) for k in range(8)],
                         [("w", sl)] + xcells("xb", s), [("ps", b)])
                cons(i, s, t0, n, b)
            if s == 1:
                prefetch(1)

    def dense8_from(src, srccells, subs, consumer):
        slot, wv = next_block()
        for s, (t0, n) in enumerate(subs):
            for i in range(4):
                b = alloc_bank()
                mm_group(bank(b, n), [(wv[:, k, i * 128:(i + 1) * 128], src.v[:, k, t0:t0 + n]) for k in range(8)],
                         [("w", slot)] + [srccells(k, s) for k in range(8)], [("ps", b)])
                consumer(i, s, t0, n, b)
        prefetch()

    def mixer(g, l, subs):
        G = GROUPS[g][1]
        chunks = group_chunks(g)
        has_sample = (g == 2)
        nch = len(chunks)
        Fb = UB("F", 0, F32, [4, GT])
        Pb = UB("P", 11520, F32, [4, GT])
        QT = UB("QT", 23040, BF16, [8, GT])
        NK = UB("NK", 34560, BF16, [8, GT])
        VT = UB("VT", 46080, BF16, [8, GT])
        A_ = UB("A", 57600, F32, [2, GT])
        B_ = UB("B", 63360, F32, [2, GT])
        R_ = UB("R", 69120, F32, [2, GT])
        QS = UB("QS", 74880, F32, [2, 360])
        if g == 0:
            dve("memset", [], ["S"], ap=S[:], constant=0.0)
        else:
            sp_dma(S[:], hp[l].rearrange("h d v -> d h v"), [("hp", l)], ["S"], base="st", nsem=2)
        act(Sb[:], S[:], AF.Copy, ["S"], ["Sb"])
        qsi = [0]
        vtog = [0]
        for hh in (0, 1):
            def cons_f(i, s, t0, n, b):
                act(Fb.v[:, i, t0:t0 + n], bank(b, n), AF.Tanh, [("ps", b)], [Fb.c(i, s)], scale=0.5)

            def cons_v(i, s, t0, n, b, hh=hh):
                h = 4 * hh + i
                act(VT.v[:, h, t0:t0 + n], bank(b, n), AF.Copy, [("ps", b)], [VT.c(h, s)])
            if hh == 0:
                dense8_pair(subs, cons_f, cons_v)
            else:
                dense8([0, 128, 256, 384], subs, cons_f)
            for i0 in (0, 2):
                pair = (i0, i0 + 1)

                def fcs(i):
                    return [Fb.c(i, 0), Fb.c(i, 1)]
                for i in pair:
                    h = 4 * hh + i
                    dve("tensor_scalar", fcs(i) + ["c0t", "c1t"], fcs(i), out=Fb.v[:, i, 0:G], in0=Fb.v[:, i, 0:G],
                        scalar1=c1t[:, l, h:h + 1], scalar2=c0t[:, l, h:h + 1], op0=ALU.mult, op1=ALU.add)
                for i in pair:
                    dve("tensor_tensor", fcs(i) + ["nEt"], [A_.c(i % 2)], out=A_.v[:, i % 2, 0:G], in0=Fb.v[:, i, 0:G],
                        in1=nEt[:, 0:G], op=ALU.mult)
                    dve("tensor_tensor", fcs(i) + ["Et"], [B_.c(i % 2)], out=B_.v[:, i % 2, 0:G], in0=Fb.v[:, i, 0:G],
                        in1=Et[:, 0:G], op=ALU.mult)
                for i in pair:
                    dve("tensor_tensor_scan", [A_.c(i % 2), B_.c(i % 2)], [Pb.c(i)], out=Pb.v[:, i, 0:G],
                        data0=A_.v[:, i % 2, 0:G], data1=B_.v[:, i % 2, 0:G], initial=0.0, op0=ALU.mult, op1=ALU.add)
                for i in pair:
                    dve("reciprocal", [Pb.c(i)], [R_.c(i % 2)], out=R_.v[:, i % 2, 0:G], in_=Pb.v[:, i, 0:G])
                for i in pair:
                    h = 4 * hh + i
                    dve("scalar_tensor_tensor", fcs(i) + [R_.c(i % 2)], [NK.c(h)], out=NK.v[:, h, 0:G],
                        in0=Fb.v[:, i, 0:G], scalar=1.0, in1=R_.v[:, i % 2, 0:G], op0=ALU.subtract, op1=ALU.mult)
                for i in pair:
                    h = 4 * hh + i
                    e0 = chunks[0][1] - 1
                    act(DEC[:, h, 0:nch], Pb.v[:, i, e0:e0 + 64 * (nch - 1) + 1:64], AF.Copy, [Pb.c(i)], [("DEC", h)])
                    if has_sample:
                        act(DECS[:, h, :], Pb.v[:, i, 643:704:4], AF.Copy, [Pb.c(i)], [("DECS", h)])
            if hh == 1:
                dense8([0, 128, 256, 384], subs, cons_v)

            def cons_q(i, s, t0, n, b, hh=hh):
                h = 4 * hh + i
                qv = QS.v[:, qsi[0] % 2, 0:n]
                qc = QS.c(qsi[0] % 2)
                qsi[0] += 1
                act(qv, bank(b, n), AF.Silu, [("ps", b)], [qc])
                dve("tensor_tensor", [qc, Pb.c(i)], [QT.c(h, s)], out=QT.v[:, h, t0:t0 + n], in0=qv,
                    in1=Pb.v[:, i, t0:t0 + n], op=ALU.mult)
                if has_sample and s == 1:
                    dve("tensor_tensor", [qc, Pb.c(i)], [("QTF", h)], out=QTF[:, h, :], in0=qv[:, n - 64:n],
                        in1=Pb.v[:, i, 640:704], op=ALU.mult)
            dense8([0, 128, 256, 384], subs, cons_q)
        OT = UB("OT", 0, F32, [8, GT])
        SC = UB("SC", 57600, BF16, [2, 512])
        KT = UB("KT", 59648, BF16, [2, 1024])
        VTT = UB("VTT", 63744, BF16, [2, 1024])
        KH = UB("KH", 67840, BF16, [2, 512])
        n2 = subs[0][1]

        def scell(buf, t0, L):
            ss = set()
            for (s, (a, n)) in enumerate(subs):
                if t0 < a + n and a < t0 + L:
                    ss.add(s)
            return [buf.c(h, s) for h in range(8) for s in ss]

        def chunk_common(ci, t0, L, mask_idx, dec_ap, dec_cells, par):
            qc = scell(QT, t0, L)
            kc = [NK.c(h) for h in range(8)]
            vc = scell(VT, t0, L)
            bs = alloc_bank()
            def fn_sc(e):
                ins = None
                for h in range(8):
                    ins = e.matmul(bank(bs)[0:L, h * 64:h * 64 + L], lhsT=NK.v[:, h, t0:t0 + L],
                                   rhs=QT.v[:, h, t0:t0 + L], start=True, stop=True)
                return [ins]
            tr.op("pe", fn_sc, reads=qc + kc, writes=[("ps", bs)])
            scv = SC.v[0:L, par, :].rearrange("p (h t) -> p h t", h=8, t=64)[:, :, 0:L]
            dve("tensor_tensor", [("ps", bs), "maskc"], [SC.c(par)], out=scv,
                in0=bank(bs)[0:L, :].rearrange("p (h t) -> p h t", h=8, t=64)[:, :, 0:L],
                in1=maskc[0:L, mask_idx, :].rearrange("p (h t) -> p h t", h=8, t=64)[:, :, 0:L], op=ALU.mult)
            khv = KH.v[:, par, :].rearrange("p (h t) -> p h t", h=8, t=64)[:, :, 0:L]
            dve("tensor_tensor", kc + dec_cells, [KH.c(par)], out=khv, in0=NK.v[:, :, t0:t0 + L], in1=dec_ap,
                op=ALU.mult)
            bk = alloc_bank()
            bv = alloc_bank()

            def fn_tk(e):
                ins = None
                for h in range(8):
                    ins = e.transpose(out=bank_bf(bk)[0:L, h * 128:(h + 1) * 128],
                                      in_=KH.v[:, par, h * 64:h * 64 + L], identity=identb[:])
                return [ins]
            tr.op("pe", fn_tk, reads=[KH.c(par), "identb"], writes=[("ps", bk)])

            def fn_tv(e):
                ins = None
                for h in range(8):
                    ins = e.transpose(out=bank_bf(bv)[0:L, h * 128:(h + 1) * 128],
                                      in_=VT.v[:, h, t0:t0 + L], identity=identb[:])
                return [ins]
            tr.op("pe", fn_tv, reads=vc + ["identb"], writes=[("ps", bv)])
            act(KT.v[0:L, par, :], bank_bf(bk)[0:L, :], AF.Copy, [("ps", bk)], [KT.c(par)])
            act(VTT.v[0:L, par, :], bank_bf(bv)[0:L, :], AF.Copy, [("ps", bv)], [VTT.c(par)])
            return qc

        def part_a(ci):
            t0, L = chunks[ci]
            dec_ap = DEC[:, :, ci:ci + 1].broadcast_to([128, 8, L])
            return chunk_common(ci, t0, L, 0, dec_ap, [("DEC", h) for h in range(8)], ci % 2)

        def part_b(ci, qc):
            t0, L = chunks[ci]
            par = ci % 2
            bsu = alloc_bank2()

            def fn_su(e, L=L, par=par, bsu=bsu):
                ins = None
                for h in range(8):
                    ins = e.matmul(pst[bsu // 2][:, h * 128:(h + 1) * 128], lhsT=KT.v[0:L, par, h * 128:(h + 1) * 128],
                                   rhs=VTT.v[0:L, par, h * 128:(h + 1) * 128], start=True, stop=True)
                return [ins]
            tr.op("pe", fn_su, reads=[KT.c(par), VTT.c(par)], writes=[("ps", bsu), ("ps", bsu + 1)])
            bo = alloc_bank()

            def fn_o(e, t0=t0, L=L, par=par, bo=bo):
                ins = None
                for h in range(8):
                    e.matmul(bank(bo)[:, h * 64:h * 64 + L], lhsT=Sb[:, h, :], rhs=QT.v[:, h, t0:t0 + L],
                             start=True, stop=False)
                    ins = e.matmul(bank(bo)[:, h * 64:h * 64 + L], lhsT=VTT.v[0:L, par, h * 128:(h + 1) * 128],
                                   rhs=SC.v[0:L, par, h * 64:h * 64 + L], start=False, stop=True)
                return [ins]
            tr.op("pe", fn_o, reads=["Sb", VTT.c(par), SC.c(par)] + qc, writes=[("ps", bo)])
            dve("tensor_tensor", ["S"] + [("DEC", h) for h in range(8)], ["S"], out=S[:], in0=S[:],
                in1=DEC[:, :, ci:ci + 1].broadcast_to([128, 8, 128]), op=ALU.mult)
            dve("tensor_tensor", ["S", ("ps", bsu), ("ps", bsu + 1)], ["S"], out=S[:], in0=S[:],
                in1=pst[bsu // 2][:, :].rearrange("p (h v) -> p h v", h=8, v=128), op=ALU.subtract)
            act(OT.v[:, :, t0:t0 + L], bank(bo).rearrange("p (h t) -> p h t", h=8, t=64)[:, :, 0:L], AF.Copy,
                [("ps", bo)], scell(OT, t0, L))
            if ci < nch - 1:
                act(Sb[:], S[:], AF.Copy, ["S"], ["Sb"])

        qcs = {0: part_a(0)}
        for ci in range(nch):
            if ci + 1 < nch:
                qcs[ci + 1] = part_a(ci + 1)
            part_b(ci, qcs[ci])
        sp_dma(hp[l].rearrange("h d v -> d h v"), S[:], ["S"], [("hp", l)], base="st", nsem=2)

        if has_sample:
            S0b = UB("S0b", 69888, F32, [2, 1024])
            SOb = UB("SOb", 78080, F32, [2, 1024])
            KTM = UB("KTM", 86272, BF16, [1, 1024])
            tS = 640
            parS = nch % 2
            dec_ap = DECS[:, :, :].unsqueeze(3).broadcast_to([128, 8, 16, 4])
            qcells = scell(QT, tS, 64)
            kc = [NK.c(h) for h in range(8)]
            vc = scell(VT, tS, 64)
            bs = alloc_bank()

            def fn_sc(e):
                ins = None
                for h in range(8):
                    ins = e.matmul(bank(bs)[0:64, h * 64:(h + 1) * 64], lhsT=NK.v[:, h, tS:tS + 64],
                                   rhs=QT.v[:, h, tS:tS + 64], start=True, stop=True)
                return [ins]
            tr.op("pe", fn_sc, reads=qcells + kc, writes=[("ps", bs)])
            dve("tensor_tensor", [("ps", bs), "maskc"], [SC.c(parS)], out=SC.v[0:64, parS, :], in0=bank(bs)[0:64, :],
                in1=maskc[:, 1, :], op=ALU.mult)
            dve("tensor_tensor", kc + [("DECS", h) for h in range(8)], [KH.c(parS)],
                out=KH.v[:, parS, :].rearrange("p (h q t) -> p h q t", h=8, q=16, t=4),
                in0=NK.v[:, :, tS:tS + 64].rearrange("p h (q t) -> p h q t", q=16, t=4), in1=dec_ap, op=ALU.mult)
            bk = alloc_bank()
            bv = alloc_bank()

            def fn_tk(e):
                ins = None
                for h in range(8):
                    ins = e.transpose(out=bank_bf(bk)[0:64, h * 128:(h + 1) * 128],
                                      in_=KH.v[:, parS, h * 64:(h + 1) * 64], identity=identb[:])
                return [ins]
            tr.op("pe", fn_tk, reads=[KH.c(parS), "identb"], writes=[("ps", bk)])

            def fn_tv(e):
                ins = None
                for h in range(8):
                    ins = e.transpose(out=bank_bf(bv)[0:64, h * 128:(h + 1) * 128],
                                      in_=VT.v[:, h, tS:tS + 64], identity=identb[:])
                return [ins]
            tr.op("pe", fn_tv, reads=vc + ["identb"], writes=[("ps", bv)])
            act(KT.v[0:64, parS, :], bank_bf(bk)[0:64, :], AF.Copy, [("ps", bk)], [KT.c(parS)])
            act(VTT.v[0:64, parS, :], bank_bf(bv)[0:64, :], AF.Copy, [("ps", bv)], [VTT.c(parS)])
            bo = alloc_bank()
            held.add(bo)

            def fn_o0(e):
                e.matmul(bank(bo), lhsT=zerob[:], rhs=xb[:, 0, 0:512], start=True, stop=False)
                ins = None
                for h in range(8):
                    ins = e.matmul(bank(bo)[:, h * 64:(h + 1) * 64], lhsT=VTT.v[0:64, parS, h * 128:(h + 1) * 128],
                                   rhs=SC.v[0:64, parS, h * 64:(h + 1) * 64], start=False, stop=False)
                return [ins]
            tr.op("pe", fn_o0, reads=["zerob", VTT.c(parS), SC.c(parS)] + xcells("xb", 0) + xcells("xb", 1),
                  writes=[("ps", bo)])
            for q in range(16):
                sp_dma(S0b.v[:, q % 2, :].rearrange("p (h v) -> p h v", h=8, v=128),
                       s0[l, q].rearrange("h d v -> d h v"), [], [S0b.c(q % 2)], base="si", nsem=2)

                def fn_oq(e, q=q):
                    ins = None
                    for h in range(8):
                        ins = e.matmul(bank(bo)[:, h * 64 + q * 4:h * 64 + q * 4 + 4],
                                       lhsT=S0b.v[:, q % 2, h * 128:(h + 1) * 128], rhs=QTF[:, h, q * 4:q * 4 + 4],
                                       start=False, stop=(q == 15 and h == 7))
                    return [ins]
                tr.op("pe", fn_oq, reads=[S0b.c(q % 2)] + [("QTF", h) for h in range(8)], writes=[("ps", bo)])
                dve("tensor_scalar", [KT.c(parS), "rowm"], [KTM.c(0)], out=KTM.v[0:64, 0, :], in0=KT.v[0:64, parS, :],
                    scalar1=rowm[:, q:q + 1], scalar2=None, op0=ALU.mult)
                bsu = alloc_bank2()

                def fn_su(e, bsu=bsu):
                    ins = None
                    for h in range(8):
                        ins = e.matmul(pst[bsu // 2][:, h * 128:(h + 1) * 128],
                                       lhsT=KTM.v[0:64, 0, h * 128:(h + 1) * 128],
                                       rhs=VTT.v[0:64, parS, h * 128:(h + 1) * 128], start=True, stop=True)
                    return [ins]
                tr.op("pe", fn_su, reads=[KTM.c(0), VTT.c(parS)], writes=[("ps", bsu), ("ps", bsu + 1)])
                sov = SOb.v[:, q % 2, :].rearrange("p (h v) -> p h v", h=8, v=128)
                dve("tensor_tensor", [S0b.c(q % 2)] + [("DECS", h) for h in range(8)], [SOb.c(q % 2)], out=sov,
                    in0=S0b.v[:, q % 2, :].rearrange("p (h v) -> p h v", h=8, v=128),
                    in1=DECS[:, :, q:q + 1].broadcast_to([128, 8, 128]), op=ALU.mult)
                dve("tensor_tensor", [SOb.c(q % 2), ("ps", bsu), ("ps", bsu + 1)], [SOb.c(q % 2)], out=sov, in0=sov,
                    in1=pst[bsu // 2][:, :].rearrange("p (h v) -> p h v", h=8, v=128), op=ALU.subtract)
                sp_dma(hs[l, q].rearrange("h d v -> d h v"), sov, [SOb.c(q % 2)], [], base="so", nsem=2)
            act(OT.v[:, :, tS:tS + 64], bank(bo).rearrange("p (h t) -> p h t", h=8, t=64), AF.Copy,
                [("ps", bo)], scell(OT, tS, 64))
            held.discard(bo)

        SQ2 = UB("SQ2", 23040, BF16, [4, 360])
        RS = UB("RS", 25920, F32, [4, 360])
        NRB = 4
        O2 = UB("O2", 34560, BF16, [8, GT])
        SGO = UB("SGO", 31680, F32, [2, 360])
        items = [(h, s) for h in range(8) for s in range(2)]

        def rms_sq(ii):
            h, s = items[ii]
            t0, n = subs[s]
            act(SQ2.v[:, ii % 4, 0:n], OT.v[:, h, t0:t0 + n], AF.Square, [OT.c(h, s)], [SQ2.c(ii % 4)])
        rms_sq(0)
        rms_sq(1)
        for ii, (h, s) in enumerate(items):
            t0, n = subs[s]
            p2 = ii % 4
            if ii + 2 < len(items):
                rms_sq(ii + 2)
            b = alloc_bank()
            mm_group(bank(b, n), [(ones128[:], SQ2.v[:, p2, 0:n])], ["ones128", SQ2.c(p2)], [("ps", b)])
            act(RS.v[:, p2, 0:n], bank(b, n), AF.Sqrt, [("ps", b), "epsr"], [RS.c(p2)], bias=epsr[:], scale=1.0)
            dve("reciprocal", [RS.c(p2)], [RS.c(p2)], out=RS.v[:, p2, 0:n], in_=RS.v[:, p2, 0:n])
            dve("tensor_tensor", [RS.c(p2), OT.c(h, s)], [OT.c(h, s)], out=OT.v[:, h, t0:t0 + n],
                in0=OT.v[:, h, t0:t0 + n], in1=RS.v[:, p2, 0:n], op=ALU.mult)
        sgi = [0]
        for hh in (0, 1):
            def cons_og(i, s, t0, n, b, hh=hh):
                h = 4 * hh + i
                p2 = sgi[0] % 2
                sgi[0] += 1
                act(SGO.v[:, p2, 0:n], bank(b, n), AF.Silu, [("ps", b)], [SGO.c(p2)])
                dve("scalar_tensor_tensor", [SGO.c(p2), OT.c(h, s), "normw"], [O2.c(h, s)],
                    out=O2.v[:, h, t0:t0 + n], in0=OT.v[:, h, t0:t0 + n], scalar=normw[:, l:l + 1],
                    in1=SGO.v[:, p2, 0:n], op0=ALU.mult, op1=ALU.mult)
            dense8([0, 128, 256, 384], subs, cons_og)

        TGA = UB("TGA", 46080, BF16, [8, GT])
        for hh in (0, 1):
            def cons_ga(i, s, t0, n, b, hh=hh):
                c = 4 * hh + i
                act(TGA.v[:, c, t0:t0 + n], bank(b, n), AF.Tanh, [("ps", b)], [TGA.c(c, s)], scale=0.5)
            dense8([0, 128, 256, 384], subs, cons_ga)
        MA = UB("MA", 0, F32, [8, GT])
        for hh in (0, 1):
            def cons_wa(i, s, t0, n, b, hh=hh):
                c = 4 * hh + i
                dve("scalar_tensor_tensor", [("ps", b), TGA.c(c, s)], [MA.c(c, s)], out=MA.v[:, c, t0:t0 + n],
                    in0=TGA.v[:, c, t0:t0 + n], scalar=1.0, in1=bank(b, n), op0=ALU.add, op1=ALU.mult)
            dense8_from(O2, lambda k, s: O2.c(k, s), subs, cons_wa)

        U2 = UB("U2", 23040, F32, [8, GT + 2])
        CV = UB("CV", 57664, F32, [8, GT])
        CGT = UB("CGT", 80704, F32, [2, 360])
        Gp = 640 if has_sample else G
        for c in range(8):
            pass
        dve("tensor_copy", ["ubuf"], [U2.c("halo")], out=U2.v[:, :, 0:2], in_=ubuf[:, l, :, :])
        cgi = [0]
        for j in range(4):
            def cons_ch(i, s, t0, n, b, j=j):
                c = 2 * j + (i % 2)
                if i < 2:
                    act(CGT.v[:, i, 0:n], bank(b, n), AF.Copy, [("ps", b)], [CGT.c(i)])
                else:
                    dve("tensor_tensor", [("ps", b), CGT.c(i - 2)], [U2.c(c, s)], out=U2.v[:, c, 2 + t0:2 + t0 + n],
                        in0=bank(b, n), in1=CGT.v[:, i - 2, 0:n], op=ALU.mult)
            dense8([0, 128, 256, 384], subs, cons_ch)
        for c in range(8):
            uc = [U2.c(c, 0), U2.c(c, 1), U2.c("halo")]
            dve("tensor_scalar", uc + ["convw"], [CV.c(c)], out=CV.v[:, c, 0:Gp], in0=U2.v[:, c, 0:Gp],
                scalar1=convw[:, l, 0, c:c + 1], scalar2=None, op0=ALU.mult)
            dve("scalar_tensor_tensor", uc + ["convw", CV.c(c)], [CV.c(c)], out=CV.v[:, c, 0:Gp],
                in0=U2.v[:, c, 1:Gp + 1], scalar=convw[:, l, 1, c:c + 1], in1=CV.v[:, c, 0:Gp],
                op0=ALU.mult, op1=ALU.add)
            dve("scalar_tensor_tensor", uc + ["convw", CV.c(c)], [CV.c(c)], out=CV.v[:, c, 0:Gp],
                in0=U2.v[:, c, 2:Gp + 2], scalar=convw[:, l, 2, c:c + 1], in1=CV.v[:, c, 0:Gp],
                op0=ALU.mult, op1=ALU.add)
        allu = [U2.c(c, s) for c in range(8) for s in range(2)] + [U2.c("halo")]
        allcv = [CV.c(c) for c in range(8)]
        dve("tensor_copy", allu, ["ubuf"], out=ubuf[:, l, :, :], in_=U2.v[:, :, Gp:Gp + 2])
        if has_sample:
            act(CPO[:, l, :, :], U2.v[:, :, Gp:Gp + 2], AF.Copy, allu, [("CPO", l)])
            dve("tensor_copy", ["CB"], ["US"], out=US[:, :, :, 0:2], in_=CB[:, l, :, :, :])
            dve("tensor_copy", allu + ["US"], ["US"], out=US[:, :, :, 2:6],
                in_=U2.v[:, :, 2 + 640:2 + 704].rearrange("p c (q t) -> p c q t", q=16, t=4))
            act(CSO[:, l, :, :, :], US[:, :, :, 4:6], AF.Copy, ["US"], [("CSO", l)])
            for jx in range(3):
                wbc = convw[:, l, jx, :].unsqueeze(2).unsqueeze(3).broadcast_to([128, 8, 16, 4])
                if jx == 0:
                    dve("tensor_tensor", ["US", "convw"], ["CVS"], out=CVS[:], in0=US[:, :, :, 0:4], in1=wbc,
                        op=ALU.mult)
                else:
                    dve("tensor_tensor", ["US", "convw"], ["CVS2"], out=CVS2[:], in0=US[:, :, :, jx:jx + 4], in1=wbc,
                        op=ALU.mult)
                    dve("tensor_tensor", ["CVS", "CVS2"], ["CVS"], out=CVS[:], in0=CVS[:], in1=CVS2[:], op=ALU.add)
            dve("tensor_copy", ["CVS"] + allcv, allcv, out=CV.v[:, :, 640:704],
                in_=CVS[:].rearrange("p c q t -> p c (q t)"))
        TGB = UB("TGB", 46144, BF16, [8, GT])
        for hh in (0, 1):
            def cons_gb(i, s, t0, n, b, hh=hh):
                c = 4 * hh + i
                act(TGB.v[:, c, t0:t0 + n], bank(b, n), AF.Tanh, [("ps", b)], [TGB.c(c, s)], scale=0.5)
            dense8([0, 128, 256, 384], subs, cons_gb)
        BC = UB("BC", 23040, BF16, [8, GT])
        for hh in (0, 1):
            def cons_bg(i, s, t0, n, b, hh=hh):
                c = 4 * hh + i
                dve("tensor_tensor", [("ps", b)] + allcv, [BC.c(c, s)], out=BC.v[:, c, t0:t0 + n], in0=bank(b, n),
                    in1=CV.v[:, c, t0:t0 + n], op=ALU.mult)
            dense8([0, 128, 256, 384], subs, cons_bg)
        M = UB("M", 34560, BF16, [8, GT])
        YTMP = UB("YTMP", 57664, F32, [2, 360])
        yi = [0]
        for hh in (0, 1):
            def cons_wb(i, s, t0, n, b, hh=hh):
                c = 4 * hh + i
                p2 = yi[0] % 2
                yi[0] += 1
                dve("scalar_tensor_tensor", [("ps", b), TGB.c(c, s)], [YTMP.c(p2)], out=YTMP.v[:, p2, 0:n],
                    in0=TGB.v[:, c, t0:t0 + n], scalar=1.0, in1=bank(b, n), op0=ALU.add, op1=ALU.mult)
                dve("tensor_tensor", [YTMP.c(p2), MA.c(c, s)], [M.c(c, s)], out=M.v[:, c, t0:t0 + n],
                    in0=YTMP.v[:, p2, 0:n], in1=MA.v[:, c, t0:t0 + n], op=ALU.add)
            dense8_from(BC, lambda k, s: BC.c(k, s), subs, cons_wb)
        def cons_wo(c, s, t0, n, b):
            dve("scalar_tensor_tensor", [("ps", b), ("xf", c, s)], [("xf", c, s)], out=xf[:, c, t0:t0 + n],
                in0=bank(b, n), scalar=CR, in1=xf[:, c, t0:t0 + n], op0=ALU.mult, op1=ALU.add)
        slA, wvA = next_block()
        slB, wvB = next_block()
        for s, (t0, n) in enumerate(subs):
            for hh, (sl, wv) in enumerate(((slA, wvA), (slB, wvB))):
                for i in range(4):
                    b = alloc_bank()
                    mm_group(bank(b, n), [(wv[:, k, i * 128:(i + 1) * 128], M.v[:, k, t0:t0 + n]) for k in range(8)],
                             [("w", sl)] + [M.c(k, s) for k in range(8)], [("ps", b)])
                    cons_wo(4 * hh + i, s, t0, n, b)
                if s == 1 and hh == 0:
                    ln_sub(l, 1, subs, 0, 46080, 57600, 69120)
        prefetch()
        ln_sub(l, 1, subs, 1, 46080, 57600, 69120)

    for g in range(NG):
        g0, G = GROUPS[g]
        n = G // 2
        subs = [(0, n), (n, G - n)]
        sp_dma(xf[:, :, 0:G], xT[:, :, g0:g0 + G], [], [("xf", c, s) for c in range(8) for s in range(2)])
        sp_dma(Et[:], cE[:, g, :], [], ["Et"])
        dve("tensor_scalar", ["Et"], ["nEt"], out=nEt[:], in0=Et[:], scalar1=-1.0, scalar2=1.0, op0=ALU.mult,
            op1=ALU.add)
        for s, (t0, nn) in enumerate(subs):
            act(xb[:, :, t0:t0 + nn], xf[:, :, t0:t0 + nn], AF.Copy, xcells("xf", s), xcells("xb", s))
        for l in range(NL):
            ffn(l, 0, subs)
            if DEBUG["stop"] == "ffn1":
                break
            mixer(g, l, subs)
            if DEBUG["stop"] == "mixer":
                break
            ffn(l, 1, subs)
        allxf = [("xf", c, s) for c in range(8) for s in range(2)]
        if g == 0:
            sp_dma(yT[:, :, 0:G - 16], xf[:, :, 16:G], allxf, [], base="yo", nsem=1)
        else:
            sp_dma(yT[:, :, g0 - 16:g0 - 16 + G], xf[:, :, 0:G], allxf, [], base="yo", nsem=1)
    if DEBUG["stop"] is None:
      sp_dma(cso[:, 0:NL], CSO[:, 0:NL], [("CSO", l) for l in range(NL)], [], base="yo", nsem=1)
      sp_dma(cpo[:, 0:NL], CPO[:, 0:NL], [("CPO", l) for l in range(NL)], [], base="yo", nsem=1)

    cum = {}
    for en in Tr.ENGS:
        c = 0
        arr = []
        for o in tr.ops[en]:
            if o["signal"]:
                c += 1
            arr.append(c)
        cum[en] = arr
    dsems = sorted(tr.dval.keys())
    semctx = {}
    for en in Tr.ENGS:
        cm = nc.semaphore("e_" + en)
        semctx[("e", en)] = cm.__enter__()
        ctxs.append(cm)
    for ds in dsems:
        cm = nc.semaphore("d_" + ds)
        semctx[("d", ds)] = cm.__enter__()
        ctxs.append(cm)

    def replay(en, e):
        my = semctx[("e", en)]
        for o in tr.ops[en]:
            for (k, v) in o["waits"]:
                if k[0] == "e":
                    e.wait_ge(semctx[k], cum[k[1]][v])
                else:
                    e.wait_ge(semctx[k], v)
            inss = o["fn"](e)
            if o["dma"] is not None:
                for ins in inss:
                    ins.then_inc(semctx[("d", o["dma"])], 16)
            elif o["signal"]:
                inss[-1].then_inc(my, 1)
        if en == "sp":
            for ds in dsems:
                e.wait_ge(semctx[("d", ds)], tr.dval[ds])

    with nc.Block() as block:
        @block.tensor
        def _(e):
            replay("pe", e)

        @block.scalar
        def _(e):
            replay("act", e)

        @block.vector
        def _(e):
            replay("dve", e)

        @block.gpsimd
        def _(e):
            replay("pool", e)

        @block.sync
        def _(e):
            replay("sp", e)
    for cm in reversed(ctxs):
        cm.__exit__(None, None, None)
    stats = {en: len(tr.ops[en]) for en in Tr.ENGS}
    return nc, stats


def host_consts():
    ident = np.eye(128, dtype=np.float32)
    s_ = np.arange(64)[:, None]
    t_ = np.arange(64)[None, :]
    mc = np.where(s_ <= t_, -1.0, 0.0).astype(np.float32)
    ms = np.where((s_ <= t_) & (s_ // 4 == t_ // 4), -1.0, 0.0).astype(np.float32)
    cmask = np.zeros((64, 2, 512), np.float32)
    cmask[:, 0, :] = np.tile(mc, (1, 8))
    cmask[:, 1, :] = np.tile(ms, (1, 8))
    crow = (np.arange(64)[:, None] // 4 == np.arange(16)[None, :]).astype(np.float32)
    E = np.zeros((3, GT), np.float32)
    E[0, 0] = 1.0
    E[0, 16::64] = 1.0
    E[1, 0:704:64] = 1.0
    E[2, 0:640:64] = 1.0
    E[2, 640:704:4] = 1.0
    cE = np.ascontiguousarray(np.broadcast_to(E[None], (128, 3, GT)))
    return ident, cmask, crow, cE


_CACHE = {}


def kernel(x_prompt, x_sample, state_hgrn, state_conv, meta_tokens, w_in, hgrn_lb_logits,
           hgrn_norm_w, conv_w, w_branch_a, w_branch_b, w_out, ffn1_w_gu, ffn1_w_down,
           ffn2_w_gu, ffn2_w_down, ln_g, ln_b):
    f32 = np.float32
    x_prompt = np.asarray(x_prompt, f32)
    x_sample = np.asarray(x_sample, f32)
    state_hgrn = np.asarray(state_hgrn, f32)
    state_conv = np.asarray(state_conv, f32)
    meta = np.asarray(meta_tokens, f32)
    nc, stats = build_program()
    ident, cmask, crow, cE = host_consts()

    def fm(a):
        a = np.asarray(a, f32)
        sh = a.shape[:-1]
        return np.ascontiguousarray(np.moveaxis(a.reshape(sh + (8, 128)), -1, 0))

    def dn_layout(w):
        w = np.asarray(w, f32).reshape(4, NJ, 128, 8, 128)
        return np.ascontiguousarray(w.transpose(0, 3, 2, 1, 4))

    shared = {
        "w_in": np.ascontiguousarray(np.asarray(w_in, f32)),
        "w_a": np.ascontiguousarray(np.asarray(w_branch_a, f32)),
        "w_b": np.ascontiguousarray(np.asarray(w_branch_b, f32)),
        "w_o": np.ascontiguousarray(np.asarray(w_out, f32)),
        "f1gu": np.ascontiguousarray(np.asarray(ffn1_w_gu, f32)),
        "f2gu": np.ascontiguousarray(np.asarray(ffn2_w_gu, f32)),
        "f1d": dn_layout(ffn1_w_down),
        "f2d": dn_layout(ffn2_w_down),
        "plng": fm(ln_g), "plnb": fm(ln_b), "plog": fm(hgrn_lb_logits), "pconv": fm(conv_w),
        "pnorm": np.ascontiguousarray(np.asarray(hgrn_norm_w, f32).T),
        "cident": ident, "cmask": cmask, "crow": crow, "cE": cE,
    }
    in_maps = []
    for c in range(8):
        toks = np.concatenate([meta, x_prompt[c], x_sample[16 * c:16 * c + 16].reshape(64, D)], axis=0)
        xT = np.ascontiguousarray(toks.reshape(NTOK, 8, 128).transpose(2, 1, 0))
        m = dict(shared)
        m["xT"] = xT
        m["s0"] = np.ascontiguousarray(state_hgrn[:, 16 * c:16 * c + 16])
        sc = state_conv[:, 16 * c:16 * c + 16]
        m["cb"] = np.ascontiguousarray(sc.reshape(4, 16, 2, 8, 128).transpose(4, 0, 3, 1, 2))
        in_maps.append(m)
    res = run_bass_kernel_spmd(nc, in_maps, core_ids=list(range(8)))
    y_prompt = np.zeros((8, 2048, D), f32)
    y_sample = np.zeros((128, 4, D), f32)
    nhp = np.zeros((4, 8, 8, 128, 128), f32)
    ncp = np.zeros((4, 8, 2, 1024), f32)
    nhs = np.zeros((4, 128, 8, 128, 128), f32)
    ncs = np.zeros((4, 128, 2, 1024), f32)
    for c in range(8):
        r = res.results[c]
        y = np.asarray(r["yT"]).transpose(2, 1, 0).reshape(NTOK - 16, D)
        y_prompt[c] = y[0:2048]
        y_sample[16 * c:16 * c + 16] = y[2048:].reshape(16, 4, D)
        nhp[:, c] = np.asarray(r["hp"])
        nhs[:, 16 * c:16 * c + 16] = np.asarray(r["hs"])
        cso = np.asarray(r["cso"])
        ncs[:, 16 * c:16 * c + 16] = cso.transpose(1, 3, 4, 2, 0).reshape(4, 16, 2, 1024)
        cpo = np.asarray(r["cpo"])
        ncp[:, c] = cpo.transpose(1, 3, 2, 0).reshape(4, 2, 1024)
    return (y_prompt, y_sample, nhp, ncp, nhs, ncs)
```

```python
import numpy as np
import concourse.bass as bass
import concourse.mybir as mybir
from concourse.bass_utils import run_bass_kernel_spmd

F32 = mybir.dt.float32
BF16 = mybir.dt.bfloat16
F32R = mybir.dt.float32r
AF = mybir.ActivationFunctionType
ALU = mybir.AluOpType

D = 1024
DFF = 2816
DEPTH = 4
NCK = 8
NJ = 22
NTOK = 2128
GROUPS = [(0, 720), (720, 704), (1424, 704)]
GT = 720
ALPHA = (2 * DEPTH) ** 0.25
CR = 0.5 / ALPHA
LN_EPS_P = 1e-5 / (ALPHA * ALPHA)
RMS_EPS = 1e-6
SLOT = 4096
NSLOT = 3
UBYTES = 90112

DEBUG = {"layers": DEPTH, "groups": 3, "stop": None}


class Tr:
    ENGS = ("pe", "act", "dve", "pool", "sp")

    def __init__(self):
        self.ops = {n: [] for n in self.ENGS}
        self.waited = {n: {} for n in self.ENGS}
        self.cells = {}
        self.dval = {}
        self.live = []
        self.ninst = 0

    def cell(self, key):
        c = self.cells.get(key)
        if c is None:
            pend = {}
            if isinstance(key, tuple) and isinstance(key[0], UInst):
                assert key[0].alive, ("access to retired buffer", key[0].name)
                pend = dict(key[0].pending)
            c = [{}, pend]
            self.cells[key] = c
        return c

    def op(self, eng, fn, reads=(), writes=(), dma_sem=None, ndma=1):
        deps = {}

        def merge(d):
            for k, v in d.items():
                if deps.get(k, -1) < v:
                    deps[k] = v
        for c in reads:
            merge(self.cell(c)[0])
        for c in writes:
            cc = self.cell(c)
            merge(cc[0])
            merge(cc[1])
        if dma_sem is not None:
            prev = self.dval.get(dma_sem, 0)
            if prev > 0:
                deps[("d", dma_sem)] = max(deps.get(("d", dma_sem), -1), prev)
        waits = []
        wd = self.waited[eng]
        for k, v in deps.items():
            if k == ("e", "pe") and eng == "pe":
                continue
            if wd.get(k, -1) >= v:
                continue
            wd[k] = v
            waits.append((k, v))
            if k[0] == "e":
                self.ops[k[1]][v]["signal"] = True
        idx = len(self.ops[eng])
        if dma_sem is not None:
            self.dval[dma_sem] = self.dval.get(dma_sem, 0) + 16 * ndma
            ev = (("d", dma_sem), self.dval[dma_sem])
        else:
            ev = (("e", eng), idx)
        self.ops[eng].append({"fn": fn, "waits": waits, "signal": False, "dma": dma_sem})
        for c in reads:
            r = self.cell(c)[1]
            if r.get(ev[0], -1) < ev[1]:
                r[ev[0]] = ev[1]
        for c in writes:
            self.cells[c] = [{ev[0]: ev[1]}, {}]
        return ev


class UInst:
    def __init__(self, tr, name, off, nbytes):
        self.name, self.off, self.nbytes = name, off, nbytes
        self.alive = True
        self.pending = {}
        keep = []
        for o in tr.live:
            if o.off < off + nbytes and off < o.off + o.nbytes:
                if o.alive:
                    o.alive = False
                    for k, c in list(tr.cells.items()):
                        if isinstance(k, tuple) and k[0] is o:
                            for d in (c[0], c[1]):
                                for kk, vv in d.items():
                                    if o.pending.get(kk, -1) < vv:
                                        o.pending[kk] = vv
                            del tr.cells[k]
                for kk, vv in o.pending.items():
                    if self.pending.get(kk, -1) < vv:
                        self.pending[kk] = vv
                if not (off <= o.off and o.off + o.nbytes <= off + nbytes):
                    keep.append(o)
            else:
                keep.append(o)
        keep.append(self)
        tr.live = keep


def group_chunks(g):
    if g == 0:
        return [(0, 16)] + [(16 + 64 * i, 64) for i in range(11)]
    if g == 1:
        return [(64 * i, 64) for i in range(11)]
    return [(64 * i, 64) for i in range(10)]


def build_program():
    nc = bass.Bass("TRN2", target_bir_lowering=False)
    NL = DEBUG["layers"]
    NG = DEBUG["groups"]

    def din(name, shape):
        return nc.dram_tensor(name, shape, F32, kind="ExternalInput").ap()

    def dout(name, shape):
        return nc.dram_tensor(name, shape, F32, kind="ExternalOutput").ap()

    xT = din("xT", [128, 8, NTOK])
    s0 = din("s0", [4, 16, 8, 128, 128])
    cb = din("cb", [128, 4, 8, 16, 2])
    w_in = din("w_in", [4, D, 9 * D])
    w_a = din("w_a", [4, D, D])
    w_b = din("w_b", [4, D, D])
    w_o = din("w_o", [4, D, D])
    fgu = [din("f1gu", [4, D, 2 * DFF]), din("f2gu", [4, D, 2 * DFF])]
    fdn = [din("f1d", [4, 8, 128, NJ, 128]), din("f2d", [4, 8, 128, NJ, 128])]
    plng = din("plng", [128, 4, 3, 8])
    plnb = din("plnb", [128, 4, 3, 8])
    plog = din("plog", [128, 4, 8])
    pconv = din("pconv", [128, 4, 3, 8])
    pnorm = din("pnorm", [128, 4])
    cident = din("cident", [128, 128])
    cmask = din("cmask", [64, 2, 512])
    crow = din("crow", [64, 16])
    cE = din("cE", [128, 3, GT])

    yT = dout("yT", [128, 8, NTOK - 16])
    hs = dout("hs", [4, 16, 8, 128, 128])
    hp = dout("hp", [4, 8, 128, 128])
    cso = dout("cso", [128, 4, 8, 16, 2])
    cpo = dout("cpo", [128, 4, 8, 2])

    tr = Tr()
    sb = {}
    ctxs = []

    def salloc(name, shape, dt):
        cm = nc.sbuf_tensor(name, shape, dt)
        t = cm.__enter__()
        ctxs.append(cm)
        sb[name] = t
        return t

    xf = salloc("xf", [128, 8, GT], F32)
    xb = salloc("xb", [128, 8, GT], BF16)
    S = salloc("S", [128, 8, 128], F32)
    Sb = salloc("Sb", [128, 8, 128], BF16)
    W = salloc("W", [128, NSLOT, SLOT], BF16)
    U = salloc("U", [128, UBYTES // 4], F32)
    identf = salloc("identf", [128, 128], F32)
    identb = salloc("identb", [128, 128], BF16)
    ones1k = salloc("ones1k", [128, 128], BF16)
    ones128 = salloc("ones128", [128, 128], BF16)
    zerob = salloc("zerob", [128, 128], BF16)
    lng = salloc("lng", [128, 4, 3, 8], F32)
    lnb = salloc("lnb", [128, 4, 3, 8], F32)
    lgt = salloc("lgt", [128, 4, 8], F32)
    lbt = salloc("lbt", [128, 4, 8], F32)
    c0t = salloc("c0t", [128, 4, 8], F32)
    c1t = salloc("c1t", [128, 4, 8], F32)
    sm8 = salloc("sm8", [128, 3, 8], F32)
    convw = salloc("convw", [128, 4, 3, 8], F32)
    normw = salloc("normw", [128, 4], F32)
    maskc = salloc("maskc", [64, 2, 512], F32)
    rowm = salloc("rowm", [64, 16], F32)
    Et = salloc("Et", [128, GT], F32)
    nEt = salloc("nEt", [128, GT], F32)
    ubuf = salloc("ubuf", [128, 4, 8, 2], F32)
    DEC = salloc("DEC", [128, 8, 12], F32)
    DECS = salloc("DECS", [128, 8, 16], F32)
    CB = salloc("CB", [128, 4, 8, 16, 2], F32)
    US = salloc("US", [128, 8, 16, 6], F32)
    CVS = salloc("CVS", [128, 8, 16, 4], F32)
    CVS2 = salloc("CVS2", [128, 8, 16, 4], F32)
    QTF = salloc("QTF", [128, 8, 64], F32)
    epsl = salloc("epsl", [128, 1], F32)
    epsr = salloc("epsr", [128, 1], F32)
    CSO = salloc("CSO", [128, 4, 8, 16, 2], F32)
    CPO = salloc("CPO", [128, 4, 8, 2], F32)

    pst = []
    for i in range(4):
        cm = nc.psum_tensor("ps%d" % i, [128, 1024], F32)
        pst.append(cm.__enter__())
        ctxs.append(cm)

    def bank(b, n=512):
        return pst[b // 2][:, (b % 2) * 512:(b % 2) * 512 + n]

    def bank_bf(b):
        return pst[b // 2][:, (b % 2) * 512:(b % 2) * 512 + 512].bitcast(BF16)

    bank_rr = [0]
    held = set()

    def alloc_bank():
        while True:
            b = bank_rr[0]
            bank_rr[0] = (b + 1) % 8
            if b not in held:
                return b

    def alloc_bank2():
        while True:
            if bank_rr[0] % 2:
                bank_rr[0] = (bank_rr[0] + 1) % 8
            b = bank_rr[0]
            bank_rr[0] = (b + 2) % 8
            if b not in held and (b + 1) not in held:
                return b

    def uview(off, dt, shape):
        esz = 2 if dt == BF16 else 4
        n = int(np.prod(shape))
        assert off % 4 == 0 and off + n * esz <= UBYTES, (off, shape)
        v = U[:, off // 4: off // 4 + (n * esz) // 4]
        if dt != F32:
            v = v.bitcast(dt)
        if len(shape) == 2:
            return v.rearrange("p (a b) -> p a b", a=shape[0], b=shape[1])
        if len(shape) == 3:
            return v.rearrange("p (a b c) -> p a b c", a=shape[0], b=shape[1], c=shape[2])
        return v

    class UB:
        def __init__(self, name, off, dt, shape):
            esz = 2 if dt == BF16 else 4
            self.inst = UInst(tr, name, off, int(np.prod(shape)) * esz)
            self.v = uview(off, dt, shape)

        def c(self, *key):
            return (self.inst,) + key

    wstate = {"n": 0}

    def load_block(parts, nk):
        i = wstate["n"]
        wstate["n"] += 1
        slot = i % NSLOT
        ncols = sum(p[2] for p in parts)
        assert nk * ncols <= SLOT
        wv = W[:, slot, 0:nk * ncols].rearrange("p (k c) -> p k c", k=nk, c=ncols)

        def fn(e, parts=parts, wv=wv):
            out = []
            for (src, co, ncp) in parts:
                sv = src if len(src.shape) == 3 else src.rearrange("(k p) c -> p k c", p=128)
                out.append(e.dma_start(out=wv[:, :, co:co + ncp], in_=sv))
            return out
        tr.op("pool", fn, reads=(), writes=[("w", slot)], dma_sem="w%d" % slot, ndma=len(parts))
        return slot, wv

    pending_blocks = []

    def block_sched2():
        for g in range(NG):
            for l in range(NL):
                for phase in ("ffn1", "mixer", "ffn2"):
                    if phase == "mixer":
                        wl = w_in[l]

                        def seg(sg, hh):
                            return [(wl[:, 1024 * sg + 512 * hh: 1024 * sg + 512 * hh + 512], 0, 512)]
                        for hh in (0, 1):
                            yield (seg(1, hh), 8)
                            yield (seg(2, hh), 8)
                            yield (seg(0, hh), 8)
                        for hh in (0, 1):
                            yield (seg(3, hh), 8)
                        for hh in (0, 1):
                            yield (seg(7, hh), 8)
                        for hh in (0, 1):
                            yield ([(w_a[l][:, 512 * hh: 512 * hh + 512], 0, 512)], 8)
                        for j in range(4):
                            yield ([(wl[:, 5120 + 256 * j: 5120 + 256 * j + 256], 0, 256),
                                    (wl[:, 6144 + 256 * j: 6144 + 256 * j + 256], 256, 256)], 8)
                        for hh in (0, 1):
                            yield (seg(8, hh), 8)
                        for hh in (0, 1):
                            yield (seg(4, hh), 8)
                        for hh in (0, 1):
                            yield ([(w_b[l][:, 512 * hh: 512 * hh + 512], 0, 512)], 8)
                        for hh in (0, 1):
                            yield ([(w_o[l][:, 512 * hh: 512 * hh + 512], 0, 512)], 8)
                    else:
                        fi = 0 if phase == "ffn1" else 1
                        gu = fgu[fi][l]
                        dn = fdn[fi][l]
                        for j in range(11):
                            yield ([(gu[:, 256 * j: 256 * j + 256], 0, 256),
                                    (gu[:, DFF + 256 * j: DFF + 256 * j + 256], 256, 256)], 8)
                        for rep in range(2):
                            for c in range(8):
                                yield ([(dn[c], 0, 128)], NJ)

    sched = block_sched2()
    prefetched = []

    def prefetch(maxn=99):
        k = 0
        while len(prefetched) < NSLOT and k < maxn:
            k += 1
            try:
                parts, nk = next(sched)
            except StopIteration:
                return
            prefetched.append(load_block(parts, nk))

    def next_block():
        if not prefetched:
            prefetch()
        return prefetched.pop(0)

    def mm_group(out_ap, pairs, reads, writes, first_start=True):
        def fn(e, out_ap=out_ap, pairs=pairs, first_start=first_start):
            ins = None
            n = len(pairs)
            for i, (l_, r_) in enumerate(pairs):
                ins = e.matmul(out_ap, lhsT=l_, rhs=r_, start=(first_start and i == 0), stop=(i == n - 1))
            return [ins]
        tr.op("pe", fn, reads=reads, writes=writes)

    def act(out, in_, func, reads, writes, bias=None, scale=None):
        kw = {}
        if bias is not None:
            kw["bias"] = bias
        if scale is not None:
            kw["scale"] = scale

        def fn(e):
            return [e.activation(out=out, in_=in_, func=func, **kw)]
        tr.op("act", fn, reads=reads, writes=writes)

    def dve(kind, reads, writes, eng="dve", **kw):
        def fn(e):
            return [getattr(e, kind)(**kw)]
        tr.op(eng, fn, reads=reads, writes=writes)

    def dma(eng, out, in_, reads, writes, sem):
        def fn(e):
            return [e.dma_start(out=out, in_=in_)]
        tr.op(eng, fn, reads=reads, writes=writes, dma_sem=sem)

    spsem = [0]

    def sp_dma(out, in_, reads, writes, nsem=4, base="io", sem=None):
        s = sem if sem is not None else "%s%d" % (base, spsem[0] % nsem)
        spsem[0] += 1
        dma("sp", out, in_, reads, writes, s)

    def load_const(t, src, key):
        sp_dma(t[:] if not isinstance(t, bass.AP) else t, src, [], [key])

    load_const(identf, cident, "identf")
    load_const(lng, plng, "lng")
    load_const(lnb, plnb, "lnb")
    load_const(lgt, plog, "lgt")
    load_const(convw, pconv, "convw")
    load_const(normw, pnorm, "normw")
    load_const(maskc, cmask, "maskc")
    load_const(rowm, crow, "rowm")
    load_const(CB, cb, "CB")
    dve("tensor_copy", ["identf"], ["identb"], out=identb[:], in_=identf[:])
    dve("memset", [], ["ones1k"], ap=ones1k[:], constant=1.0 / 1024.0)
    dve("memset", [], ["ones128"], ap=ones128[:], constant=1.0 / 128.0)
    dve("memset", [], ["zerob"], ap=zerob[:], constant=0.0)
    dve("memset", [], ["epsl"], ap=epsl[:], constant=LN_EPS_P)
    dve("memset", [], ["epsr"], ap=epsr[:], constant=RMS_EPS)
    dve("memset", [], ["ubuf"], ap=ubuf[:], constant=0.0)
    dve("tensor_tensor", ["lgt"], ["sm8"], out=sm8[:, 0, :], in0=lgt[:, 0, :], in1=lgt[:, 1, :], op=ALU.max)
    dve("tensor_tensor", ["lgt", "sm8"], ["sm8"], out=sm8[:, 0, :], in0=sm8[:, 0, :], in1=lgt[:, 2, :], op=ALU.max)
    dve("tensor_tensor", ["lgt", "sm8"], ["sm8"], out=sm8[:, 0, :], in0=sm8[:, 0, :], in1=lgt[:, 3, :], op=ALU.max)
    dve("tensor_tensor", ["lgt", "sm8"], ["lbt"], out=lbt[:], in0=lgt[:],
        in1=sm8[:, 0:1, :].broadcast_to([128, 4, 8]), op=ALU.subtract)
    act(lbt[:], lbt[:], AF.Exp, ["lbt"], ["lbt"])
    dve("tensor_tensor", ["lbt"], ["sm8"], out=sm8[:, 1, :], in0=lbt[:, 0, :], in1=lbt[:, 1, :], op=ALU.add)
    dve("tensor_tensor", ["lbt", "sm8"], ["sm8"], out=sm8[:, 1, :], in0=sm8[:, 1, :], in1=lbt[:, 2, :], op=ALU.add)
    dve("tensor_tensor", ["lbt", "sm8"], ["sm8"], out=sm8[:, 1, :], in0=sm8[:, 1, :], in1=lbt[:, 3, :], op=ALU.add)
    dve("reciprocal", ["sm8"], ["sm8"], out=sm8[:, 2, :], in_=sm8[:, 1, :])
    dve("tensor_tensor", ["lbt", "sm8"], ["lgt"], out=lgt[:], in0=lbt[:],
        in1=sm8[:, 2:3, :].broadcast_to([128, 4, 8]), op=ALU.mult)
    dve("memset", [], ["lbt"], ap=lbt[:, 0, :], constant=0.0)
    dve("tensor_copy", ["lgt", "lbt"], ["lbt"], out=lbt[:, 1, :], in_=lgt[:, 1, :])
    dve("tensor_tensor", ["lgt", "lbt"], ["lbt"], out=lbt[:, 2, :], in0=lbt[:, 1, :], in1=lgt[:, 2, :], op=ALU.add)
    dve("tensor_tensor", ["lgt", "lbt"], ["lbt"], out=lbt[:, 3, :], in0=lbt[:, 2, :], in1=lgt[:, 3, :], op=ALU.add)
    dve("tensor_scalar", ["lbt"], ["c1t"], out=c1t[:], in0=lbt[:], scalar1=-0.5, scalar2=0.5,
        op0=ALU.mult, op1=ALU.add)
    dve("tensor_tensor", ["lbt", "c1t"], ["c0t"], out=c0t[:], in0=lbt[:], in1=c1t[:], op=ALU.add)

    prefetch()

    def xcells(name, s):
        return [(name, c, s) for c in range(8)]

    def ln_sub(l, k, subs, s, off_sq, off_yt, off_st):
        nmax = max(n for _, n in subs)
        t0, n = subs[s]
        SQ = UB("SQ", off_sq, BF16, [16, nmax])
        YT = UB("YT", off_yt, F32, [8, nmax])
        ST = UB("ST", off_st, F32, [4, nmax])
        dve("tensor_copy", xcells("xf", s), [SQ.c(0)], out=SQ.v[:, 0:8, 0:n], in_=xf[:, :, t0:t0 + n])
        act(SQ.v[:, 8:16, 0:n], xf[:, :, t0:t0 + n], AF.Square, xcells("xf", s), [SQ.c(1)])
        bm = alloc_bank()
        be = alloc_bank()
        mm_group(bank(bm, n), [(ones1k[:], SQ.v[:, c, 0:n]) for c in range(8)],
                 ["ones1k", SQ.c(0)], [("ps", bm)])
        mm_group(bank(be, n), [(ones1k[:], SQ.v[:, 8 + c, 0:n]) for c in range(8)],
                 ["ones1k", SQ.c(1)], [("ps", be)])
        act(ST.v[:, 0, 0:n], bank(bm, n), AF.Identity, [("ps", bm)], [ST.c(0)])
        act(ST.v[:, 1, 0:n], bank(bm, n), AF.Square, [("ps", bm)], [ST.c(1)])
        dve("tensor_tensor", [("ps", be), ST.c(1)], [ST.c(2)], out=ST.v[:, 2, 0:n], in0=bank(be, n),
            in1=ST.v[:, 1, 0:n], op=ALU.subtract)
        act(ST.v[:, 2, 0:n], ST.v[:, 2, 0:n], AF.Sqrt, [ST.c(2), "epsl"], [ST.c(2)], bias=epsl[:], scale=1.0)
        dve("tensor_tensor", xcells("xf", s) + [ST.c(0)], [YT.c(0)], out=YT.v[:, :, 0:n],
            in0=xf[:, :, t0:t0 + n], in1=ST.v[:, 0:1, 0:n].broadcast_to([128, 8, n]), op=ALU.subtract)
        dve("reciprocal", [ST.c(2)], [ST.c(3)], out=ST.v[:, 3, 0:n], in_=ST.v[:, 2, 0:n])
        dve("tensor_tensor", [YT.c(0), ST.c(3)], [YT.c(0)], out=YT.v[:, :, 0:n],
            in0=YT.v[:, :, 0:n], in1=ST.v[:, 3:4, 0:n].broadcast_to([128, 8, n]), op=ALU.mult)
        for c in range(8):
            act(xb[:, c, t0:t0 + n], YT.v[:, c, 0:n], AF.Identity, [YT.c(0), "lng", "lnb"], [("xb", c, s)],
                scale=lng[:, l, k, c:c + 1], bias=lnb[:, l, k, c:c + 1])
            dve("tensor_scalar", [YT.c(0), "lng", "lnb"], [("xf", c, s)], out=xf[:, c, t0:t0 + n],
                in0=YT.v[:, c, 0:n], scalar1=lng[:, l, k, c:c + 1], scalar2=lnb[:, l, k, c:c + 1],
                op0=ALU.mult, op1=ALU.add)

    def ffn(l, fi, subs):
        H = UB("H", 0, BF16, [NJ, GT])
        SG = UB("SG", 60480, BF16, [4, 360])
        sgi = [0]

        def gu_groups(j, slot, wv, s):
            t0, n = subs[s]
            for jj in range(2):
                bg = alloc_bank()
                bu = alloc_bank()
                mm_group(bank(bg, n), [(wv[:, k, jj * 128:(jj + 1) * 128], xb[:, k, t0:t0 + n]) for k in range(8)],
                         [("w", slot)] + xcells("xb", s), [("ps", bg)])
                mm_group(bank(bu, n), [(wv[:, k, 256 + jj * 128:256 + (jj + 1) * 128], xb[:, k, t0:t0 + n])
                                       for k in range(8)],
                         [("w", slot)] + xcells("xb", s), [("ps", bu)])
                sgv = SG.v[:, sgi[0] % 4, 0:n]
                sgc = SG.c(sgi[0] % 4)
                sgi[0] += 1
                act(sgv, bank(bg, n), AF.Silu, [("ps", bg)], [sgc])
                dve("tensor_tensor", [sgc, ("ps", bu)], [H.c(2 * j + jj, s)], out=H.v[:, 2 * j + jj, t0:t0 + n],
                    in0=bank(bu, n), in1=sgv, op=ALU.mult)
        sl0, wv0 = next_block()
        sl1, wv1 = next_block()
        gu_groups(0, sl0, wv0, 0)
        gu_groups(1, sl1, wv1, 0)
        gu_groups(0, sl0, wv0, 1)
        prefetch(1)
        gu_groups(1, sl1, wv1, 1)
        prefetch(1)
        for j in range(2, 11):
            slot, wv = next_block()
            for s in range(2):
                gu_groups(j, slot, wv, s)
            prefetch()
        kk = 0 if fi == 0 else 2
        for s, (t0, n) in enumerate(subs):
            for c in range(8):
                slot, wv = next_block()
                b = alloc_bank()
                mm_group(bank(b, n), [(wv[:, k, 0:128], H.v[:, k, t0:t0 + n]) for k in range(NJ)],
                         [("w", slot)] + [H.c(k, s) for k in range(NJ)], [("ps", b)])
                dve("scalar_tensor_tensor", [("ps", b), ("xf", c, s)], [("xf", c, s)], out=xf[:, c, t0:t0 + n],
                    in0=bank(b, n), scalar=CR, in1=xf[:, c, t0:t0 + n], op0=ALU.mult, op1=ALU.add)
                prefetch()
                if s == 1 and c == 1:
                    ln_sub(l, kk, subs, 0, 31680, 43200, 54720)
        ln_sub(l, kk, subs, 1, 31680, 43200, 54720)

    def dense8(out_chunks, subs, consumer):
        slot, wv = next_block()
        for s, (t0, n) in enumerate(subs):
            for i, co in enumerate(out_chunks):
                b = alloc_bank()
                mm_group(bank(b, n), [(wv[:, k, co:co + 128], xb[:, k, t0:t0 + n]) for k in range(8)],
                         [("w", slot)] + xcells("xb", s), [("ps", b)])
                consumer(i, s, t0, n, b)
        prefetch()

    def dense8_pair(subs, consA, consB):
        slA, wvA = next_block()
        slB, wvB = next_block()
        for s, (sl, wv, cons) in ((0, (slA, wvA, consA)), (0, (slB, wvB, consB)), (1, (slA, wvA, consA)),
                                  (1, (slB, wvB, consB))):
            t0, n = subs[s]
            for i in range(4):
                b = alloc_bank()
                mm_group(bank(b, n), [(wv[:, k, i * 128:(i + 1) * 128], xb[:, k, t0:t0 + n]) for k in range(8)],
                         [("w", sl)] + xcells("xb", s), [("ps", b)])
                cons(i, s, t0, n, b)
            if s == 1:
                prefetch(1)

    def dense8_from(src, srccells, subs, consumer):
        slot, wv = next_block()
        for s, (t0, n) in enumerate(subs):
            for i in range(4):
                b = alloc_bank()
                mm_group(bank(b, n), [(wv[:, k, i * 128:(i + 1) * 128], src.v[:, k, t0:t0 + n]) for k in range(8)],
                         [("w", slot)] + [srccells(k, s) for k in range(8)], [("ps", b)])
                consumer(i, s, t0, n, b)
        prefetch()

    def mixer(g, l, subs):
        G = GROUPS[g][1]
        chunks = group_chunks(g)
        has_sample = (g == 2)
        nch = len(chunks)
        Fb = UB("F", 0, F32, [4, GT])
        Pb = UB("P", 11520, F32, [4, GT])
        QT = UB("QT", 23040, BF16, [8, GT])
        NK = UB("NK", 34560, BF16, [8, GT])
        VT = UB("VT", 46080, BF16, [8, GT])
        A_ = UB("A", 57600, F32, [2, GT])
        B_ = UB("B", 63360, F32, [2, GT])
        R_ = UB("R", 69120, F32, [2, GT])
        QS = UB("QS", 74880, F32, [2, 360])
        if g == 0:
            dve("memset", [], ["S"], ap=S[:], constant=0.0)
        else:
            sp_dma(S[:], hp[l].rearrange("h d v -> d h v"), [("hp", l)], ["S"], base="st", nsem=2)
        act(Sb[:], S[:], AF.Copy, ["S"], ["Sb"])
        qsi = [0]
        vtog = [0]
        for hh in (0, 1):
            def cons_f(i, s, t0, n, b):
                act(Fb.v[:, i, t0:t0 + n], bank(b, n), AF.Tanh, [("ps", b)], [Fb.c(i, s)], scale=0.5)

            def cons_v(i, s, t0, n, b, hh=hh):
                h = 4 * hh + i
                act(VT.v[:, h, t0:t0 + n], bank(b, n), AF.Copy, [("ps", b)], [VT.c(h, s)])
            if hh == 0:
                dense8_pair(subs, cons_f, cons_v)
            else:
                dense8([0, 128, 256, 384], subs, cons_f)
            for i0 in (0, 2):
                pair = (i0, i0 + 1)

                def fcs(i):
                    return [Fb.c(i, 0), Fb.c(i, 1)]
                for i in pair:
                    h = 4 * hh + i
                    dve("tensor_scalar", fcs(i) + ["c0t", "c1t"], fcs(i), out=Fb.v[:, i, 0:G], in0=Fb.v[:, i, 0:G],
                        scalar1=c1t[:, l, h:h + 1], scalar2=c0t[:, l, h:h + 1], op0=ALU.mult, op1=ALU.add)
                for i in pair:
                    dve("tensor_tensor", fcs(i) + ["nEt"], [A_.c(i % 2)], out=A_.v[:, i % 2, 0:G], in0=Fb.v[:, i, 0:G],
                        in1=nEt[:, 0:G], op=ALU.mult)
                    dve("tensor_tensor", fcs(i) + ["Et"], [B_.c(i % 2)], out=B_.v[:, i % 2, 0:G], in0=Fb.v[:, i, 0:G],
                        in1=Et[:, 0:G], op=ALU.mult)
                for i in pair:
                    dve("tensor_tensor_scan", [A_.c(i % 2), B_.c(i % 2)], [Pb.c(i)], out=Pb.v[:, i, 0:G],
                        data0=A_.v[:, i % 2, 0:G], data1=B_.v[:, i % 2, 0:G], initial=0.0, op0=ALU.mult, op1=ALU.add)
                for i in pair:
                    dve("reciprocal", [Pb.c(i)], [R_.c(i % 2)], out=R_.v[:, i % 2, 0:G], in_=Pb.v[:, i, 0:G])
                for i in pair:
                    h = 4 * hh + i
                    dve("scalar_tensor_tensor", fcs(i) + [R_.c(i % 2)], [NK.c(h)], out=NK.v[:, h, 0:G],
                        in0=Fb.v[:, i, 0:G], scalar=1.0, in1=R_.v[:, i % 2, 0:G], op0=ALU.subtract, op1=ALU.mult)
                for i in pair:
                    h = 4 * hh + i
                    e0 = chunks[0][1] - 1
                    act(DEC[:, h, 0:nch], Pb.v[:, i, e0:e0 + 64 * (nch - 1) + 1:64], AF.Copy, [Pb.c(i)], [("DEC", h)])
                    if has_sample:
                        act(DECS[:, h, :], Pb.v[:, i, 643:704:4], AF.Copy, [Pb.c(i)], [("DECS", h)])
            if hh == 1:
                dense8([0, 128, 256, 384], subs, cons_v)

            def cons_q(i, s, t0, n, b, hh=hh):
                h = 4 * hh + i
                qv = QS.v[:, qsi[0] % 2, 0:n]
                qc = QS.c(qsi[0] % 2)
                qsi[0] += 1
                act(qv, bank(b, n), AF.Silu, [("ps", b)], [qc])
                dve("tensor_tensor", [qc, Pb.c(i)], [QT.c(h, s)], out=QT.v[:, h, t0:t0 + n], in0=qv,
                    in1=Pb.v[:, i, t0:t0 + n], op=ALU.mult)
                if has_sample and s == 1:
                    dve("tensor_tensor", [qc, Pb.c(i)], [("QTF", h)], out=QTF[:, h, :], in0=qv[:, n - 64:n],
                        in1=Pb.v[:, i, 640:704], op=ALU.mult)
            dense8([0, 128, 256, 384], subs, cons_q)
        OT = UB("OT", 0, F32, [8, GT])
        SC = UB("SC", 57600, BF16, [2, 512])
        KT = UB("KT", 59648, BF16, [2, 1024])
        VTT = UB("VTT", 63744, BF16, [2, 1024])
        KH = UB("KH", 67840, BF16, [2, 512])
        n2 = subs[0][1]

        def scell(buf, t0, L):
            ss = set()
            for (s, (a, n)) in enumerate(subs):
                if t0 < a + n and a < t0 + L:
                    ss.add(s)
            return [buf.c(h, s) for h in range(8) for s in ss]

        def chunk_common(ci, t0, L, mask_idx, dec_ap, dec_cells, par):
            qc = scell(QT, t0, L)
            kc = [NK.c(h) for h in range(8)]
            vc = scell(VT, t0, L)
            bs = alloc_bank()
            def fn_sc(e):
                ins = None
                for h in range(8):
                    ins = e.matmul(bank(bs)[0:L, h * 64:h * 64 + L], lhsT=NK.v[:, h, t0:t0 + L],
                                   rhs=QT.v[:, h, t0:t0 + L], start=True, stop=True)
                return [ins]
            tr.op("pe", fn_sc, reads=qc + kc, writes=[("ps", bs)])
            scv = SC.v[0:L, par, :].rearrange("p (h t) -> p h t", h=8, t=64)[:, :, 0:L]
            dve("tensor_tensor", [("ps", bs), "maskc"], [SC.c(par)], out=scv,
                in0=bank(bs)[0:L, :].rearrange("p (h t) -> p h t", h=8, t=64)[:, :, 0:L],
                in1=maskc[0:L, mask_idx, :].rearrange("p (h t) -> p h t", h=8, t=64)[:, :, 0:L], op=ALU.mult)
            khv = KH.v[:, par, :].rearrange("p (h t) -> p h t", h=8, t=64)[:, :, 0:L]
            dve("tensor_tensor", kc + dec_cells, [KH.c(par)], out=khv, in0=NK.v[:, :, t0:t0 + L], in1=dec_ap,
                op=ALU.mult)
            bk = alloc_bank()
            bv = alloc_bank()

            def fn_tk(e):
                ins = None
                for h in range(8):
                    ins = e.transpose(out=bank_bf(bk)[0:L, h * 128:(h + 1) * 128],
                                      in_=KH.v[:, par, h * 64:h * 64 + L], identity=identb[:])
                return [ins]
            tr.op("pe", fn_tk, reads=[KH.c(par), "identb"], writes=[("ps", bk)])

            def fn_tv(e):
                ins = None
                for h in range(8):
                    ins = e.transpose(out=bank_bf(bv)[0:L, h * 128:(h + 1) * 128],
                                      in_=VT.v[:, h, t0:t0 + L], identity=identb[:])
                return [ins]
            tr.op("pe", fn_tv, reads=vc + ["identb"], writes=[("ps", bv)])
            act(KT.v[0:L, par, :], bank_bf(bk)[0:L, :], AF.Copy, [("ps", bk)], [KT.c(par)])
            act(VTT.v[0:L, par, :], bank_bf(bv)[0:L, :], AF.Copy, [("ps", bv)], [VTT.c(par)])
            return qc

        if has_sample:
            S0b = UB("S0b", 69888, F32, [2, 1024])
            SOb = UB("SOb", 78080, F32, [2, 1024])
            KTM = UB("KTM", 86272, BF16, [1, 1024])

            def s0_load(q):
                sp_dma(S0b.v[:, q % 2, :].rearrange("p (h v) -> p h v", h=8, v=128),
                       s0[l, q].rearrange("h d v -> d h v"), [], [S0b.c(q % 2)], sem="si%d" % (q % 2))
            s0_load(0)
            s0_load(1)

        def part_a(ci):
            t0, L = chunks[ci]
            dec_ap = DEC[:, :, ci:ci + 1].broadcast_to([128, 8, L])
            return chunk_common(ci, t0, L, 0, dec_ap, [("DEC", h) for h in range(8)], ci % 2)

        def part_b(ci, qc):
            t0, L = chunks[ci]
            par = ci % 2
            bsu = alloc_bank2()

            def fn_su(e, L=L, par=par, bsu=bsu):
                ins = None
                for h in range(8):
                    ins = e.matmul(pst[bsu // 2][:, h * 128:(h + 1) * 128], lhsT=KT.v[0:L, par, h * 128:(h + 1) * 128],
                                   rhs=VTT.v[0:L, par, h * 128:(h + 1) * 128], start=True, stop=True)
                return [ins]
            tr.op("pe", fn_su, reads=[KT.c(par), VTT.c(par)], writes=[("ps", bsu), ("ps", bsu + 1)])
            bo = alloc_bank()

            def fn_o(e, t0=t0, L=L, par=par, bo=bo):
                ins = None
                for h in range(8):
                    e.matmul(bank(bo)[:, h * 64:h * 64 + L], lhsT=Sb[:, h, :], rhs=QT.v[:, h, t0:t0 + L],
                             start=True, stop=False)
                    ins = e.matmul(bank(bo)[:, h * 64:h * 64 + L], lhsT=VTT.v[0:L, par, h * 128:(h + 1) * 128],
                                   rhs=SC.v[0:L, par, h * 64:h * 64 + L], start=False, stop=True)
                return [ins]
            tr.op("pe", fn_o, reads=["Sb", VTT.c(par), SC.c(par)] + qc, writes=[("ps", bo)])
            dve("tensor_tensor", ["S"] + [("DEC", h) for h in range(8)], ["S"], out=S[:], in0=S[:],
                in1=DEC[:, :, ci:ci + 1].broadcast_to([128, 8, 128]), op=ALU.mult)
            dve("tensor_tensor", ["S", ("ps", bsu), ("ps", bsu + 1)], ["S"], out=S[:], in0=S[:],
                in1=pst[bsu // 2][:, :].rearrange("p (h v) -> p h v", h=8, v=128), op=ALU.subtract)
            act(OT.v[:, :, t0:t0 + L], bank(bo).rearrange("p (h t) -> p h t", h=8, t=64)[:, :, 0:L], AF.Copy,
                [("ps", bo)], scell(OT, t0, L))
            if ci < nch - 1:
                act(Sb[:], S[:], AF.Copy, ["S"], ["Sb"])

        qcs = {0: part_a(0)}
        for ci in range(nch):
            if ci + 1 < nch:
                qcs[ci + 1] = part_a(ci + 1)
            part_b(ci, qcs[ci])
        sp_dma(hp[l].rearrange("h d v -> d h v"), S[:], ["S"], [("hp", l)], base="st", nsem=2)

        if has_sample:
            tS = 640
            parS = nch % 2
            dec_ap = DECS[:, :, :].unsqueeze(3).broadcast_to([128, 8, 16, 4])
            qcells = scell(QT, tS, 64)
            kc = [NK.c(h) for h in range(8)]
            vc = scell(VT, tS, 64)
            bs = alloc_bank()

            def fn_sc(e):
                ins = None
                for h in range(8):
                    ins = e.matmul(bank(bs)[0:64, h * 64:(h + 1) * 64], lhsT=NK.v[:, h, tS:tS + 64],
                                   rhs=QT.v[:, h, tS:tS + 64], start=True, stop=True)
                return [ins]
            tr.op("pe", fn_sc, reads=qcells + kc, writes=[("ps", bs)])
            dve("tensor_tensor", [("ps", bs), "maskc"], [SC.c(parS)], out=SC.v[0:64, parS, :], in0=bank(bs)[0:64, :],
                in1=maskc[:, 1, :], op=ALU.mult)
            dve("tensor_tensor", kc + [("DECS", h) for h in range(8)], [KH.c(parS)],
                out=KH.v[:, parS, :].rearrange("p (h q t) -> p h q t", h=8, q=16, t=4),
                in0=NK.v[:, :, tS:tS + 64].rearrange("p h (q t) -> p h q t", q=16, t=4), in1=dec_ap, op=ALU.mult)
            bk = alloc_bank()
            bv = alloc_bank()

            def fn_tk(e):
                ins = None
                for h in range(8):
                    ins = e.transpose(out=bank_bf(bk)[0:64, h * 128:(h + 1) * 128],
                                      in_=KH.v[:, parS, h * 64:(h + 1) * 64], identity=identb[:])
                return [ins]
            tr.op("pe", fn_tk, reads=[KH.c(parS), "identb"], writes=[("ps", bk)])

            def fn_tv(e):
                ins = None
                for h in range(8):
                    ins = e.transpose(out=bank_bf(bv)[0:64, h * 128:(h + 1) * 128],
                                      in_=VT.v[:, h, tS:tS + 64], identity=identb[:])
                return [ins]
            tr.op("pe", fn_tv, reads=vc + ["identb"], writes=[("ps", bv)])
            act(KT.v[0:64, parS, :], bank_bf(bk)[0:64, :], AF.Copy, [("ps", bk)], [KT.c(parS)])
            act(VTT.v[0:64, parS, :], bank_bf(bv)[0:64, :], AF.Copy, [("ps", bv)], [VTT.c(parS)])
            bo = alloc_bank()
            held.add(bo)

            def fn_o0(e):
                e.matmul(bank(bo), lhsT=zerob[:], rhs=xb[:, 0, 0:512], start=True, stop=False)
                ins = None
                for h in range(8):
                    ins = e.matmul(bank(bo)[:, h * 64:(h + 1) * 64], lhsT=VTT.v[0:64, parS, h * 128:(h + 1) * 128],
                                   rhs=SC.v[0:64, parS, h * 64:(h + 1) * 64], start=False, stop=False)
                return [ins]
            tr.op("pe", fn_o0, reads=["zerob", VTT.c(parS), SC.c(parS)] + xcells("xb", 0) + xcells("xb", 1),
                  writes=[("ps", bo)])
            for q in range(16):

                def fn_oq(e, q=q):
                    ins = None
                    for h in range(8):
                        ins = e.matmul(bank(bo)[:, h * 64 + q * 4:h * 64 + q * 4 + 4],
                                       lhsT=S0b.v[:, q % 2, h * 128:(h + 1) * 128], rhs=QTF[:, h, q * 4:q * 4 + 4],
                                       start=False, stop=(q == 15 and h == 7))
                    return [ins]
                tr.op("pe", fn_oq, reads=[S0b.c(q % 2)] + [("QTF", h) for h in range(8)], writes=[("ps", bo)])
                dve("tensor_scalar", [KT.c(parS), "rowm"], [KTM.c(0)], out=KTM.v[0:64, 0, :], in0=KT.v[0:64, parS, :],
                    scalar1=rowm[:, q:q + 1], scalar2=None, op0=ALU.mult)
                bsu = alloc_bank2()

                def fn_su(e, bsu=bsu):
                    ins = None
                    for h in range(8):
                        ins = e.matmul(pst[bsu // 2][:, h * 128:(h + 1) * 128],
                                       lhsT=KTM.v[0:64, 0, h * 128:(h + 1) * 128],
                                       rhs=VTT.v[0:64, parS, h * 128:(h + 1) * 128], start=True, stop=True)
                    return [ins]
                tr.op("pe", fn_su, reads=[KTM.c(0), VTT.c(parS)], writes=[("ps", bsu), ("ps", bsu + 1)])
                sov = SOb.v[:, q % 2, :].rearrange("p (h v) -> p h v", h=8, v=128)
                dve("tensor_tensor", [S0b.c(q % 2)] + [("DECS", h) for h in range(8)], [SOb.c(q % 2)], out=sov,
                    in0=S0b.v[:, q % 2, :].rearrange("p (h v) -> p h v", h=8, v=128),
                    in1=DECS[:, :, q:q + 1].broadcast_to([128, 8, 128]), op=ALU.mult)
                dve("tensor_tensor", [SOb.c(q % 2), ("ps", bsu), ("ps", bsu + 1)], [SOb.c(q % 2)], out=sov, in0=sov,
                    in1=pst[bsu // 2][:, :].rearrange("p (h v) -> p h v", h=8, v=128), op=ALU.subtract)
                if q + 2 < 16:
                    s0_load(q + 2)
                sp_dma(hs[l, q].rearrange("h d v -> d h v"), sov, [SOb.c(q % 2)], [], sem="so%d" % (q % 2))
            act(OT.v[:, :, tS:tS + 64], bank(bo).rearrange("p (h t) -> p h t", h=8, t=64), AF.Copy,
                [("ps", bo)], scell(OT, tS, 64))
            held.discard(bo)

        SQ2 = UB("SQ2", 23040, BF16, [4, 360])
        RS = UB("RS", 25920, F32, [4, 360])
        NRB = 4
        O2 = UB("O2", 34560, BF16, [8, GT])
        SGO = UB("SGO", 31680, F32, [2, 360])
        items = [(h, s) for h in range(8) for s in range(2)]

        def rms_sq(ii):
            h, s = items[ii]
            t0, n = subs[s]
            act(SQ2.v[:, ii % 4, 0:n], OT.v[:, h, t0:t0 + n], AF.Square, [OT.c(h, s)], [SQ2.c(ii % 4)])
        rms_sq(0)
        rms_sq(1)
        for ii, (h, s) in enumerate(items):
            t0, n = subs[s]
            p2 = ii % 4
            if ii + 2 < len(items):
                rms_sq(ii + 2)
            b = alloc_bank()
            mm_group(bank(b, n), [(ones128[:], SQ2.v[:, p2, 0:n])], ["ones128", SQ2.c(p2)], [("ps", b)])
            act(RS.v[:, p2, 0:n], bank(b, n), AF.Sqrt, [("ps", b), "epsr"], [RS.c(p2)], bias=epsr[:], scale=1.0)
            dve("reciprocal", [RS.c(p2)], [RS.c(p2)], out=RS.v[:, p2, 0:n], in_=RS.v[:, p2, 0:n])
            dve("tensor_tensor", [RS.c(p2), OT.c(h, s)], [OT.c(h, s)], out=OT.v[:, h, t0:t0 + n],
                in0=OT.v[:, h, t0:t0 + n], in1=RS.v[:, p2, 0:n], op=ALU.mult)
        sgi = [0]
        for hh in (0, 1):
            def cons_og(i, s, t0, n, b, hh=hh):
                h = 4 * hh + i
                p2 = sgi[0] % 2
                sgi[0] += 1
                act(SGO.v[:, p2, 0:n], bank(b, n), AF.Silu, [("ps", b)], [SGO.c(p2)])
                dve("scalar_tensor_tensor", [SGO.c(p2), OT.c(h, s), "normw"], [O2.c(h, s)],
                    out=O2.v[:, h, t0:t0 + n], in0=OT.v[:, h, t0:t0 + n], scalar=normw[:, l:l + 1],
                    in1=SGO.v[:, p2, 0:n], op0=ALU.mult, op1=ALU.mult)
            dense8([0, 128, 256, 384], subs, cons_og)

        TGA = UB("TGA", 46080, BF16, [8, GT])
        for hh in (0, 1):
            def cons_ga(i, s, t0, n, b, hh=hh):
                c = 4 * hh + i
                act(TGA.v[:, c, t0:t0 + n], bank(b, n), AF.Tanh, [("ps", b)], [TGA.c(c, s)], scale=0.5)
            dense8([0, 128, 256, 384], subs, cons_ga)
        MA = UB("MA", 0, F32, [8, GT])
        for hh in (0, 1):
            def cons_wa(i, s, t0, n, b, hh=hh):
                c = 4 * hh + i
                dve("scalar_tensor_tensor", [("ps", b), TGA.c(c, s)], [MA.c(c, s)], out=MA.v[:, c, t0:t0 + n],
                    in0=TGA.v[:, c, t0:t0 + n], scalar=1.0, in1=bank(b, n), op0=ALU.add, op1=ALU.mult)
            dense8_from(O2, lambda k, s: O2.c(k, s), subs, cons_wa)

        U2 = UB("U2", 23040, F32, [8, GT + 2])
        CV = UB("CV", 57664, F32, [8, GT])
        CGT = UB("CGT", 80704, F32, [2, 360])
        Gp = 640 if has_sample else G
        for c in range(8):
            pass
        dve("tensor_copy", ["ubuf"], [U2.c("halo")], out=U2.v[:, :, 0:2], in_=ubuf[:, l, :, :])
        cgi = [0]
        for j in range(4):
            def cons_ch(i, s, t0, n, b, j=j):
                c = 2 * j + (i % 2)
                if i < 2:
                    act(CGT.v[:, i, 0:n], bank(b, n), AF.Copy, [("ps", b)], [CGT.c(i)])
                else:
                    dve("tensor_tensor", [("ps", b), CGT.c(i - 2)], [U2.c(c, s)], out=U2.v[:, c, 2 + t0:2 + t0 + n],
                        in0=bank(b, n), in1=CGT.v[:, i - 2, 0:n], op=ALU.mult)
            dense8([0, 128, 256, 384], subs, cons_ch)
        for c in range(8):
            uc = [U2.c(c, 0), U2.c(c, 1), U2.c("halo")]
            dve("tensor_scalar", uc + ["convw"], [CV.c(c)], out=CV.v[:, c, 0:Gp], in0=U2.v[:, c, 0:Gp],
                scalar1=convw[:, l, 0, c:c + 1], scalar2=None, op0=ALU.mult)
            dve("scalar_tensor_tensor", uc + ["convw", CV.c(c)], [CV.c(c)], out=CV.v[:, c, 0:Gp],
                in0=U2.v[:, c, 1:Gp + 1], scalar=convw[:, l, 1, c:c + 1], in1=CV.v[:, c, 0:Gp],
                op0=ALU.mult, op1=ALU.add)
            dve("scalar_tensor_tensor", uc + ["convw", CV.c(c)], [CV.c(c)], out=CV.v[:, c, 0:Gp],
                in0=U2.v[:, c, 2:Gp + 2], scalar=convw[:, l, 2, c:c + 1], in1=CV.v[:, c, 0:Gp],
                op0=ALU.mult, op1=ALU.add)
        allu = [U2.c(c, s) for c in range(8) for s in range(2)] + [U2.c("halo")]
        allcv = [CV.c(c) for c in range(8)]
        dve("tensor_copy", allu, ["ubuf"], out=ubuf[:, l, :, :], in_=U2.v[:, :, Gp:Gp + 2])
        if has_sample:
            act(CPO[:, l, :, :], U2.v[:, :, Gp:Gp + 2], AF.Copy, allu, [("CPO", l)])
            dve("tensor_copy", ["CB"], ["US"], out=US[:, :, :, 0:2], in_=CB[:, l, :, :, :])
            dve("tensor_copy", allu + ["US"], ["US"], out=US[:, :, :, 2:6],
                in_=U2.v[:, :, 2 + 640:2 + 704].rearrange("p c (q t) -> p c q t", q=16, t=4))
            act(CSO[:, l, :, :, :], US[:, :, :, 4:6], AF.Copy, ["US"], [("CSO", l)])
            for jx in range(3):
                wbc = convw[:, l, jx, :].unsqueeze(2).unsqueeze(3).broadcast_to([128, 8, 16, 4])
                if jx == 0:
                    dve("tensor_tensor", ["US", "convw"], ["CVS"], out=CVS[:], in0=US[:, :, :, 0:4], in1=wbc,
                        op=ALU.mult)
                else:
                    dve("tensor_tensor", ["US", "convw"], ["CVS2"], out=CVS2[:], in0=US[:, :, :, jx:jx + 4], in1=wbc,
                        op=ALU.mult)
                    dve("tensor_tensor", ["CVS", "CVS2"], ["CVS"], out=CVS[:], in0=CVS[:], in1=CVS2[:], op=ALU.add)
            dve("tensor_copy", ["CVS"] + allcv, allcv, out=CV.v[:, :, 640:704],
                in_=CVS[:].rearrange("p c q t -> p c (q t)"))
        TGB = UB("TGB", 46144, BF16, [8, GT])
        for hh in (0, 1):
            def cons_gb(i, s, t0, n, b, hh=hh):
                c = 4 * hh + i
                act(TGB.v[:, c, t0:t0 + n], bank(b, n), AF.Tanh, [("ps", b)], [TGB.c(c, s)], scale=0.5)
            dense8([0, 128, 256, 384], subs, cons_gb)
        BC = UB("BC", 23040, BF16, [8, GT])
        for hh in (0, 1):
            def cons_bg(i, s, t0, n, b, hh=hh):
                c = 4 * hh + i
                dve("tensor_tensor", [("ps", b)] + allcv, [BC.c(c, s)], out=BC.v[:, c, t0:t0 + n], in0=bank(b, n),
                    in1=CV.v[:, c, t0:t0 + n], op=ALU.mult)
            dense8([0, 128, 256, 384], subs, cons_bg)
        M = UB("M", 34560, BF16, [8, GT])
        YTMP = UB("YTMP", 57664, F32, [2, 360])
        yi = [0]
        for hh in (0, 1):
            def cons_wb(i, s, t0, n, b, hh=hh):
                c = 4 * hh + i
                p2 = yi[0] % 2
                yi[0] += 1
                dve("scalar_tensor_tensor", [("ps", b), TGB.c(c, s)], [YTMP.c(p2)], out=YTMP.v[:, p2, 0:n],
                    in0=TGB.v[:, c, t0:t0 + n], scalar=1.0, in1=bank(b, n), op0=ALU.add, op1=ALU.mult)
                dve("tensor_tensor", [YTMP.c(p2), MA.c(c, s)], [M.c(c, s)], out=M.v[:, c, t0:t0 + n],
                    in0=YTMP.v[:, p2, 0:n], in1=MA.v[:, c, t0:t0 + n], op=ALU.add)
            dense8_from(BC, lambda k, s: BC.c(k, s), subs, cons_wb)
        def cons_wo(c, s, t0, n, b):
            dve("scalar_tensor_tensor", [("ps", b), ("xf", c, s)], [("xf", c, s)], out=xf[:, c, t0:t0 + n],
                in0=bank(b, n), scalar=CR, in1=xf[:, c, t0:t0 + n], op0=ALU.mult, op1=ALU.add)
        slA, wvA = next_block()
        slB, wvB = next_block()
        for s, (t0, n) in enumerate(subs):
            for hh, (sl, wv) in enumerate(((slA, wvA), (slB, wvB))):
                for i in range(4):
                    b = alloc_bank()
                    mm_group(bank(b, n), [(wv[:, k, i * 128:(i + 1) * 128], M.v[:, k, t0:t0 + n]) for k in range(8)],
                             [("w", sl)] + [M.c(k, s) for k in range(8)], [("ps", b)])
                    cons_wo(4 * hh + i, s, t0, n, b)
                if s == 1 and hh == 0:
                    ln_sub(l, 1, subs, 0, 46080, 57600, 69120)
        prefetch()
        ln_sub(l, 1, subs, 1, 46080, 57600, 69120)

    for g in range(NG):
        g0, G = GROUPS[g]
        n = G // 2
        subs = [(0, n), (n, G - n)]
        sp_dma(xf[:, :, 0:G], xT[:, :, g0:g0 + G], [], [("xf", c, s) for c in range(8) for s in range(2)])
        sp_dma(Et[:], cE[:, g, :], [], ["Et"])
        dve("tensor_scalar", ["Et"], ["nEt"], out=nEt[:], in0=Et[:], scalar1=-1.0, scalar2=1.0, op0=ALU.mult,
            op1=ALU.add)
        for s, (t0, nn) in enumerate(subs):
            act(xb[:, :, t0:t0 + nn], xf[:, :, t0:t0 + nn], AF.Copy, xcells("xf", s), xcells("xb", s))
        for l in range(NL):
            ffn(l, 0, subs)
            if DEBUG["stop"] == "ffn1":
                break
            mixer(g, l, subs)
            if DEBUG["stop"] == "mixer":
                break
            ffn(l, 1, subs)
        allxf = [("xf", c, s) for c in range(8) for s in range(2)]
        if g == 0:
            sp_dma(yT[:, :, 0:G - 16], xf[:, :, 16:G], allxf, [], base="yo", nsem=1)
        else:
            sp_dma(yT[:, :, g0 - 16:g0 - 16 + G], xf[:, :, 0:G], allxf, [], base="yo", nsem=1)
    if DEBUG["stop"] is None:
      sp_dma(cso[:, 0:NL], CSO[:, 0:NL], [("CSO", l) for l in range(NL)], [], base="yo", nsem=1)
      sp_dma(cpo[:, 0:NL], CPO[:, 0:NL], [("CPO", l) for l in range(NL)], [], base="yo", nsem=1)

    cum = {}
    for en in Tr.ENGS:
        c = 0
        arr = []
        for o in tr.ops[en]:
            if o["signal"]:
                c += 1
            arr.append(c)
        cum[en] = arr
    dsems = sorted(tr.dval.keys())
    semctx = {}
    for en in Tr.ENGS:
        cm = nc.semaphore("e_" + en)
        semctx[("e", en)] = cm.__enter__()
        ctxs.append(cm)
    for ds in dsems:
        cm = nc.semaphore("d_" + ds)
        semctx[("d", ds)] = cm.__enter__()
        ctxs.append(cm)

    def replay(en, e):
        my = semctx[("e", en)]
        for o in tr.ops[en]:
            for (k, v) in o["waits"]:
                if k[0] == "e":
                    e.wait_ge(semctx[k], cum[k[1]][v])
                else:
                    e.wait_ge(semctx[k], v)
            inss = o["fn"](e)
            if o["dma"] is not None:
                for ins in inss:
                    ins.then_inc(semctx[("d", o["dma"])], 16)
            elif o["signal"]:
                inss[-1].then_inc(my, 1)
        if en == "sp":
            for ds in dsems:
                e.wait_ge(semctx[("d", ds)], tr.dval[ds])

    with nc.Block() as block:
        @block.tensor
        def _(e):
            replay("pe", e)

        @block.scalar
        def _(e):
            replay("act", e)

        @block.vector
        def _(e):
            replay("dve", e)

        @block.gpsimd
        def _(e):
            replay("pool", e)

        @block.sync
        def _(e):
            replay("sp", e)
    for cm in reversed(ctxs):
        cm.__exit__(None, None, None)
    stats = {en: len(tr.ops[en]) for en in Tr.ENGS}
    return nc, stats


def host_consts():
    ident = np.eye(128, dtype=np.float32)
    s_ = np.arange(64)[:, None]
    t_ = np.arange(64)[None, :]
    mc = np.where(s_ <= t_, -1.0, 0.0).astype(np.float32)
    ms = np.where((s_ <= t_) & (s_ // 4 == t_ // 4), -1.0, 0.0).astype(np.float32)
    cmask = np.zeros((64, 2, 512), np.float32)
    cmask[:, 0, :] = np.tile(mc, (1, 8))
    cmask[:, 1, :] = np.tile(ms, (1, 8))
    crow = (np.arange(64)[:, None] // 4 == np.arange(16)[None, :]).astype(np.float32)
    E = np.zeros((3, GT), np.float32)
    E[0, 0] = 1.0
    E[0, 16::64] = 1.0
    E[1, 0:704:64] = 1.0
    E[2, 0:640:64] = 1.0
    E[2, 640:704:4] = 1.0
    cE = np.ascontiguousarray(np.broadcast_to(E[None], (128, 3, GT)))
    return ident, cmask, crow, cE


_CACHE = {}


def kernel(x_prompt, x_sample, state_hgrn, state_conv, meta_tokens, w_in, hgrn_lb_logits,
           hgrn_norm_w, conv_w, w_branch_a, w_branch_b, w_out, ffn1_w_gu, ffn1_w_down,
           ffn2_w_gu, ffn2_w_down, ln_g, ln_b):
    f32 = np.float32
    x_prompt = np.asarray(x_prompt, f32)
    x_sample = np.asarray(x_sample, f32)
    state_hgrn = np.asarray(state_hgrn, f32)
    state_conv = np.asarray(state_conv, f32)
    meta = np.asarray(meta_tokens, f32)
    nc, stats = build_program()
    ident, cmask, crow, cE = host_consts()

    def fm(a):
        a = np.asarray(a, f32)
        sh = a.shape[:-1]
        return np.ascontiguousarray(np.moveaxis(a.reshape(sh + (8, 128)), -1, 0))

    def dn_layout(w):
        w = np.asarray(w, f32).reshape(4, NJ, 128, 8, 128)
        return np.ascontiguousarray(w.transpose(0, 3, 2, 1, 4))

    shared = {
        "w_in": np.ascontiguousarray(np.asarray(w_in, f32)),
        "w_a": np.ascontiguousarray(np.asarray(w_branch_a, f32)),
        "w_b": np.ascontiguousarray(np.asarray(w_branch_b, f32)),
        "w_o": np.ascontiguousarray(np.asarray(w_out, f32)),
        "f1gu": np.ascontiguousarray(np.asarray(ffn1_w_gu, f32)),
        "f2gu": np.ascontiguousarray(np.asarray(ffn2_w_gu, f32)),
        "f1d": dn_layout(ffn1_w_down),
        "f2d": dn_layout(ffn2_w_down),
        "plng": fm(ln_g), "plnb": fm(ln_b), "plog": fm(hgrn_lb_logits), "pconv": fm(conv_w),
        "pnorm": np.ascontiguousarray(np.asarray(hgrn_norm_w, f32).T),
        "cident": ident, "cmask": cmask, "crow": crow, "cE": cE,
    }
    in_maps = []
    for c in range(8):
        toks = np.concatenate([meta, x_prompt[c], x_sample[16 * c:16 * c + 16].reshape(64, D)], axis=0)
        xT = np.ascontiguousarray(toks.reshape(NTOK, 8, 128).transpose(2, 1, 0))
        m = dict(shared)
        m["xT"] = xT
        m["s0"] = np.ascontiguousarray(state_hgrn[:, 16 * c:16 * c + 16])
        sc = state_conv[:, 16 * c:16 * c + 16]
        m["cb"] = np.ascontiguousarray(sc.reshape(4, 16, 2, 8, 128).transpose(4, 0, 3, 1, 2))
        in_maps.append(m)
    res = run_bass_kernel_spmd(nc, in_maps, core_ids=list(range(8)))
    y_prompt = np.zeros((8, 2048, D), f32)
    y_sample = np.zeros((128, 4, D), f32)
    nhp = np.zeros((4, 8, 8, 128, 128), f32)
    ncp = np.zeros((4, 8, 2, 1024), f32)
    nhs = np.zeros((4, 128, 8, 128, 128), f32)
    ncs = np.zeros((4, 128, 2, 1024), f32)
    for c in range(8):
        r = res.results[c]
        y = np.asarray(r["yT"]).transpose(2, 1, 0).reshape(NTOK - 16, D)
        y_prompt[c] = y[0:2048]
        y_sample[16 * c:16 * c + 16] = y[2048:].reshape(16, 4, D)
        nhp[:, c] = np.asarray(r["hp"])
        nhs[:, 16 * c:16 * c + 16] = np.asarray(r["hs"])
        cso = np.asarray(r["cso"])
        ncs[:, 16 * c:16 * c + 16] = cso.transpose(1, 3, 4, 2, 0).reshape(4, 16, 2, 1024)
        cpo = np.asarray(r["cpo"])
        ncp[:, c] = cpo.transpose(1, 3, 2, 0).reshape(4, 2, 1024)
    return (y_prompt, y_sample, nhp, ncp, nhs, ncs)
```

```python
import numpy as np
import concourse.bass as bass
import concourse.mybir as mybir
from concourse.bass_utils import run_bass_kernel_spmd

F32 = mybir.dt.float32
BF16 = mybir.dt.bfloat16
F32R = mybir.dt.float32r
AF = mybir.ActivationFunctionType
ALU = mybir.AluOpType

D = 1024
DFF = 2816
DEPTH = 4
NCK = 8
NJ = 22
NTOK = 2128
GROUPS = [(0, 720), (720, 704), (1424, 704)]
GT = 720
ALPHA = (2 * DEPTH) ** 0.25
CR = 0.5 / ALPHA
LN_EPS_P = 1e-5 / (ALPHA * ALPHA)
RMS_EPS = 1e-6
SLOT = 4096
NSLOT = 3
UBYTES = 90112

DEBUG = {"layers": DEPTH, "groups": 3, "stop": None}


class Tr:
    ENGS = ("pe", "act", "dve", "pool", "sp")

    def __init__(self):
        self.ops = {n: [] for n in self.ENGS}
        self.waited = {n: {} for n in self.ENGS}
        self.cells = {}
        self.dval = {}
        self.live = []
        self.ninst = 0

    def cell(self, key):
        c = self.cells.get(key)
        if c is None:
            pend = {}
            if isinstance(key, tuple) and isinstance(key[0], UInst):
                assert key[0].alive, ("access to retired buffer", key[0].name)
                pend = dict(key[0].pending)
            c = [{}, pend]
            self.cells[key] = c
        return c

    def op(self, eng, fn, reads=(), writes=(), dma_sem=None, ndma=1):
        deps = {}

        def merge(d):
            for k, v in d.items():
                if deps.get(k, -1) < v:
                    deps[k] = v
        for c in reads:
            merge(self.cell(c)[0])
        for c in writes:
            cc = self.cell(c)
            merge(cc[0])
            merge(cc[1])
        if dma_sem is not None:
            prev = self.dval.get(dma_sem, 0)
            if prev > 0:
                deps[("d", dma_sem)] = max(deps.get(("d", dma_sem), -1), prev)
        waits = []
        wd = self.waited[eng]
        for k, v in deps.items():
            if k == ("e", "pe") and eng == "pe":
                continue
            if wd.get(k, -1) >= v:
                continue
            wd[k] = v
            waits.append((k, v))
            if k[0] == "e":
                self.ops[k[1]][v]["signal"] = True
        idx = len(self.ops[eng])
        if dma_sem is not None:
            self.dval[dma_sem] = self.dval.get(dma_sem, 0) + 16 * ndma
            ev = (("d", dma_sem), self.dval[dma_sem])
        else:
            ev = (("e", eng), idx)
        self.ops[eng].append({"fn": fn, "waits": waits, "signal": False, "dma": dma_sem})
        for c in reads:
            r = self.cell(c)[1]
            if r.get(ev[0], -1) < ev[1]:
                r[ev[0]] = ev[1]
        for c in writes:
            self.cells[c] = [{ev[0]: ev[1]}, {}]
        return ev


class UInst:
    def __init__(self, tr, name, off, nbytes):
        self.name, self.off, self.nbytes = name, off, nbytes
        self.alive = True
        self.pending = {}
        keep = []
        for o in tr.live:
            if o.off < off + nbytes and off < o.off + o.nbytes:
                if o.alive:
                    o.alive = False
                    for k, c in list(tr.cells.items()):
                        if isinstance(k, tuple) and k[0] is o:
                            for d in (c[0], c[1]):
                                for kk, vv in d.items():
                                    if o.pending.get(kk, -1) < vv:
                                        o.pending[kk] = vv
                            del tr.cells[k]
                for kk, vv in o.pending.items():
                    if self.pending.get(kk, -1) < vv:
                        self.pending[kk] = vv
                if not (off <= o.off and o.off + o.nbytes <= off + nbytes):
                    keep.append(o)
            else:
                keep.append(o)
        keep.append(self)
        tr.live = keep


def group_chunks(g):
    if g == 0:
        return [(0, 16)] + [(16 + 64 * i, 64) for i in range(11)]
    if g == 1:
        return [(64 * i, 64) for i in range(11)]
    return [(64 * i, 64) for i in range(10)]


def build_program():
    nc = bass.Bass("TRN2", target_bir_lowering=False)
    NL = DEBUG["layers"]
    NG = DEBUG["groups"]

    def din(name, shape):
        return nc.dram_tensor(name, shape, F32, kind="ExternalInput").ap()

    def dout(name, shape):
        return nc.dram_tensor(name, shape, F32, kind="ExternalOutput").ap()

    xT = din("xT", [128, 8, NTOK])
    s0 = din("s0", [4, 16, 8, 128, 128])
    cb = din("cb", [128, 4, 8, 16, 2])
    w_in = din("w_in", [4, D, 9 * D])
    w_a = din("w_a", [4, D, D])
    w_b = din("w_b", [4, D, D])
    w_o = din("w_o", [4, D, D])
    fgu = [din("f1gu", [4, D, 2 * DFF]), din("f2gu", [4, D, 2 * DFF])]
    fdn = [din("f1d", [4, 8, 128, NJ, 128]), din("f2d", [4, 8, 128, NJ, 128])]
    plng = din("plng", [128, 4, 3, 8])
    plnb = din("plnb", [128, 4, 3, 8])
    plog = din("plog", [128, 4, 8])
    pconv = din("pconv", [128, 4, 3, 8])
    pnorm = din("pnorm", [128, 4])
    cident = din("cident", [128, 128])
    cmask = din("cmask", [64, 2, 512])
    crow = din("crow", [64, 16])
    cE = din("cE", [128, 3, GT])

    yT = dout("yT", [128, 8, NTOK - 16])
    hs = dout("hs", [4, 16, 8, 128, 128])
    hp = dout("hp", [4, 8, 128, 128])
    cso = dout("cso", [128, 4, 8, 16, 2])
    cpo = dout("cpo", [128, 4, 8, 2])

    tr = Tr()
    sb = {}
    ctxs = []

    def salloc(name, shape, dt):
        cm = nc.sbuf_tensor(name, shape, dt)
        t = cm.__enter__()
        ctxs.append(cm)
        sb[name] = t
        return t

    xf = salloc("xf", [128, 8, GT], F32)
    xb = salloc("xb", [128, 8, GT], BF16)
    S = salloc("S", [128, 8, 128], F32)
    Sb = salloc("Sb", [128, 8, 128], BF16)
    W = salloc("W", [128, NSLOT, SLOT], BF16)
    U = salloc("U", [128, UBYTES // 4], F32)
    identf = salloc("identf", [128, 128], F32)
    identb = salloc("identb", [128, 128], BF16)
    ones1k = salloc("ones1k", [128, 128], BF16)
    ones128 = salloc("ones128", [128, 128], BF16)
    zerob = salloc("zerob", [128, 128], BF16)
    lng = salloc("lng", [128, 4, 3, 8], F32)
    lnb = salloc("lnb", [128, 4, 3, 8], F32)
    lgt = salloc("lgt", [128, 4, 8], F32)
    lbt = salloc("lbt", [128, 4, 8], F32)
    c0t = salloc("c0t", [128, 4, 8], F32)
    c1t = salloc("c1t", [128, 4, 8], F32)
    sm8 = salloc("sm8", [128, 3, 8], F32)
    convw = salloc("convw", [128, 4, 3, 8], F32)
    normw = salloc("normw", [128, 4], F32)
    maskc = salloc("maskc", [64, 2, 512], F32)
    rowm = salloc("rowm", [64, 16], F32)
    Et = salloc("Et", [128, GT], F32)
    nEt = salloc("nEt", [128, GT], F32)
    ubuf = salloc("ubuf", [128, 4, 8, 2], F32)
    DEC = salloc("DEC", [128, 8, 12], F32)
    DECS = salloc("DECS", [128, 8, 16], F32)
    CB = salloc("CB", [128, 4, 8, 16, 2], F32)
    US = salloc("US", [128, 8, 16, 6], F32)
    CVS = salloc("CVS", [128, 8, 16, 4], F32)
    CVS2 = salloc("CVS2", [128, 8, 16, 4], F32)
    QTF = salloc("QTF", [128, 8, 64], F32)
    epsl = salloc("epsl", [128, 1], F32)
    epsr = salloc("epsr", [128, 1], F32)
    CSO = salloc("CSO", [128, 4, 8, 16, 2], F32)
    CPO = salloc("CPO", [128, 4, 8, 2], F32)

    pst = []
    for i in range(4):
        cm = nc.psum_tensor("ps%d" % i, [128, 1024], F32)
        pst.append(cm.__enter__())
        ctxs.append(cm)

    def bank(b, n=512):
        return pst[b // 2][:, (b % 2) * 512:(b % 2) * 512 + n]

    def bank_bf(b):
        return pst[b // 2][:, (b % 2) * 512:(b % 2) * 512 + 512].bitcast(BF16)

    bank_rr = [0]
    held = set()

    def alloc_bank():
        while True:
            b = bank_rr[0]
            bank_rr[0] = (b + 1) % 8
            if b not in held:
                return b

    def alloc_bank2():
        while True:
            if bank_rr[0] % 2:
                bank_rr[0] = (bank_rr[0] + 1) % 8
            b = bank_rr[0]
            bank_rr[0] = (b + 2) % 8
            if b not in held and (b + 1) not in held:
                return b

    def uview(off, dt, shape):
        esz = 2 if dt == BF16 else 4
        n = int(np.prod(shape))
        assert off % 4 == 0 and off + n * esz <= UBYTES, (off, shape)
        v = U[:, off // 4: off // 4 + (n * esz) // 4]
        if dt != F32:
            v = v.bitcast(dt)
        if len(shape) == 2:
            return v.rearrange("p (a b) -> p a b", a=shape[0], b=shape[1])
        if len(shape) == 3:
            return v.rearrange("p (a b c) -> p a b c", a=shape[0], b=shape[1], c=shape[2])
        return v

    class UB:
        def __init__(self, name, off, dt, shape):
            esz = 2 if dt == BF16 else 4
            self.inst = UInst(tr, name, off, int(np.prod(shape)) * esz)
            self.v = uview(off, dt, shape)

        def c(self, *key):
            return (self.inst,) + key

    wstate = {"n": 0}

    def load_block(parts, nk):
        i = wstate["n"]
        wstate["n"] += 1
        slot = i % NSLOT
        ncols = sum(p[2] for p in parts)
        assert nk * ncols <= SLOT
        wv = W[:, slot, 0:nk * ncols].rearrange("p (k c) -> p k c", k=nk, c=ncols)

        def fn(e, parts=parts, wv=wv):
            out = []
            for (src, co, ncp) in parts:
                sv = src if len(src.shape) == 3 else src.rearrange("(k p) c -> p k c", p=128)
                out.append(e.dma_start(out=wv[:, :, co:co + ncp], in_=sv))
            return out
        tr.op("pool", fn, reads=(), writes=[("w", slot)], dma_sem="w%d" % slot, ndma=len(parts))
        return slot, wv

    pending_blocks = []

    def block_sched2():
        for g in range(NG):
            for l in range(NL):
                for phase in ("ffn1", "mixer", "ffn2"):
                    if phase == "mixer":
                        wl = w_in[l]

                        def seg(sg, hh):
                            return [(wl[:, 1024 * sg + 512 * hh: 1024 * sg + 512 * hh + 512], 0, 512)]
                        for hh in (0, 1):
                            yield (seg(1, hh), 8)
                            yield (seg(2, hh), 8)
                            yield (seg(0, hh), 8)
                        for hh in (0, 1):
                            yield (seg(3, hh), 8)
                        for hh in (0, 1):
                            yield (seg(7, hh), 8)
                        for hh in (0, 1):
                            yield ([(w_a[l][:, 512 * hh: 512 * hh + 512], 0, 512)], 8)
                        for j in range(4):
                            yield ([(wl[:, 5120 + 256 * j: 5120 + 256 * j + 256], 0, 256),
                                    (wl[:, 6144 + 256 * j: 6144 + 256 * j + 256], 256, 256)], 8)
                        for hh in (0, 1):
                            yield (seg(8, hh), 8)
                        for hh in (0, 1):
                            yield (seg(4, hh), 8)
                        for hh in (0, 1):
                            yield ([(w_b[l][:, 512 * hh: 512 * hh + 512], 0, 512)], 8)
                        for hh in (0, 1):
                            yield ([(w_o[l][:, 512 * hh: 512 * hh + 512], 0, 512)], 8)
                    else:
                        fi = 0 if phase == "ffn1" else 1
                        gu = fgu[fi][l]
                        dn = fdn[fi][l]
                        for j in range(11):
                            yield ([(gu[:, 256 * j: 256 * j + 256], 0, 256),
                                    (gu[:, DFF + 256 * j: DFF + 256 * j + 256], 256, 256)], 8)
                        for rep in range(2):
                            for c in range(8):
                                yield ([(dn[c], 0, 128)], NJ)

    sched = block_sched2()
    prefetched = []

    def prefetch(maxn=99):
        k = 0
        while len(prefetched) < NSLOT and k < maxn:
            k += 1
            try:
                parts, nk = next(sched)
            except StopIteration:
                return
            prefetched.append(load_block(parts, nk))

    def next_block():
        if not prefetched:
            prefetch()
        return prefetched.pop(0)

    def mm_group(out_ap, pairs, reads, writes, first_start=True):
        def fn(e, out_ap=out_ap, pairs=pairs, first_start=first_start):
            ins = None
            n = len(pairs)
            for i, (l_, r_) in enumerate(pairs):
                ins = e.matmul(out_ap, lhsT=l_, rhs=r_, start=(first_start and i == 0), stop=(i == n - 1))
            return [ins]
        tr.op("pe", fn, reads=reads, writes=writes)

    def act(out, in_, func, reads, writes, bias=None, scale=None):
        kw = {}
        if bias is not None:
            kw["bias"] = bias
        if scale is not None:
            kw["scale"] = scale

        def fn(e):
            return [e.activation(out=out, in_=in_, func=func, **kw)]
        tr.op("act", fn, reads=reads, writes=writes)

    def dve(kind, reads, writes, eng="dve", **kw):
        def fn(e):
            return [getattr(e, kind)(**kw)]
        tr.op(eng, fn, reads=reads, writes=writes)

    def dma(eng, out, in_, reads, writes, sem):
        def fn(e):
            return [e.dma_start(out=out, in_=in_)]
        tr.op(eng, fn, reads=reads, writes=writes, dma_sem=sem)

    spsem = [0]

    def sp_dma(out, in_, reads, writes, nsem=4, base="io", sem=None):
        s = sem if sem is not None else "%s%d" % (base, spsem[0] % nsem)
        spsem[0] += 1
        dma("sp", out, in_, reads, writes, s)

    def load_const(t, src, key):
        sp_dma(t[:] if not isinstance(t, bass.AP) else t, src, [], [key])

    load_const(identf, cident, "identf")
    load_const(lng, plng, "lng")
    load_const(lnb, plnb, "lnb")
    load_const(lgt, plog, "lgt")
    load_const(convw, pconv, "convw")
    load_const(normw, pnorm, "normw")
    load_const(maskc, cmask, "maskc")
    load_const(rowm, crow, "rowm")
    load_const(CB, cb, "CB")
    dve("tensor_copy", ["identf"], ["identb"], out=identb[:], in_=identf[:])
    dve("memset", [], ["ones1k"], ap=ones1k[:], constant=1.0 / 1024.0)
    dve("memset", [], ["ones128"], ap=ones128[:], constant=1.0 / 128.0)
    dve("memset", [], ["zerob"], ap=zerob[:], constant=0.0)
    dve("memset", [], ["epsl"], ap=epsl[:], constant=LN_EPS_P)
    dve("memset", [], ["epsr"], ap=epsr[:], constant=RMS_EPS)
    dve("memset", [], ["ubuf"], ap=ubuf[:], constant=0.0)
    dve("tensor_tensor", ["lgt"], ["sm8"], out=sm8[:, 0, :], in0=lgt[:, 0, :], in1=lgt[:, 1, :], op=ALU.max)
    dve("tensor_tensor", ["lgt", "sm8"], ["sm8"], out=sm8[:, 0, :], in0=sm8[:, 0, :], in1=lgt[:, 2, :], op=ALU.max)
    dve("tensor_tensor", ["lgt", "sm8"], ["sm8"], out=sm8[:, 0, :], in0=sm8[:, 0, :], in1=lgt[:, 3, :], op=ALU.max)
    dve("tensor_tensor", ["lgt", "sm8"], ["lbt"], out=lbt[:], in0=lgt[:],
        in1=sm8[:, 0:1, :].broadcast_to([128, 4, 8]), op=ALU.subtract)
    act(lbt[:], lbt[:], AF.Exp, ["lbt"], ["lbt"])
    dve("tensor_tensor", ["lbt"], ["sm8"], out=sm8[:, 1, :], in0=lbt[:, 0, :], in1=lbt[:, 1, :], op=ALU.add)
    dve("tensor_tensor", ["lbt", "sm8"], ["sm8"], out=sm8[:, 1, :], in0=sm8[:, 1, :], in1=lbt[:, 2, :], op=ALU.add)
    dve("tensor_tensor", ["lbt", "sm8"], ["sm8"], out=sm8[:, 1, :], in0=sm8[:, 1, :], in1=lbt[:, 3, :], op=ALU.add)
    dve("reciprocal", ["sm8"], ["sm8"], out=sm8[:, 2, :], in_=sm8[:, 1, :])
    dve("tensor_tensor", ["lbt", "sm8"], ["lgt"], out=lgt[:], in0=lbt[:],
        in1=sm8[:, 2:3, :].broadcast_to([128, 4, 8]), op=ALU.mult)
    dve("memset", [], ["lbt"], ap=lbt[:, 0, :], constant=0.0)
    dve("tensor_copy", ["lgt", "lbt"], ["lbt"], out=lbt[:, 1, :], in_=lgt[:, 1, :])
    dve("tensor_tensor", ["lgt", "lbt"], ["lbt"], out=lbt[:, 2, :], in0=lbt[:, 1, :], in1=lgt[:, 2, :], op=ALU.add)
    dve("tensor_tensor", ["lgt", "lbt"], ["lbt"], out=lbt[:, 3, :], in0=lbt[:, 2, :], in1=lgt[:, 3, :], op=ALU.add)
    dve("tensor_scalar", ["lbt"], ["c1t"], out=c1t[:], in0=lbt[:], scalar1=-0.5, scalar2=0.5,
        op0=ALU.mult, op1=ALU.add)
    dve("tensor_tensor", ["lbt", "c1t"], ["c0t"], out=c0t[:], in0=lbt[:], in1=c1t[:], op=ALU.add)

    prefetch()

    def xcells(name, s):
        return [(name, c, s) for c in range(8)]

    def ln_sub(l, k, subs, s, off_sq, off_yt, off_st):
        nmax = max(n for _, n in subs)
        t0, n = subs[s]
        SQ = UB("SQ", off_sq, BF16, [16, nmax])
        YT = UB("YT", off_yt, F32, [8, nmax])
        ST = UB("ST", off_st, F32, [4, nmax])
        dve("tensor_copy", xcells("xf", s), [SQ.c(0)], out=SQ.v[:, 0:8, 0:n], in_=xf[:, :, t0:t0 + n])
        act(SQ.v[:, 8:16, 0:n], xf[:, :, t0:t0 + n], AF.Square, xcells("xf", s), [SQ.c(1)])
        bm = alloc_bank()
        be = alloc_bank()
        mm_group(bank(bm, n), [(ones1k[:], SQ.v[:, c, 0:n]) for c in range(8)],
                 ["ones1k", SQ.c(0)], [("ps", bm)])
        mm_group(bank(be, n), [(ones1k[:], SQ.v[:, 8 + c, 0:n]) for c in range(8)],
                 ["ones1k", SQ.c(1)], [("ps", be)])
        act(ST.v[:, 0, 0:n], bank(bm, n), AF.Identity, [("ps", bm)], [ST.c(0)])
        act(ST.v[:, 1, 0:n], bank(bm, n), AF.Square, [("ps", bm)], [ST.c(1)])
        dve("tensor_tensor", [("ps", be), ST.c(1)], [ST.c(2)], out=ST.v[:, 2, 0:n], in0=bank(be, n),
            in1=ST.v[:, 1, 0:n], op=ALU.subtract)
        act(ST.v[:, 2, 0:n], ST.v[:, 2, 0:n], AF.Sqrt, [ST.c(2), "epsl"], [ST.c(2)], bias=epsl[:], scale=1.0)
        dve("tensor_tensor", xcells("xf", s) + [ST.c(0)], [YT.c(0)], out=YT.v[:, :, 0:n],
            in0=xf[:, :, t0:t0 + n], in1=ST.v[:, 0:1, 0:n].broadcast_to([128, 8, n]), op=ALU.subtract)
        dve("reciprocal", [ST.c(2)], [ST.c(3)], out=ST.v[:, 3, 0:n], in_=ST.v[:, 2, 0:n])
        dve("tensor_tensor", [YT.c(0), ST.c(3)], [YT.c(0)], out=YT.v[:, :, 0:n],
            in0=YT.v[:, :, 0:n], in1=ST.v[:, 3:4, 0:n].broadcast_to([128, 8, n]), op=ALU.mult)
        for c in range(8):
            act(xb[:, c, t0:t0 + n], YT.v[:, c, 0:n], AF.Identity, [YT.c(0), "lng", "lnb"], [("xb", c, s)],
                scale=lng[:, l, k, c:c + 1], bias=lnb[:, l, k, c:c + 1])
        for c in range(8):
            if c % 2 == 0:
                act(xf[:, c, t0:t0 + n], YT.v[:, c, 0:n], AF.Identity, [YT.c(0), "lng", "lnb"], [("xf", c, s)],
                    scale=lng[:, l, k, c:c + 1], bias=lnb[:, l, k, c:c + 1])
            else:
                dve("tensor_scalar", [YT.c(0), "lng", "lnb"], [("xf", c, s)], out=xf[:, c, t0:t0 + n],
                    in0=YT.v[:, c, 0:n], scalar1=lng[:, l, k, c:c + 1], scalar2=lnb[:, l, k, c:c + 1],
                    op0=ALU.mult, op1=ALU.add)

    def ffn(l, fi, subs):
        H = UB("H", 0, BF16, [NJ, GT])
        SG = UB("SG", 60480, BF16, [4, 360])
        sgi = [0]

        def gu_groups(j, slot, wv, s):
            t0, n = subs[s]
            for jj in range(2):
                bg = alloc_bank()
                bu = alloc_bank()
                mm_group(bank(bg, n), [(wv[:, k, jj * 128:(jj + 1) * 128], xb[:, k, t0:t0 + n]) for k in range(8)],
                         [("w", slot)] + xcells("xb", s), [("ps", bg)])
                mm_group(bank(bu, n), [(wv[:, k, 256 + jj * 128:256 + (jj + 1) * 128], xb[:, k, t0:t0 + n])
                                       for k in range(8)],
                         [("w", slot)] + xcells("xb", s), [("ps", bu)])
                sgv = SG.v[:, sgi[0] % 4, 0:n]
                sgc = SG.c(sgi[0] % 4)
                sgi[0] += 1
                act(sgv, bank(bg, n), AF.Silu, [("ps", bg)], [sgc])
                dve("tensor_tensor", [sgc, ("ps", bu)], [H.c(2 * j + jj, s)], out=H.v[:, 2 * j + jj, t0:t0 + n],
                    in0=bank(bu, n), in1=sgv, op=ALU.mult)
        sl0, wv0 = next_block()
        sl1, wv1 = next_block()
        gu_groups(0, sl0, wv0, 0)
        gu_groups(1, sl1, wv1, 0)
        gu_groups(0, sl0, wv0, 1)
        prefetch(1)
        gu_groups(1, sl1, wv1, 1)
        prefetch(1)
        for j in range(2, 11):
            slot, wv = next_block()
            for s in range(2):
                gu_groups(j, slot, wv, s)
            prefetch()
        kk = 0 if fi == 0 else 2
        for s, (t0, n) in enumerate(subs):
            for c in range(8):
                slot, wv = next_block()
                b = alloc_bank()
                mm_group(bank(b, n), [(wv[:, k, 0:128], H.v[:, k, t0:t0 + n]) for k in range(NJ)],
                         [("w", slot)] + [H.c(k, s) for k in range(NJ)], [("ps", b)])
                dve("scalar_tensor_tensor", [("ps", b), ("xf", c, s)], [("xf", c, s)], out=xf[:, c, t0:t0 + n],
                    in0=bank(b, n), scalar=CR, in1=xf[:, c, t0:t0 + n], op0=ALU.mult, op1=ALU.add)
                prefetch()
                if s == 1 and c == 1:
                    ln_sub(l, kk, subs, 0, 31680, 43200, 54720)
        ln_sub(l, kk, subs, 1, 31680, 43200, 54720)

    def dense8(out_chunks, subs, consumer):
        slot, wv = next_block()
        for s, (t0, n) in enumerate(subs):
            for i, co in enumerate(out_chunks):
                b = alloc_bank()
                mm_group(bank(b, n), [(wv[:, k, co:co + 128], xb[:, k, t0:t0 + n]) for k in range(8)],
                         [("w", slot)] + xcells("xb", s), [("ps", b)])
                consumer(i, s, t0, n, b)
        prefetch()

    def dense8_pair(subs, consA, consB):
        slA, wvA = next_block()
        slB, wvB = next_block()
        for s, (sl, wv, cons) in ((0, (slA, wvA, consA)), (0, (slB, wvB, consB)), (1, (slA, wvA, consA)),
                                  (1, (slB, wvB, consB))):
            t0, n = subs[s]
            for i in range(4):
                b = alloc_bank()
                mm_group(bank(b, n), [(wv[:, k, i * 128:(i + 1) * 128], xb[:, k, t0:t0 + n]) for k in range(8)],
                         [("w", sl)] + xcells("xb", s), [("ps", b)])
                cons(i, s, t0, n, b)
            if s == 1:
                prefetch(1)

    def dense8_from(src, srccells, subs, consumer):
        slot, wv = next_block()
        for s, (t0, n) in enumerate(subs):
            for i in range(4):
                b = alloc_bank()
                mm_group(bank(b, n), [(wv[:, k, i * 128:(i + 1) * 128], src.v[:, k, t0:t0 + n]) for k in range(8)],
                         [("w", slot)] + [srccells(k, s) for k in range(8)], [("ps", b)])
                consumer(i, s, t0, n, b)
        prefetch()

    def mixer(g, l, subs):
        G = GROUPS[g][1]
        chunks = group_chunks(g)
        has_sample = (g == 2)
        nch = len(chunks)
        Fb = UB("F", 0, F32, [4, GT])
        Pb = UB("P", 11520, F32, [4, GT])
        QT = UB("QT", 23040, BF16, [8, GT])
        NK = UB("NK", 34560, BF16, [8, GT])
        VT = UB("VT", 46080, BF16, [8, GT])
        A_ = UB("A", 57600, F32, [2, GT])
        B_ = UB("B", 63360, F32, [2, GT])
        R_ = UB("R", 69120, F32, [2, GT])
        QS = UB("QS", 74880, F32, [2, 360])
        if g == 0:
            dve("memset", [], ["S"], ap=S[:], constant=0.0)
        else:
            sp_dma(S[:], hp[l].rearrange("h d v -> d h v"), [("hp", l)], ["S"], base="st", nsem=2)
        act(Sb[:], S[:], AF.Copy, ["S"], ["Sb"])
        qsi = [0]
        vtog = [0]
        for hh in (0, 1):
            def cons_f(i, s, t0, n, b):
                act(Fb.v[:, i, t0:t0 + n], bank(b, n), AF.Tanh, [("ps", b)], [Fb.c(i, s)], scale=0.5)

            def cons_v(i, s, t0, n, b, hh=hh):
                h = 4 * hh + i
                act(VT.v[:, h, t0:t0 + n], bank(b, n), AF.Copy, [("ps", b)], [VT.c(h, s)])
            if hh == 0:
                dense8_pair(subs, cons_f, cons_v)
            else:
                dense8([0, 128, 256, 384], subs, cons_f)
            for i0 in (0, 2):
                pair = (i0, i0 + 1)

                def fcs(i):
                    return [Fb.c(i, 0), Fb.c(i, 1)]
                for i in pair:
                    h = 4 * hh + i
                    dve("tensor_scalar", fcs(i) + ["c0t", "c1t"], fcs(i), out=Fb.v[:, i, 0:G], in0=Fb.v[:, i, 0:G],
                        scalar1=c1t[:, l, h:h + 1], scalar2=c0t[:, l, h:h + 1], op0=ALU.mult, op1=ALU.add)
                for i in pair:
                    dve("tensor_tensor", fcs(i) + ["nEt"], [A_.c(i % 2)], out=A_.v[:, i % 2, 0:G], in0=Fb.v[:, i, 0:G],
                        in1=nEt[:, 0:G], op=ALU.mult)
                    dve("tensor_tensor", fcs(i) + ["Et"], [B_.c(i % 2)], out=B_.v[:, i % 2, 0:G], in0=Fb.v[:, i, 0:G],
                        in1=Et[:, 0:G], op=ALU.mult)
                for i in pair:
                    dve("tensor_tensor_scan", [A_.c(i % 2), B_.c(i % 2)], [Pb.c(i)], out=Pb.v[:, i, 0:G],
                        data0=A_.v[:, i % 2, 0:G], data1=B_.v[:, i % 2, 0:G], initial=0.0, op0=ALU.mult, op1=ALU.add)
                for i in pair:
                    dve("reciprocal", [Pb.c(i)], [R_.c(i % 2)], out=R_.v[:, i % 2, 0:G], in_=Pb.v[:, i, 0:G])
                for i in pair:
                    h = 4 * hh + i
                    dve("scalar_tensor_tensor", fcs(i) + [R_.c(i % 2)], [NK.c(h)], out=NK.v[:, h, 0:G],
                        in0=Fb.v[:, i, 0:G], scalar=1.0, in1=R_.v[:, i % 2, 0:G], op0=ALU.subtract, op1=ALU.mult)
                for i in pair:
                    h = 4 * hh + i
                    e0 = chunks[0][1] - 1
                    act(DEC[:, h, 0:nch], Pb.v[:, i, e0:e0 + 64 * (nch - 1) + 1:64], AF.Copy, [Pb.c(i)], [("DEC", h)])
                    if has_sample:
                        act(DECS[:, h, :], Pb.v[:, i, 643:704:4], AF.Copy, [Pb.c(i)], [("DECS", h)])
            if hh == 1:
                dense8([0, 128, 256, 384], subs, cons_v)

            def cons_q(i, s, t0, n, b, hh=hh):
                h = 4 * hh + i
                qv = QS.v[:, qsi[0] % 2, 0:n]
                qc = QS.c(qsi[0] % 2)
                qsi[0] += 1
                act(qv, bank(b, n), AF.Silu, [("ps", b)], [qc])
                dve("tensor_tensor", [qc, Pb.c(i)], [QT.c(h, s)], out=QT.v[:, h, t0:t0 + n], in0=qv,
                    in1=Pb.v[:, i, t0:t0 + n], op=ALU.mult)
                if has_sample and s == 1:
                    dve("tensor_tensor", [qc, Pb.c(i)], [("QTF", h)], out=QTF[:, h, :], in0=qv[:, n - 64:n],
                        in1=Pb.v[:, i, 640:704], op=ALU.mult)
            dense8([0, 128, 256, 384], subs, cons_q)
        OT = UB("OT", 0, F32, [8, GT])
        SC = UB("SC", 57600, BF16, [2, 512])
        KT = UB("KT", 59648, BF16, [2, 1024])
        VTT = UB("VTT", 63744, BF16, [2, 1024])
        KH = UB("KH", 67840, BF16, [2, 512])
        n2 = subs[0][1]

        def scell(buf, t0, L):
            ss = set()
            for (s, (a, n)) in enumerate(subs):
                if t0 < a + n and a < t0 + L:
                    ss.add(s)
            return [buf.c(h, s) for h in range(8) for s in ss]

        def chunk_common(ci, t0, L, mask_idx, dec_ap, dec_cells, par):
            qc = scell(QT, t0, L)
            kc = [NK.c(h) for h in range(8)]
            vc = scell(VT, t0, L)
            bs = alloc_bank()
            def fn_sc(e):
                ins = None
                for h in range(8):
                    ins = e.matmul(bank(bs)[0:L, h * 64:h * 64 + L], lhsT=NK.v[:, h, t0:t0 + L],
                                   rhs=QT.v[:, h, t0:t0 + L], start=True, stop=True)
                return [ins]
            tr.op("pe", fn_sc, reads=qc + kc, writes=[("ps", bs)])
            scv = SC.v[0:L, par, :].rearrange("p (h t) -> p h t", h=8, t=64)[:, :, 0:L]
            dve("tensor_tensor", [("ps", bs), "maskc"], [SC.c(par)], out=scv,
                in0=bank(bs)[0:L, :].rearrange("p (h t) -> p h t", h=8, t=64)[:, :, 0:L],
                in1=maskc[0:L, mask_idx, :].rearrange("p (h t) -> p h t", h=8, t=64)[:, :, 0:L], op=ALU.mult)
            khv = KH.v[:, par, :].rearrange("p (h t) -> p h t", h=8, t=64)[:, :, 0:L]
            dve("tensor_tensor", kc + dec_cells, [KH.c(par)], out=khv, in0=NK.v[:, :, t0:t0 + L], in1=dec_ap,
                op=ALU.mult)
            bk = alloc_bank()
            bv = alloc_bank()

            def fn_tk(e):
                ins = None
                for h in range(8):
                    ins = e.transpose(out=bank_bf(bk)[0:L, h * 128:(h + 1) * 128],
                                      in_=KH.v[:, par, h * 64:h * 64 + L], identity=identb[:])
                return [ins]
            tr.op("pe", fn_tk, reads=[KH.c(par), "identb"], writes=[("ps", bk)])

            def fn_tv(e):
                ins = None
                for h in range(8):
                    ins = e.transpose(out=bank_bf(bv)[0:L, h * 128:(h + 1) * 128],
                                      in_=VT.v[:, h, t0:t0 + L], identity=identb[:])
                return [ins]
            tr.op("pe", fn_tv, reads=vc + ["identb"], writes=[("ps", bv)])
            act(KT.v[0:L, par, :], bank_bf(bk)[0:L, :], AF.Copy, [("ps", bk)], [KT.c(par)])
            act(VTT.v[0:L, par, :], bank_bf(bv)[0:L, :], AF.Copy, [("ps", bv)], [VTT.c(par)])
            return qc

        if has_sample:
            S0b = UB("S0b", 69888, F32, [2, 1024])
            SOb = UB("SOb", 78080, F32, [2, 1024])
            KTM = UB("KTM", 86272, BF16, [1, 1024])

            def s0_load(q):
                sp_dma(S0b.v[:, q % 2, :].rearrange("p (h v) -> p h v", h=8, v=128),
                       s0[l, q].rearrange("h d v -> d h v"), [], [S0b.c(q % 2)], sem="si%d" % (q % 2))
            s0_load(0)
            s0_load(1)

        def part_a(ci):
            t0, L = chunks[ci]
            dec_ap = DEC[:, :, ci:ci + 1].broadcast_to([128, 8, L])
            return chunk_common(ci, t0, L, 0, dec_ap, [("DEC", h) for h in range(8)], ci % 2)

        def part_b(ci, qc):
            t0, L = chunks[ci]
            par = ci % 2
            bsu = alloc_bank2()

            def fn_su(e, L=L, par=par, bsu=bsu):
                ins = None
                for h in range(8):
                    ins = e.matmul(pst[bsu // 2][:, h * 128:(h + 1) * 128], lhsT=KT.v[0:L, par, h * 128:(h + 1) * 128],
                                   rhs=VTT.v[0:L, par, h * 128:(h + 1) * 128], start=True, stop=True)
                return [ins]
            tr.op("pe", fn_su, reads=[KT.c(par), VTT.c(par)], writes=[("ps", bsu), ("ps", bsu + 1)])
            bo = alloc_bank()

            def fn_o(e, t0=t0, L=L, par=par, bo=bo):
                ins = None
                for h in range(8):
                    e.matmul(bank(bo)[:, h * 64:h * 64 + L], lhsT=Sb[:, h, :], rhs=QT.v[:, h, t0:t0 + L],
                             start=True, stop=False)
                    ins = e.matmul(bank(bo)[:, h * 64:h * 64 + L], lhsT=VTT.v[0:L, par, h * 128:(h + 1) * 128],
                                   rhs=SC.v[0:L, par, h * 64:h * 64 + L], start=False, stop=True)
                return [ins]
            tr.op("pe", fn_o, reads=["Sb", VTT.c(par), SC.c(par)] + qc, writes=[("ps", bo)])
            dve("tensor_tensor", ["S"] + [("DEC", h) for h in range(8)], ["S"], out=S[:], in0=S[:],
                in1=DEC[:, :, ci:ci + 1].broadcast_to([128, 8, 128]), op=ALU.mult)
            dve("tensor_tensor", ["S", ("ps", bsu), ("ps", bsu + 1)], ["S"], out=S[:], in0=S[:],
                in1=pst[bsu // 2][:, :].rearrange("p (h v) -> p h v", h=8, v=128), op=ALU.subtract)
            act(OT.v[:, :, t0:t0 + L], bank(bo).rearrange("p (h t) -> p h t", h=8, t=64)[:, :, 0:L], AF.Copy,
                [("ps", bo)], scell(OT, t0, L))
            if ci < nch - 1:
                act(Sb[:], S[:], AF.Copy, ["S"], ["Sb"])

        qcs = {0: part_a(0)}
        for ci in range(nch):
            if ci + 1 < nch:
                qcs[ci + 1] = part_a(ci + 1)
            part_b(ci, qcs[ci])
        sp_dma(hp[l].rearrange("h d v -> d h v"), S[:], ["S"], [("hp", l)], base="st", nsem=2)

        if has_sample:
            tS = 640
            parS = nch % 2
            dec_ap = DECS[:, :, :].unsqueeze(3).broadcast_to([128, 8, 16, 4])
            qcells = scell(QT, tS, 64)
            kc = [NK.c(h) for h in range(8)]
            vc = scell(VT, tS, 64)
            bs = alloc_bank()

            def fn_sc(e):
                ins = None
                for h in range(8):
                    ins = e.matmul(bank(bs)[0:64, h * 64:(h + 1) * 64], lhsT=NK.v[:, h, tS:tS + 64],
                                   rhs=QT.v[:, h, tS:tS + 64], start=True, stop=True)
                return [ins]
            tr.op("pe", fn_sc, reads=qcells + kc, writes=[("ps", bs)])
            dve("tensor_tensor", [("ps", bs), "maskc"], [SC.c(parS)], out=SC.v[0:64, parS, :], in0=bank(bs)[0:64, :],
                in1=maskc[:, 1, :], op=ALU.mult)
            dve("tensor_tensor", kc + [("DECS", h) for h in range(8)], [KH.c(parS)],
                out=KH.v[:, parS, :].rearrange("p (h q t) -> p h q t", h=8, q=16, t=4),
                in0=NK.v[:, :, tS:tS + 64].rearrange("p h (q t) -> p h q t", q=16, t=4), in1=dec_ap, op=ALU.mult)
            bk = alloc_bank()
            bv = alloc_bank()

            def fn_tk(e):
                ins = None
                for h in range(8):
                    ins = e.transpose(out=bank_bf(bk)[0:64, h * 128:(h + 1) * 128],
                                      in_=KH.v[:, parS, h * 64:(h + 1) * 64], identity=identb[:])
                return [ins]
            tr.op("pe", fn_tk, reads=[KH.c(parS), "identb"], writes=[("ps", bk)])

            def fn_tv(e):
                ins = None
                for h in range(8):
                    ins = e.transpose(out=bank_bf(bv)[0:64, h * 128:(h + 1) * 128],
                                      in_=VT.v[:, h, tS:tS + 64], identity=identb[:])
                return [ins]
            tr.op("pe", fn_tv, reads=vc + ["identb"], writes=[("ps", bv)])
            act(KT.v[0:64, parS, :], bank_bf(bk)[0:64, :], AF.Copy, [("ps", bk)], [KT.c(parS)])
            act(VTT.v[0:64, parS, :], bank_bf(bv)[0:64, :], AF.Copy, [("ps", bv)], [VTT.c(parS)])
            bo = alloc_bank()
            held.add(bo)

            def fn_o0(e):
                e.matmul(bank(bo), lhsT=zerob[:], rhs=xb[:, 0, 0:512], start=True, stop=False)
                ins = None
                for h in range(8):
                    ins = e.matmul(bank(bo)[:, h * 64:(h + 1) * 64], lhsT=VTT.v[0:64, parS, h * 128:(h + 1) * 128],
                                   rhs=SC.v[0:64, parS, h * 64:(h + 1) * 64], start=False, stop=False)
                return [ins]
            tr.op("pe", fn_o0, reads=["zerob", VTT.c(parS), SC.c(parS)] + xcells("xb", 0) + xcells("xb", 1),
                  writes=[("ps", bo)])
            for q in range(16):

                def fn_oq(e, q=q):
                    ins = None
                    for h in range(8):
                        ins = e.matmul(bank(bo)[:, h * 64 + q * 4:h * 64 + q * 4 + 4],
                                       lhsT=S0b.v[:, q % 2, h * 128:(h + 1) * 128], rhs=QTF[:, h, q * 4:q * 4 + 4],
                                       start=False, stop=(q == 15 and h == 7))
                    return [ins]
                tr.op("pe", fn_oq, reads=[S0b.c(q % 2)] + [("QTF", h) for h in range(8)], writes=[("ps", bo)])
                dve("tensor_scalar", [KT.c(parS), "rowm"], [KTM.c(0)], out=KTM.v[0:64, 0, :], in0=KT.v[0:64, parS, :],
                    scalar1=rowm[:, q:q + 1], scalar2=None, op0=ALU.mult)
                bsu = alloc_bank2()

                def fn_su(e, bsu=bsu):
                    ins = None
                    for h in range(8):
                        ins = e.matmul(pst[bsu // 2][:, h * 128:(h + 1) * 128],
                                       lhsT=KTM.v[0:64, 0, h * 128:(h + 1) * 128],
                                       rhs=VTT.v[0:64, parS, h * 128:(h + 1) * 128], start=True, stop=True)
                    return [ins]
                tr.op("pe", fn_su, reads=[KTM.c(0), VTT.c(parS)], writes=[("ps", bsu), ("ps", bsu + 1)])
                sov = SOb.v[:, q % 2, :].rearrange("p (h v) -> p h v", h=8, v=128)
                dve("tensor_tensor", [S0b.c(q % 2)] + [("DECS", h) for h in range(8)], [SOb.c(q % 2)], out=sov,
                    in0=S0b.v[:, q % 2, :].rearrange("p (h v) -> p h v", h=8, v=128),
                    in1=DECS[:, :, q:q + 1].broadcast_to([128, 8, 128]), op=ALU.mult)
                dve("tensor_tensor", [SOb.c(q % 2), ("ps", bsu), ("ps", bsu + 1)], [SOb.c(q % 2)], out=sov, in0=sov,
                    in1=pst[bsu // 2][:, :].rearrange("p (h v) -> p h v", h=8, v=128), op=ALU.subtract)
                if q + 2 < 16:
                    s0_load(q + 2)
                sp_dma(hs[l, q].rearrange("h d v -> d h v"), sov, [SOb.c(q % 2)], [], sem="so%d" % (q % 2))
            act(OT.v[:, :, tS:tS + 64], bank(bo).rearrange("p (h t) -> p h t", h=8, t=64), AF.Copy,
                [("ps", bo)], scell(OT, tS, 64))
            held.discard(bo)

        SQ2 = UB("SQ2", 23040, BF16, [4, 360])
        RS = UB("RS", 25920, F32, [4, 360])
        NRB = 4
        O2 = UB("O2", 34560, BF16, [8, GT])
        SGO = UB("SGO", 31680, F32, [2, 360])
        items = [(h, s) for h in range(8) for s in range(2)]

        def rms_sq(ii):
            h, s = items[ii]
            t0, n = subs[s]
            act(SQ2.v[:, ii % 4, 0:n], OT.v[:, h, t0:t0 + n], AF.Square, [OT.c(h, s)], [SQ2.c(ii % 4)])
        rms_sq(0)
        rms_sq(1)
        for ii, (h, s) in enumerate(items):
            t0, n = subs[s]
            p2 = ii % 4
            if ii + 2 < len(items):
                rms_sq(ii + 2)
            b = alloc_bank()
            mm_group(bank(b, n), [(ones128[:], SQ2.v[:, p2, 0:n])], ["ones128", SQ2.c(p2)], [("ps", b)])
            act(RS.v[:, p2, 0:n], bank(b, n), AF.Sqrt, [("ps", b), "epsr"], [RS.c(p2)], bias=epsr[:], scale=1.0)
            dve("reciprocal", [RS.c(p2)], [RS.c(p2)], out=RS.v[:, p2, 0:n], in_=RS.v[:, p2, 0:n])
            dve("tensor_tensor", [RS.c(p2), OT.c(h, s)], [OT.c(h, s)], out=OT.v[:, h, t0:t0 + n],
                in0=OT.v[:, h, t0:t0 + n], in1=RS.v[:, p2, 0:n], op=ALU.mult)
        sgi = [0]
        for hh in (0, 1):
            def cons_og(i, s, t0, n, b, hh=hh):
                h = 4 * hh + i
                p2 = sgi[0] % 2
                sgi[0] += 1
                act(SGO.v[:, p2, 0:n], bank(b, n), AF.Silu, [("ps", b)], [SGO.c(p2)])
                dve("scalar_tensor_tensor", [SGO.c(p2), OT.c(h, s), "normw"], [O2.c(h, s)],
                    out=O2.v[:, h, t0:t0 + n], in0=OT.v[:, h, t0:t0 + n], scalar=normw[:, l:l + 1],
                    in1=SGO.v[:, p2, 0:n], op0=ALU.mult, op1=ALU.mult)
            dense8([0, 128, 256, 384], subs, cons_og)

        TGA = UB("TGA", 46080, BF16, [8, GT])
        for hh in (0, 1):
            def cons_ga(i, s, t0, n, b, hh=hh):
                c = 4 * hh + i
                act(TGA.v[:, c, t0:t0 + n], bank(b, n), AF.Tanh, [("ps", b)], [TGA.c(c, s)], scale=0.5)
            dense8([0, 128, 256, 384], subs, cons_ga)
        MA = UB("MA", 0, F32, [8, GT])
        for hh in (0, 1):
            def cons_wa(i, s, t0, n, b, hh=hh):
                c = 4 * hh + i
                dve("scalar_tensor_tensor", [("ps", b), TGA.c(c, s)], [MA.c(c, s)], out=MA.v[:, c, t0:t0 + n],
                    in0=TGA.v[:, c, t0:t0 + n], scalar=1.0, in1=bank(b, n), op0=ALU.add, op1=ALU.mult)
            dense8_from(O2, lambda k, s: O2.c(k, s), subs, cons_wa)

        U2 = UB("U2", 23040, F32, [8, GT + 2])
        CV = UB("CV", 57664, F32, [8, GT])
        CGT = UB("CGT", 80704, F32, [2, 360])
        Gp = 640 if has_sample else G
        for c in range(8):
            pass
        dve("tensor_copy", ["ubuf"], [U2.c("halo")], out=U2.v[:, :, 0:2], in_=ubuf[:, l, :, :])
        cgi = [0]
        for j in range(4):
            def cons_ch(i, s, t0, n, b, j=j):
                c = 2 * j + (i % 2)
                if i < 2:
                    act(CGT.v[:, i, 0:n], bank(b, n), AF.Copy, [("ps", b)], [CGT.c(i)])
                else:
                    dve("tensor_tensor", [("ps", b), CGT.c(i - 2)], [U2.c(c, s)], out=U2.v[:, c, 2 + t0:2 + t0 + n],
                        in0=bank(b, n), in1=CGT.v[:, i - 2, 0:n], op=ALU.mult)
            dense8([0, 128, 256, 384], subs, cons_ch)
        for c in range(8):
            uc = [U2.c(c, 0), U2.c(c, 1), U2.c("halo")]
            dve("tensor_scalar", uc + ["convw"], [CV.c(c)], out=CV.v[:, c, 0:Gp], in0=U2.v[:, c, 0:Gp],
                scalar1=convw[:, l, 0, c:c + 1], scalar2=None, op0=ALU.mult)
            dve("scalar_tensor_tensor", uc + ["convw", CV.c(c)], [CV.c(c)], out=CV.v[:, c, 0:Gp],
                in0=U2.v[:, c, 1:Gp + 1], scalar=convw[:, l, 1, c:c + 1], in1=CV.v[:, c, 0:Gp],
                op0=ALU.mult, op1=ALU.add)
            dve("scalar_tensor_tensor", uc + ["convw", CV.c(c)], [CV.c(c)], out=CV.v[:, c, 0:Gp],
                in0=U2.v[:, c, 2:Gp + 2], scalar=convw[:, l, 2, c:c + 1], in1=CV.v[:, c, 0:Gp],
                op0=ALU.mult, op1=ALU.add)
        allu = [U2.c(c, s) for c in range(8) for s in range(2)] + [U2.c("halo")]
        allcv = [CV.c(c) for c in range(8)]
        dve("tensor_copy", allu, ["ubuf"], out=ubuf[:, l, :, :], in_=U2.v[:, :, Gp:Gp + 2])
        if has_sample:
            act(CPO[:, l, :, :], U2.v[:, :, Gp:Gp + 2], AF.Copy, allu, [("CPO", l)])
            dve("tensor_copy", ["CB"], ["US"], out=US[:, :, :, 0:2], in_=CB[:, l, :, :, :])
            dve("tensor_copy", allu + ["US"], ["US"], out=US[:, :, :, 2:6],
                in_=U2.v[:, :, 2 + 640:2 + 704].rearrange("p c (q t) -> p c q t", q=16, t=4))
            act(CSO[:, l, :, :, :], US[:, :, :, 4:6], AF.Copy, ["US"], [("CSO", l)])
            for jx in range(3):
                wbc = convw[:, l, jx, :].unsqueeze(2).unsqueeze(3).broadcast_to([128, 8, 16, 4])
                if jx == 0:
                    dve("tensor_tensor", ["US", "convw"], ["CVS"], out=CVS[:], in0=US[:, :, :, 0:4], in1=wbc,
                        op=ALU.mult)
                else:
                    dve("tensor_tensor", ["US", "convw"], ["CVS2"], out=CVS2[:], in0=US[:, :, :, jx:jx + 4], in1=wbc,
                        op=ALU.mult)
                    dve("tensor_tensor", ["CVS", "CVS2"], ["CVS"], out=CVS[:], in0=CVS[:], in1=CVS2[:], op=ALU.add)
            dve("tensor_copy", ["CVS"] + allcv, allcv, out=CV.v[:, :, 640:704],
                in_=CVS[:].rearrange("p c q t -> p c (q t)"))
        TGB = UB("TGB", 46144, BF16, [8, GT])
        for hh in (0, 1):
            def cons_gb(i, s, t0, n, b, hh=hh):
                c = 4 * hh + i
                act(TGB.v[:, c, t0:t0 + n], bank(b, n), AF.Tanh, [("ps", b)], [TGB.c(c, s)], scale=0.5)
            dense8([0, 128, 256, 384], subs, cons_gb)
        BC = UB("BC", 23040, BF16, [8, GT])
        for hh in (0, 1):
            def cons_bg(i, s, t0, n, b, hh=hh):
                c = 4 * hh + i
                dve("tensor_tensor", [("ps", b)] + allcv, [BC.c(c, s)], out=BC.v[:, c, t0:t0 + n], in0=bank(b, n),
                    in1=CV.v[:, c, t0:t0 + n], op=ALU.mult)
            dense8([0, 128, 256, 384], subs, cons_bg)
        M = UB("M", 34560, BF16, [8, GT])
        YTMP = UB("YTMP", 57664, F32, [2, 360])
        yi = [0]
        for hh in (0, 1):
            def cons_wb(i, s, t0, n, b, hh=hh):
                c = 4 * hh + i
                p2 = yi[0] % 2
                yi[0] += 1
                dve("scalar_tensor_tensor", [("ps", b), TGB.c(c, s)], [YTMP.c(p2)], out=YTMP.v[:, p2, 0:n],
                    in0=TGB.v[:, c, t0:t0 + n], scalar=1.0, in1=bank(b, n), op0=ALU.add, op1=ALU.mult)
                dve("tensor_tensor", [YTMP.c(p2), MA.c(c, s)], [M.c(c, s)], out=M.v[:, c, t0:t0 + n],
                    in0=YTMP.v[:, p2, 0:n], in1=MA.v[:, c, t0:t0 + n], op=ALU.add)
            dense8_from(BC, lambda k, s: BC.c(k, s), subs, cons_wb)
        def cons_wo(c, s, t0, n, b):
            dve("scalar_tensor_tensor", [("ps", b), ("xf", c, s)], [("xf", c, s)], out=xf[:, c, t0:t0 + n],
                in0=bank(b, n), scalar=CR, in1=xf[:, c, t0:t0 + n], op0=ALU.mult, op1=ALU.add)
        slA, wvA = next_block()
        slB, wvB = next_block()
        for s, (t0, n) in enumerate(subs):
            for hh, (sl, wv) in enumerate(((slA, wvA), (slB, wvB))):
                for i in range(4):
                    b = alloc_bank()
                    mm_group(bank(b, n), [(wv[:, k, i * 128:(i + 1) * 128], M.v[:, k, t0:t0 + n]) for k in range(8)],
                             [("w", sl)] + [M.c(k, s) for k in range(8)], [("ps", b)])
                    cons_wo(4 * hh + i, s, t0, n, b)
                if s == 1 and hh == 0:
                    ln_sub(l, 1, subs, 0, 46080, 57600, 69120)
        prefetch()
        ln_sub(l, 1, subs, 1, 46080, 57600, 69120)

    for g in range(NG):
        g0, G = GROUPS[g]
        n = G // 2
        subs = [(0, n), (n, G - n)]
        sp_dma(xf[:, :, 0:G], xT[:, :, g0:g0 + G], [], [("xf", c, s) for c in range(8) for s in range(2)])
        sp_dma(Et[:], cE[:, g, :], [], ["Et"])
        dve("tensor_scalar", ["Et"], ["nEt"], out=nEt[:], in0=Et[:], scalar1=-1.0, scalar2=1.0, op0=ALU.mult,
            op1=ALU.add)
        for s, (t0, nn) in enumerate(subs):
            act(xb[:, :, t0:t0 + nn], xf[:, :, t0:t0 + nn], AF.Copy, xcells("xf", s), xcells("xb", s))
        for l in range(NL):
            ffn(l, 0, subs)
            if DEBUG["stop"] == "ffn1":
                break
            mixer(g, l, subs)
            if DEBUG["stop"] == "mixer":
                break
            ffn(l, 1, subs)
        allxf = [("xf", c, s) for c in range(8) for s in range(2)]
        if g == 0:
            sp_dma(yT[:, :, 0:G - 16], xf[:, :, 16:G], allxf, [], base="yo", nsem=1)
        else:
            sp_dma(yT[:, :, g0 - 16:g0 - 16 + G], xf[:, :, 0:G], allxf, [], base="yo", nsem=1)
    if DEBUG["stop"] is None:
      sp_dma(cso[:, 0:NL], CSO[:, 0:NL], [("CSO", l) for l in range(NL)], [], base="yo", nsem=1)
      sp_dma(cpo[:, 0:NL], CPO[:, 0:NL], [("CPO", l) for l in range(NL)], [], base="yo", nsem=1)

    cum = {}
    for en in Tr.ENGS:
        c = 0
        arr = []
        for o in tr.ops[en]:
            if o["signal"]:
                c += 1
            arr.append(c)
        cum[en] = arr
    dsems = sorted(tr.dval.keys())
    semctx = {}
    for en in Tr.ENGS:
        cm = nc.semaphore("e_" + en)
        semctx[("e", en)] = cm.__enter__()
        ctxs.append(cm)
    for ds in dsems:
        cm = nc.semaphore("d_" + ds)
        semctx[("d", ds)] = cm.__enter__()
        ctxs.append(cm)

    def replay(en, e):
        my = semctx[("e", en)]
        for o in tr.ops[en]:
            for (k, v) in o["waits"]:
                if k[0] == "e":
                    e.wait_ge(semctx[k], cum[k[1]][v])
                else:
                    e.wait_ge(semctx[k], v)
            inss = o["fn"](e)
            if o["dma"] is not None:
                for ins in inss:
                    ins.then_inc(semctx[("d", o["dma"])], 16)
            elif o["signal"]:
                inss[-1].then_inc(my, 1)
        if en == "sp":
            for ds in dsems:
                e.wait_ge(semctx[("d", ds)], tr.dval[ds])

    with nc.Block() as block:
        @block.tensor
        def _(e):
            replay("pe", e)

        @block.scalar
        def _(e):
            replay("act", e)

        @block.vector
        def _(e):
            replay("dve", e)

        @block.gpsimd
        def _(e):
            replay("pool", e)

        @block.sync
        def _(e):
            replay("sp", e)
    for cm in reversed(ctxs):
        cm.__exit__(None, None, None)
    stats = {en: len(tr.ops[en]) for en in Tr.ENGS}
    return nc, stats


def host_consts():
    ident = np.eye(128, dtype=np.float32)
    s_ = np.arange(64)[:, None]
    t_ = np.arange(64)[None, :]
    mc = np.where(s_ <= t_, -1.0, 0.0).astype(np.float32)
    ms = np.where((s_ <= t_) & (s_ // 4 == t_ // 4), -1.0, 0.0).astype(np.float32)
    cmask = np.zeros((64, 2, 512), np.float32)
    cmask[:, 0, :] = np.tile(mc, (1, 8))
    cmask[:, 1, :] = np.tile(ms, (1, 8))
    crow = (np.arange(64)[:, None] // 4 == np.arange(16)[None, :]).astype(np.float32)
    E = np.zeros((3, GT), np.float32)
    E[0, 0] = 1.0
    E[0, 16::64] = 1.0
    E[1, 0:704:64] = 1.0
    E[2, 0:640:64] = 1.0
    E[2, 640:704:4] = 1.0
    cE = np.ascontiguousarray(np.broadcast_to(E[None], (128, 3, GT)))
    return ident, cmask, crow, cE


_CACHE = {}


def kernel(x_prompt, x_sample, state_hgrn, state_conv, meta_tokens, w_in, hgrn_lb_logits,
           hgrn_norm_w, conv_w, w_branch_a, w_branch_b, w_out, ffn1_w_gu, ffn1_w_down,
           ffn2_w_gu, ffn2_w_down, ln_g, ln_b):
    f32 = np.float32
    x_prompt = np.asarray(x_prompt, f32)
    x_sample = np.asarray(x_sample, f32)
    state_hgrn = np.asarray(state_hgrn, f32)
    state_conv = np.asarray(state_conv, f32)
    meta = np.asarray(meta_tokens, f32)
    nc, stats = build_program()
    ident, cmask, crow, cE = host_consts()

    def fm(a):
        a = np.asarray(a, f32)
        sh = a.shape[:-1]
        return np.ascontiguousarray(np.moveaxis(a.reshape(sh + (8, 128)), -1, 0))

    def dn_layout(w):
        w = np.asarray(w, f32).reshape(4, NJ, 128, 8, 128)
        return np.ascontiguousarray(w.transpose(0, 3, 2, 1, 4))

    shared = {
        "w_in": np.ascontiguousarray(np.asarray(w_in, f32)),
        "w_a": np.ascontiguousarray(np.asarray(w_branch_a, f32)),
        "w_b": np.ascontiguousarray(np.asarray(w_branch_b, f32)),
        "w_o": np.ascontiguousarray(np.asarray(w_out, f32)),
        "f1gu": np.ascontiguousarray(np.asarray(ffn1_w_gu, f32)),
        "f2gu": np.ascontiguousarray(np.asarray(ffn2_w_gu, f32)),
        "f1d": dn_layout(ffn1_w_down),
        "f2d": dn_layout(ffn2_w_down),
        "plng": fm(ln_g), "plnb": fm(ln_b), "plog": fm(hgrn_lb_logits), "pconv": fm(conv_w),
        "pnorm": np.ascontiguousarray(np.asarray(hgrn_norm_w, f32).T),
        "cident": ident, "cmask": cmask, "crow": crow, "cE": cE,
    }
    in_maps = []
    for c in range(8):
        toks = np.concatenate([meta, x_prompt[c], x_sample[16 * c:16 * c + 16].reshape(64, D)], axis=0)
        xT = np.ascontiguousarray(toks.reshape(NTOK, 8, 128).transpose(2, 1, 0))
        m = dict(shared)
        m["xT"] = xT
        m["s0"] = np.ascontiguousarray(state_hgrn[:, 16 * c:16 * c + 16])
        sc = state_conv[:, 16 * c:16 * c + 16]
        m["cb"] = np.ascontiguousarray(sc.reshape(4, 16, 2, 8, 128).transpose(4, 0, 3, 1, 2))
        in_maps.append(m)
    res = run_bass_kernel_spmd(nc, in_maps, core_ids=list(range(8)))
    y_prompt = np.zeros((8, 2048, D), f32)
    y_sample = np.zeros((128, 4, D), f32)
    nhp = np.zeros((4, 8, 8, 128, 128), f32)
    ncp = np.zeros((4, 8, 2, 1024), f32)
    nhs = np.zeros((4, 128, 8, 128, 128), f32)
    ncs = np.zeros((4, 128, 2, 1024), f32)
    for c in range(8):
        r = res.results[c]
        y = np.asarray(r["yT"]).transpose(2, 1, 0).reshape(NTOK - 16, D)
        y_prompt[c] = y[0:2048]
        y_sample[16 * c:16 * c + 16] = y[2048:].reshape(16, 4, D)
        nhp[:, c] = np.asarray(r["hp"])
        nhs[:, 16 * c:16 * c + 16] = np.asarray(r["hs"])
        cso = np.asarray(r["cso"])
        ncs[:, 16 * c:16 * c + 16] = cso.transpose(1, 3, 4, 2, 0).reshape(4, 16, 2, 1024)
        cpo = np.asarray(r["cpo"])
        ncp[:, c] = cpo.transpose(1, 3, 2, 0).reshape(4, 2, 1024)
    return (y_prompt, y_sample, nhp, ncp, nhs, ncs)
```

```python
import numpy as np
import concourse.bass as bass
import concourse.mybir as mybir
from concourse.bass_utils import run_bass_kernel_spmd

F32 = mybir.dt.float32
BF16 = mybir.dt.bfloat16
F32R = mybir.dt.float32r
AF = mybir.ActivationFunctionType
ALU = mybir.AluOpType

D = 1024
DFF = 2816
DEPTH = 4
NCK = 8
NJ = 22
NTOK = 2128
GROUPS = [(0, 720), (720, 704), (1424, 704)]
GT = 720
ALPHA = (2 * DEPTH) ** 0.25
CR = 0.5 / ALPHA
LN_EPS_P = 1e-5 / (ALPHA * ALPHA)
RMS_EPS = 1e-6
SLOT = 4096
NSLOT = 3
UBYTES = 90112

DEBUG = {"layers": DEPTH, "groups": 3, "stop": None}


class Tr:
    ENGS = ("pe", "act", "dve", "pool", "sp")

    def __init__(self):
        self.ops = {n: [] for n in self.ENGS}
        self.waited = {n: {} for n in self.ENGS}
        self.cells = {}
        self.dval = {}
        self.live = []
        self.ninst = 0

    def cell(self, key):
        c = self.cells.get(key)
        if c is None:
            pend = {}
            if isinstance(key, tuple) and isinstance(key[0], UInst):
                assert key[0].alive, ("access to retired buffer", key[0].name)
                pend = dict(key[0].pending)
            c = [{}, pend]
            self.cells[key] = c
        return c

    def op(self, eng, fn, reads=(), writes=(), dma_sem=None, ndma=1):
        deps = {}

        def merge(d):
            for k, v in d.items():
                if deps.get(k, -1) < v:
                    deps[k] = v
        for c in reads:
            merge(self.cell(c)[0])
        for c in writes:
            cc = self.cell(c)
            merge(cc[0])
            merge(cc[1])
        if dma_sem is not None:
            prev = self.dval.get(dma_sem, 0)
            if prev > 0:
                deps[("d", dma_sem)] = max(deps.get(("d", dma_sem), -1), prev)
        waits = []
        wd = self.waited[eng]
        for k, v in deps.items():
            if k == ("e", "pe") and eng == "pe":
                continue
            if wd.get(k, -1) >= v:
                continue
            wd[k] = v
            waits.append((k, v))
            if k[0] == "e":
                self.ops[k[1]][v]["signal"] = True
        idx = len(self.ops[eng])
        if dma_sem is not None:
            self.dval[dma_sem] = self.dval.get(dma_sem, 0) + 16 * ndma
            ev = (("d", dma_sem), self.dval[dma_sem])
        else:
            ev = (("e", eng), idx)
        self.ops[eng].append({"fn": fn, "waits": waits, "signal": False, "dma": dma_sem})
        for c in reads:
            r = self.cell(c)[1]
            if r.get(ev[0], -1) < ev[1]:
                r[ev[0]] = ev[1]
        for c in writes:
            self.cells[c] = [{ev[0]: ev[1]}, {}]
        return ev


class UInst:
    def __init__(self, tr, name, off, nbytes):
        self.name, self.off, self.nbytes = name, off, nbytes
        self.alive = True
        self.pending = {}
        keep = []
        for o in tr.live:
            if o.off < off + nbytes and off < o.off + o.nbytes:
                if o.alive:
                    o.alive = False
                    for k, c in list(tr.cells.items()):
                        if isinstance(k, tuple) and k[0] is o:
                            for d in (c[0], c[1]):
                                for kk, vv in d.items():
                                    if o.pending.get(kk, -1) < vv:
                                        o.pending[kk] = vv
                            del tr.cells[k]
                for kk, vv in o.pending.items():
                    if self.pending.get(kk, -1) < vv:
                        self.pending[kk] = vv
                if not (off <= o.off and o.off + o.nbytes <= off + nbytes):
                    keep.append(o)
            else:
                keep.append(o)
        keep.append(self)
        tr.live = keep


def group_chunks(g):
    if g == 0:
        return [(0, 16)] + [(16 + 64 * i, 64) for i in range(11)]
    if g == 1:
        return [(64 * i, 64) for i in range(11)]
    return [(64 * i, 64) for i in range(10)]


def build_program():
    nc = bass.Bass("TRN2", target_bir_lowering=False)
    NL = DEBUG["layers"]
    NG = DEBUG["groups"]

    def din(name, shape):
        return nc.dram_tensor(name, shape, F32, kind="ExternalInput").ap()

    def dout(name, shape):
        return nc.dram_tensor(name, shape, F32, kind="ExternalOutput").ap()

    xT = din("xT", [128, 8, NTOK])
    s0 = din("s0", [4, 16, 8, 128, 128])
    cb = din("cb", [128, 4, 8, 16, 2])
    w_in = din("w_in", [4, D, 9 * D])
    w_a = din("w_a", [4, D, D])
    w_b = din("w_b", [4, D, D])
    w_o = din("w_o", [4, D, D])
    fgu = [din("f1gu", [4, D, 2 * DFF]), din("f2gu", [4, D, 2 * DFF])]
    fdn = [din("f1d", [4, 8, 128, NJ, 128]), din("f2d", [4, 8, 128, NJ, 128])]
    plng = din("plng", [128, 4, 3, 8])
    plnb = din("plnb", [128, 4, 3, 8])
    plog = din("plog", [128, 4, 8])
    pconv = din("pconv", [128, 4, 3, 8])
    pnorm = din("pnorm", [128, 4])
    cident = din("cident", [128, 128])
    cmask = din("cmask", [64, 2, 512])
    crow = din("crow", [64, 16])
    cE = din("cE", [128, 3, GT])

    yT = dout("yT", [128, 8, NTOK - 16])
    hs = dout("hs", [4, 16, 8, 128, 128])
    hp = dout("hp", [4, 8, 128, 128])
    cso = dout("cso", [128, 4, 8, 16, 2])
    cpo = dout("cpo", [128, 4, 8, 2])

    tr = Tr()
    sb = {}
    ctxs = []

    def salloc(name, shape, dt):
        cm = nc.sbuf_tensor(name, shape, dt)
        t = cm.__enter__()
        ctxs.append(cm)
        sb[name] = t
        return t

    xf = salloc("xf", [128, 8, GT], F32)
    xb = salloc("xb", [128, 8, GT], BF16)
    S = salloc("S", [128, 8, 128], F32)
    Sb = salloc("Sb", [128, 8, 128], BF16)
    W = salloc("W", [128, NSLOT, SLOT], BF16)
    U = salloc("U", [128, UBYTES // 4], F32)
    identf = salloc("identf", [128, 128], F32)
    identb = salloc("identb", [128, 128], BF16)
    ones1k = salloc("ones1k", [128, 128], BF16)
    ones128 = salloc("ones128", [128, 128], BF16)
    zerob = salloc("zerob", [128, 128], BF16)
    lng = salloc("lng", [128, 4, 3, 8], F32)
    lnb = salloc("lnb", [128, 4, 3, 8], F32)
    lgt = salloc("lgt", [128, 4, 8], F32)
    lbt = salloc("lbt", [128, 4, 8], F32)
    c0t = salloc("c0t", [128, 4, 8], F32)
    c1t = salloc("c1t", [128, 4, 8], F32)
    sm8 = salloc("sm8", [128, 3, 8], F32)
    convw = salloc("convw", [128, 4, 3, 8], F32)
    normw = salloc("normw", [128, 4], F32)
    maskc = salloc("maskc", [64, 2, 512], F32)
    rowm = salloc("rowm", [64, 16], F32)
    Et = salloc("Et", [128, GT], F32)
    nEt = salloc("nEt", [128, GT], F32)
    ubuf = salloc("ubuf", [128, 4, 8, 2], F32)
    DEC = salloc("DEC", [128, 8, 12], F32)
    DECS = salloc("DECS", [128, 8, 16], F32)
    CB = salloc("CB", [128, 4, 8, 16, 2], F32)
    US = salloc("US", [128, 8, 16, 6], F32)
    CVS = salloc("CVS", [128, 8, 16, 4], F32)
    CVS2 = salloc("CVS2", [128, 8, 16, 4], F32)
    QTF = salloc("QTF", [128, 8, 64], F32)
    epsl = salloc("epsl", [128, 1], F32)
    epsr = salloc("epsr", [128, 1], F32)
    CSO = salloc("CSO", [128, 4, 8, 16, 2], F32)
    CPO = salloc("CPO", [128, 4, 8, 2], F32)

    pst = []
    for i in range(4):
        cm = nc.psum_tensor("ps%d" % i, [128, 1024], F32)
        pst.append(cm.__enter__())
        ctxs.append(cm)

    def bank(b, n=512):
        return pst[b // 2][:, (b % 2) * 512:(b % 2) * 512 + n]

    def bank_bf(b):
        return pst[b // 2][:, (b % 2) * 512:(b % 2) * 512 + 512].bitcast(BF16)

    bank_rr = [0]
    held = set()

    def alloc_bank():
        while True:
            b = bank_rr[0]
            bank_rr[0] = (b + 1) % 8
            if b not in held:
                return b

    def alloc_bank2():
        while True:
            if bank_rr[0] % 2:
                bank_rr[0] = (bank_rr[0] + 1) % 8
            b = bank_rr[0]
            bank_rr[0] = (b + 2) % 8
            if b not in held and (b + 1) not in held:
                return b

    def uview(off, dt, shape):
        esz = 2 if dt == BF16 else 4
        n = int(np.prod(shape))
        assert off % 4 == 0 and off + n * esz <= UBYTES, (off, shape)
        v = U[:, off // 4: off // 4 + (n * esz) // 4]
        if dt != F32:
            v = v.bitcast(dt)
        if len(shape) == 2:
            return v.rearrange("p (a b) -> p a b", a=shape[0], b=shape[1])
        if len(shape) == 3:
            return v.rearrange("p (a b c) -> p a b c", a=shape[0], b=shape[1], c=shape[2])
        return v

    class UB:
        def __init__(self, name, off, dt, shape):
            esz = 2 if dt == BF16 else 4
            self.inst = UInst(tr, name, off, int(np.prod(shape)) * esz)
            self.v = uview(off, dt, shape)

        def c(self, *key):
            return (self.inst,) + key

    wstate = {"n": 0}

    def load_block(parts, nk):
        i = wstate["n"]
        wstate["n"] += 1
        slot = i % NSLOT
        ncols = sum(p[2] for p in parts)
        assert nk * ncols <= SLOT
        wv = W[:, slot, 0:nk * ncols].rearrange("p (k c) -> p k c", k=nk, c=ncols)

        def fn(e, parts=parts, wv=wv):
            out = []
            for (src, co, ncp) in parts:
                sv = src if len(src.shape) == 3 else src.rearrange("(k p) c -> p k c", p=128)
                out.append(e.dma_start(out=wv[:, :, co:co + ncp], in_=sv))
            return out
        tr.op("pool", fn, reads=(), writes=[("w", slot)], dma_sem="w%d" % slot, ndma=len(parts))
        return slot, wv

    pending_blocks = []

    def block_sched2():
        for g in range(NG):
            for l in range(NL):
                for phase in ("ffn1", "mixer", "ffn2"):
                    if phase == "mixer":
                        wl = w_in[l]

                        def seg(sg, hh):
                            return [(wl[:, 1024 * sg + 512 * hh: 1024 * sg + 512 * hh + 512], 0, 512)]
                        for hh in (0, 1):
                            yield (seg(1, hh), 8)
                            yield (seg(2, hh), 8)
                            yield (seg(0, hh), 8)
                        for hh in (0, 1):
                            yield (seg(3, hh), 8)
                        for hh in (0, 1):
                            yield (seg(7, hh), 8)
                        for hh in (0, 1):
                            yield ([(w_a[l][:, 512 * hh: 512 * hh + 512], 0, 512)], 8)
                        for j in range(4):
                            yield ([(wl[:, 5120 + 256 * j: 5120 + 256 * j + 256], 0, 256),
                                    (wl[:, 6144 + 256 * j: 6144 + 256 * j + 256], 256, 256)], 8)
                        for hh in (0, 1):
                            yield (seg(8, hh), 8)
                        for hh in (0, 1):
                            yield (seg(4, hh), 8)
                        for hh in (0, 1):
                            yield ([(w_b[l][:, 512 * hh: 512 * hh + 512], 0, 512)], 8)
                        for hh in (0, 1):
                            yield ([(w_o[l][:, 512 * hh: 512 * hh + 512], 0, 512)], 8)
                    else:
                        fi = 0 if phase == "ffn1" else 1
                        gu = fgu[fi][l]
                        dn = fdn[fi][l]
                        for j in range(11):
                            yield ([(gu[:, 256 * j: 256 * j + 256], 0, 256),
                                    (gu[:, DFF + 256 * j: DFF + 256 * j + 256], 256, 256)], 8)
                        for rep in range(2):
                            for c in range(8):
                                yield ([(dn[c], 0, 128)], NJ)

    sched = block_sched2()
    prefetched = []

    def prefetch(maxn=99):
        k = 0
        while len(prefetched) < NSLOT and k < maxn:
            k += 1
            try:
                parts, nk = next(sched)
            except StopIteration:
                return
            prefetched.append(load_block(parts, nk))

    def next_block():
        if not prefetched:
            prefetch()
        return prefetched.pop(0)

    def mm_group(out_ap, pairs, reads, writes, first_start=True):
        def fn(e, out_ap=out_ap, pairs=pairs, first_start=first_start):
            ins = None
            n = len(pairs)
            for i, (l_, r_) in enumerate(pairs):
                ins = e.matmul(out_ap, lhsT=l_, rhs=r_, start=(first_start and i == 0), stop=(i == n - 1))
            return [ins]
        tr.op("pe", fn, reads=reads, writes=writes)

    def act(out, in_, func, reads, writes, bias=None, scale=None):
        kw = {}
        if bias is not None:
            kw["bias"] = bias
        if scale is not None:
            kw["scale"] = scale

        def fn(e):
            return [e.activation(out=out, in_=in_, func=func, **kw)]
        tr.op("act", fn, reads=reads, writes=writes)

    def dve(kind, reads, writes, eng="dve", **kw):
        def fn(e):
            return [getattr(e, kind)(**kw)]
        tr.op(eng, fn, reads=reads, writes=writes)

    def dma(eng, out, in_, reads, writes, sem):
        def fn(e):
            return [e.dma_start(out=out, in_=in_)]
        tr.op(eng, fn, reads=reads, writes=writes, dma_sem=sem)

    spsem = [0]

    def sp_dma(out, in_, reads, writes, nsem=4, base="io", sem=None):
        s = sem if sem is not None else "%s%d" % (base, spsem[0] % nsem)
        spsem[0] += 1
        dma("sp", out, in_, reads, writes, s)

    def load_const(t, src, key):
        sp_dma(t[:] if not isinstance(t, bass.AP) else t, src, [], [key])

    load_const(identf, cident, "identf")
    load_const(lng, plng, "lng")
    load_const(lnb, plnb, "lnb")
    load_const(lgt, plog, "lgt")
    load_const(convw, pconv, "convw")
    load_const(normw, pnorm, "normw")
    load_const(maskc, cmask, "maskc")
    load_const(rowm, crow, "rowm")
    load_const(CB, cb, "CB")
    dve("tensor_copy", ["identf"], ["identb"], out=identb[:], in_=identf[:])
    dve("memset", [], ["ones1k"], ap=ones1k[:], constant=1.0 / 1024.0)
    dve("memset", [], ["ones128"], ap=ones128[:], constant=1.0 / 128.0)
    dve("memset", [], ["zerob"], ap=zerob[:], constant=0.0)
    dve("memset", [], ["epsl"], ap=epsl[:], constant=LN_EPS_P)
    dve("memset", [], ["epsr"], ap=epsr[:], constant=RMS_EPS)
    dve("memset", [], ["ubuf"], ap=ubuf[:], constant=0.0)
    dve("tensor_tensor", ["lgt"], ["sm8"], out=sm8[:, 0, :], in0=lgt[:, 0, :], in1=lgt[:, 1, :], op=ALU.max)
    dve("tensor_tensor", ["lgt", "sm8"], ["sm8"], out=sm8[:, 0, :], in0=sm8[:, 0, :], in1=lgt[:, 2, :], op=ALU.max)
    dve("tensor_tensor", ["lgt", "sm8"], ["sm8"], out=sm8[:, 0, :], in0=sm8[:, 0, :], in1=lgt[:, 3, :], op=ALU.max)
    dve("tensor_tensor", ["lgt", "sm8"], ["lbt"], out=lbt[:], in0=lgt[:],
        in1=sm8[:, 0:1, :].broadcast_to([128, 4, 8]), op=ALU.subtract)
    act(lbt[:], lbt[:], AF.Exp, ["lbt"], ["lbt"])
    dve("tensor_tensor", ["lbt"], ["sm8"], out=sm8[:, 1, :], in0=lbt[:, 0, :], in1=lbt[:, 1, :], op=ALU.add)
    dve("tensor_tensor", ["lbt", "sm8"], ["sm8"], out=sm8[:, 1, :], in0=sm8[:, 1, :], in1=lbt[:, 2, :], op=ALU.add)
    dve("tensor_tensor", ["lbt", "sm8"], ["sm8"], out=sm8[:, 1, :], in0=sm8[:, 1, :], in1=lbt[:, 3, :], op=ALU.add)
    dve("reciprocal", ["sm8"], ["sm8"], out=sm8[:, 2, :], in_=sm8[:, 1, :])
    dve("tensor_tensor", ["lbt", "sm8"], ["lgt"], out=lgt[:], in0=lbt[:],
        in1=sm8[:, 2:3, :].broadcast_to([128, 4, 8]), op=ALU.mult)
    dve("memset", [], ["lbt"], ap=lbt[:, 0, :], constant=0.0)
    dve("tensor_copy", ["lgt", "lbt"], ["lbt"], out=lbt[:, 1, :], in_=lgt[:, 1, :])
    dve("tensor_tensor", ["lgt", "lbt"], ["lbt"], out=lbt[:, 2, :], in0=lbt[:, 1, :], in1=lgt[:, 2, :], op=ALU.add)
    dve("tensor_tensor", ["lgt", "lbt"], ["lbt"], out=lbt[:, 3, :], in0=lbt[:, 2, :], in1=lgt[:, 3, :], op=ALU.add)
    dve("tensor_scalar", ["lbt"], ["c1t"], out=c1t[:], in0=lbt[:], scalar1=-0.5, scalar2=0.5,
        op0=ALU.mult, op1=ALU.add)
    dve("tensor_tensor", ["lbt", "c1t"], ["c0t"], out=c0t[:], in0=lbt[:], in1=c1t[:], op=ALU.add)

    prefetch()

    def xcells(name, s):
        return [(name, c, s) for c in range(8)]

    def ln_sub(l, k, subs, s, off_sq, off_yt, off_st):
        nmax = max(n for _, n in subs)
        t0, n = subs[s]
        SQ = UB("SQ", off_sq, BF16, [16, nmax])
        YT = UB("YT", off_yt, F32, [8, nmax])
        ST = UB("ST", off_st, F32, [4, nmax])
        dve("tensor_copy", xcells("xf", s), [SQ.c(0)], out=SQ.v[:, 0:8, 0:n], in_=xf[:, :, t0:t0 + n])
        act(SQ.v[:, 8:16, 0:n], xf[:, :, t0:t0 + n], AF.Square, xcells("xf", s), [SQ.c(1)])
        bm = alloc_bank()
        be = alloc_bank()
        mm_group(bank(bm, n), [(ones1k[:], SQ.v[:, c, 0:n]) for c in range(8)],
                 ["ones1k", SQ.c(0)], [("ps", bm)])
        mm_group(bank(be, n), [(ones1k[:], SQ.v[:, 8 + c, 0:n]) for c in range(8)],
                 ["ones1k", SQ.c(1)], [("ps", be)])
        act(ST.v[:, 0, 0:n], bank(bm, n), AF.Identity, [("ps", bm)], [ST.c(0)])
        act(ST.v[:, 1, 0:n], bank(bm, n), AF.Square, [("ps", bm)], [ST.c(1)])
        dve("tensor_tensor", [("ps", be), ST.c(1)], [ST.c(2)], out=ST.v[:, 2, 0:n], in0=bank(be, n),
            in1=ST.v[:, 1, 0:n], op=ALU.subtract)
        act(ST.v[:, 2, 0:n], ST.v[:, 2, 0:n], AF.Sqrt, [ST.c(2), "epsl"], [ST.c(2)], bias=epsl[:], scale=1.0)
        dve("tensor_tensor", xcells("xf", s) + [ST.c(0)], [YT.c(0)], out=YT.v[:, :, 0:n],
            in0=xf[:, :, t0:t0 + n], in1=ST.v[:, 0:1, 0:n].broadcast_to([128, 8, n]), op=ALU.subtract)
        dve("reciprocal", [ST.c(2)], [ST.c(3)], out=ST.v[:, 3, 0:n], in_=ST.v[:, 2, 0:n])
        dve("tensor_tensor", [YT.c(0), ST.c(3)], [YT.c(0)], out=YT.v[:, :, 0:n],
            in0=YT.v[:, :, 0:n], in1=ST.v[:, 3:4, 0:n].broadcast_to([128, 8, n]), op=ALU.mult)
        for c in range(8):
            act(xb[:, c, t0:t0 + n], YT.v[:, c, 0:n], AF.Identity, [YT.c(0), "lng", "lnb"], [("xb", c, s)],
                scale=lng[:, l, k, c:c + 1], bias=lnb[:, l, k, c:c + 1])
        for c in range(8):
            if c % 2 == 0:
                act(xf[:, c, t0:t0 + n], YT.v[:, c, 0:n], AF.Identity, [YT.c(0), "lng", "lnb"], [("xf", c, s)],
                    scale=lng[:, l, k, c:c + 1], bias=lnb[:, l, k, c:c + 1])
            else:
                dve("tensor_scalar", [YT.c(0), "lng", "lnb"], [("xf", c, s)], out=xf[:, c, t0:t0 + n],
                    in0=YT.v[:, c, 0:n], scalar1=lng[:, l, k, c:c + 1], scalar2=lnb[:, l, k, c:c + 1],
                    op0=ALU.mult, op1=ALU.add)

    def ffn(l, fi, subs):
        H = UB("H", 0, BF16, [NJ, GT])
        SG = UB("SG", 60480, BF16, [4, 360])
        sgi = [0]

        def gu_groups(j, slot, wv, s):
            t0, n = subs[s]
            for jj in range(2):
                bg = alloc_bank()
                bu = alloc_bank()
                mm_group(bank(bg, n), [(wv[:, k, jj * 128:(jj + 1) * 128], xb[:, k, t0:t0 + n]) for k in range(8)],
                         [("w", slot)] + xcells("xb", s), [("ps", bg)])
                mm_group(bank(bu, n), [(wv[:, k, 256 + jj * 128:256 + (jj + 1) * 128], xb[:, k, t0:t0 + n])
                                       for k in range(8)],
                         [("w", slot)] + xcells("xb", s), [("ps", bu)])
                sgv = SG.v[:, sgi[0] % 4, 0:n]
                sgc = SG.c(sgi[0] % 4)
                sgi[0] += 1
                act(sgv, bank(bg, n), AF.Silu, [("ps", bg)], [sgc])
                dve("tensor_tensor", [sgc, ("ps", bu)], [H.c(2 * j + jj, s)], out=H.v[:, 2 * j + jj, t0:t0 + n],
                    in0=bank(bu, n), in1=sgv, op=ALU.mult)
        sl0, wv0 = next_block()
        sl1, wv1 = next_block()
        gu_groups(0, sl0, wv0, 0)
        gu_groups(1, sl1, wv1, 0)
        gu_groups(0, sl0, wv0, 1)
        prefetch(1)
        gu_groups(1, sl1, wv1, 1)
        prefetch(1)
        for j in range(2, 11):
            slot, wv = next_block()
            for s in range(2):
                gu_groups(j, slot, wv, s)
            prefetch()
        kk = 0 if fi == 0 else 2
        for s, (t0, n) in enumerate(subs):
            for c in range(8):
                slot, wv = next_block()
                b = alloc_bank()
                mm_group(bank(b, n), [(wv[:, k, 0:128], H.v[:, k, t0:t0 + n]) for k in range(NJ)],
                         [("w", slot)] + [H.c(k, s) for k in range(NJ)], [("ps", b)])
                dve("scalar_tensor_tensor", [("ps", b), ("xf", c, s)], [("xf", c, s)], out=xf[:, c, t0:t0 + n],
                    in0=bank(b, n), scalar=CR, in1=xf[:, c, t0:t0 + n], op0=ALU.mult, op1=ALU.add)
                prefetch()
                if s == 1 and c == 1:
                    ln_sub(l, kk, subs, 0, 31680, 43200, 54720)
        ln_sub(l, kk, subs, 1, 31680, 43200, 54720)

    def dense8(out_chunks, subs, consumer):
        slot, wv = next_block()
        for s, (t0, n) in enumerate(subs):
            for i, co in enumerate(out_chunks):
                b = alloc_bank()
                mm_group(bank(b, n), [(wv[:, k, co:co + 128], xb[:, k, t0:t0 + n]) for k in range(8)],
                         [("w", slot)] + xcells("xb", s), [("ps", b)])
                consumer(i, s, t0, n, b)
        prefetch()

    def dense8_pair(subs, consA, consB):
        slA, wvA = next_block()
        slB, wvB = next_block()
        for s, (sl, wv, cons) in ((0, (slA, wvA, consA)), (0, (slB, wvB, consB)), (1, (slA, wvA, consA)),
                                  (1, (slB, wvB, consB))):
            t0, n = subs[s]
            for i in range(4):
                b = alloc_bank()
                mm_group(bank(b, n), [(wv[:, k, i * 128:(i + 1) * 128], xb[:, k, t0:t0 + n]) for k in range(8)],
                         [("w", sl)] + xcells("xb", s), [("ps", b)])
                cons(i, s, t0, n, b)
            if s == 1:
                prefetch(1)

    def dense8_from(src, srccells, subs, consumer):
        slot, wv = next_block()
        for s, (t0, n) in enumerate(subs):
            for i in range(4):
                b = alloc_bank()
                mm_group(bank(b, n), [(wv[:, k, i * 128:(i + 1) * 128], src.v[:, k, t0:t0 + n]) for k in range(8)],
                         [("w", slot)] + [srccells(k, s) for k in range(8)], [("ps", b)])
                consumer(i, s, t0, n, b)
        prefetch()

    def mixer(g, l, subs):
        G = GROUPS[g][1]
        chunks = group_chunks(g)
        has_sample = (g == 2)
        nch = len(chunks)
        Fb = UB("F", 0, F32, [4, GT])
        Pb = UB("P", 11520, F32, [4, GT])
        QT = UB("QT", 23040, BF16, [8, GT])
        NK = UB("NK", 34560, BF16, [8, GT])
        VT = UB("VT", 46080, BF16, [8, GT])
        A_ = UB("A", 57600, F32, [2, GT])
        B_ = UB("B", 63360, F32, [2, GT])
        R_ = UB("R", 69120, F32, [2, GT])
        QS = UB("QS", 74880, F32, [2, 360])
        if g == 0:
            dve("memset", [], ["S"], ap=S[:], constant=0.0)
        else:
            sp_dma(S[:], hp[l].rearrange("h d v -> d h v"), [("hp", l)], ["S"], base="st", nsem=2)
        act(Sb[:], S[:], AF.Copy, ["S"], ["Sb"])
        qsi = [0]
        vtog = [0]
        for hh in (0, 1):
            def cons_f(i, s, t0, n, b, hh=hh):
                h = 4 * hh + i
                act(Fb.v[:, i, t0:t0 + n], bank(b, n), AF.Tanh, [("ps", b)], [Fb.c(i, s)], scale=0.5)
                act(Fb.v[:, i, t0:t0 + n], Fb.v[:, i, t0:t0 + n], AF.Identity, [Fb.c(i, s), "c0t", "c1t"], [Fb.c(i, s)],
                    scale=c1t[:, l, h:h + 1], bias=c0t[:, l, h:h + 1])

            def cons_v(i, s, t0, n, b, hh=hh):
                h = 4 * hh + i
                act(VT.v[:, h, t0:t0 + n], bank(b, n), AF.Copy, [("ps", b)], [VT.c(h, s)])
            if hh == 0:
                dense8_pair(subs, cons_f, cons_v)
            else:
                dense8([0, 128, 256, 384], subs, cons_f)
            for i0 in (0, 2):
                pair = (i0, i0 + 1)

                def fcs(i):
                    return [Fb.c(i, 0), Fb.c(i, 1)]
                for i in pair:
                    dve("tensor_tensor", fcs(i) + ["nEt"], [A_.c(i % 2)], out=A_.v[:, i % 2, 0:G], in0=Fb.v[:, i, 0:G],
                        in1=nEt[:, 0:G], op=ALU.mult)
                    dve("tensor_tensor", fcs(i) + ["Et"], [B_.c(i % 2)], out=B_.v[:, i % 2, 0:G], in0=Fb.v[:, i, 0:G],
                        in1=Et[:, 0:G], op=ALU.mult)
                for i in pair:
                    dve("tensor_tensor_scan", [A_.c(i % 2), B_.c(i % 2)], [Pb.c(i)], out=Pb.v[:, i, 0:G],
                        data0=A_.v[:, i % 2, 0:G], data1=B_.v[:, i % 2, 0:G], initial=0.0, op0=ALU.mult, op1=ALU.add)
                for i in pair:
                    dve("reciprocal", [Pb.c(i)], [R_.c(i % 2)], out=R_.v[:, i % 2, 0:G], in_=Pb.v[:, i, 0:G])
                for i in pair:
                    h = 4 * hh + i
                    dve("scalar_tensor_tensor", fcs(i) + [R_.c(i % 2)], [NK.c(h)], out=NK.v[:, h, 0:G],
                        in0=Fb.v[:, i, 0:G], scalar=1.0, in1=R_.v[:, i % 2, 0:G], op0=ALU.subtract, op1=ALU.mult)
                for i in pair:
                    h = 4 * hh + i
                    e0 = chunks[0][1] - 1
                    act(DEC[:, h, 0:nch], Pb.v[:, i, e0:e0 + 64 * (nch - 1) + 1:64], AF.Copy, [Pb.c(i)], [("DEC", h)])
                    if has_sample:
                        act(DECS[:, h, :], Pb.v[:, i, 643:704:4], AF.Copy, [Pb.c(i)], [("DECS", h)])
            if hh == 1:
                dense8([0, 128, 256, 384], subs, cons_v)

            def cons_q(i, s, t0, n, b, hh=hh):
                h = 4 * hh + i
                qv = QS.v[:, qsi[0] % 2, 0:n]
                qc = QS.c(qsi[0] % 2)
                qsi[0] += 1
                act(qv, bank(b, n), AF.Silu, [("ps", b)], [qc])
                dve("tensor_tensor", [qc, Pb.c(i)], [QT.c(h, s)], out=QT.v[:, h, t0:t0 + n], in0=qv,
                    in1=Pb.v[:, i, t0:t0 + n], op=ALU.mult)
                if has_sample and s == 1:
                    dve("tensor_tensor", [qc, Pb.c(i)], [("QTF", h)], out=QTF[:, h, :], in0=qv[:, n - 64:n],
                        in1=Pb.v[:, i, 640:704], op=ALU.mult)
            dense8([0, 128, 256, 384], subs, cons_q)
        OT = UB("OT", 0, F32, [8, GT])
        SC = UB("SC", 57600, BF16, [2, 512])
        KT = UB("KT", 59648, BF16, [2, 1024])
        VTT = UB("VTT", 63744, BF16, [2, 1024])
        KH = UB("KH", 67840, BF16, [2, 512])
        n2 = subs[0][1]

        def scell(buf, t0, L):
            ss = set()
            for (s, (a, n)) in enumerate(subs):
                if t0 < a + n and a < t0 + L:
                    ss.add(s)
            return [buf.c(h, s) for h in range(8) for s in ss]

        def chunk_common(ci, t0, L, mask_idx, dec_ap, dec_cells, par):
            qc = scell(QT, t0, L)
            kc = [NK.c(h) for h in range(8)]
            vc = scell(VT, t0, L)
            bs = alloc_bank()
            def fn_sc(e):
                ins = None
                for h in range(8):
                    ins = e.matmul(bank(bs)[0:L, h * 64:h * 64 + L], lhsT=NK.v[:, h, t0:t0 + L],
                                   rhs=QT.v[:, h, t0:t0 + L], start=True, stop=True)
                return [ins]
            tr.op("pe", fn_sc, reads=qc + kc, writes=[("ps", bs)])
            scv = SC.v[0:L, par, :].rearrange("p (h t) -> p h t", h=8, t=64)[:, :, 0:L]
            dve("tensor_tensor", [("ps", bs), "maskc"], [SC.c(par)], out=scv,
                in0=bank(bs)[0:L, :].rearrange("p (h t) -> p h t", h=8, t=64)[:, :, 0:L],
                in1=maskc[0:L, mask_idx, :].rearrange("p (h t) -> p h t", h=8, t=64)[:, :, 0:L], op=ALU.mult)
            khv = KH.v[:, par, :].rearrange("p (h t) -> p h t", h=8, t=64)[:, :, 0:L]
            dve("tensor_tensor", kc + dec_cells, [KH.c(par)], out=khv, in0=NK.v[:, :, t0:t0 + L], in1=dec_ap,
                op=ALU.mult)
            bk = alloc_bank()
            bv = alloc_bank()

            def fn_tk(e):
                ins = None
                for h in range(8):
                    ins = e.transpose(out=bank_bf(bk)[0:L, h * 128:(h + 1) * 128],
                                      in_=KH.v[:, par, h * 64:h * 64 + L], identity=identb[:])
                return [ins]
            tr.op("pe", fn_tk, reads=[KH.c(par), "identb"], writes=[("ps", bk)])

            def fn_tv(e):
                ins = None
                for h in range(8):
                    ins = e.transpose(out=bank_bf(bv)[0:L, h * 128:(h + 1) * 128],
                                      in_=VT.v[:, h, t0:t0 + L], identity=identb[:])
                return [ins]
            tr.op("pe", fn_tv, reads=vc + ["identb"], writes=[("ps", bv)])
            act(KT.v[0:L, par, :], bank_bf(bk)[0:L, :], AF.Copy, [("ps", bk)], [KT.c(par)])
            act(VTT.v[0:L, par, :], bank_bf(bv)[0:L, :], AF.Copy, [("ps", bv)], [VTT.c(par)])
            return qc

        if has_sample:
            S0b = UB("S0b", 69888, F32, [2, 1024])
            SOb = UB("SOb", 78080, F32, [2, 1024])
            KTM = UB("KTM", 86272, BF16, [1, 1024])

            def s0_load(q):
                sp_dma(S0b.v[:, q % 2, :].rearrange("p (h v) -> p h v", h=8, v=128),
                       s0[l, q].rearrange("h d v -> d h v"), [], [S0b.c(q % 2)], sem="si%d" % (q % 2))
            s0_load(0)
            s0_load(1)

        def part_a(ci):
            t0, L = chunks[ci]
            dec_ap = DEC[:, :, ci:ci + 1].broadcast_to([128, 8, L])
            return chunk_common(ci, t0, L, 0, dec_ap, [("DEC", h) for h in range(8)], ci % 2)

        def part_b(ci, qc):
            t0, L = chunks[ci]
            par = ci % 2
            bsu = alloc_bank2()

            def fn_su(e, L=L, par=par, bsu=bsu):
                ins = None
                for h in range(8):
                    ins = e.matmul(pst[bsu // 2][:, h * 128:(h + 1) * 128], lhsT=KT.v[0:L, par, h * 128:(h + 1) * 128],
                                   rhs=VTT.v[0:L, par, h * 128:(h + 1) * 128], start=True, stop=True)
                return [ins]
            tr.op("pe", fn_su, reads=[KT.c(par), VTT.c(par)], writes=[("ps", bsu), ("ps", bsu + 1)])
            bo = alloc_bank()

            def fn_o(e, t0=t0, L=L, par=par, bo=bo):
                ins = None
                for h in range(8):
                    e.matmul(bank(bo)[:, h * 64:h * 64 + L], lhsT=Sb[:, h, :], rhs=QT.v[:, h, t0:t0 + L],
                             start=True, stop=False)
                    ins = e.matmul(bank(bo)[:, h * 64:h * 64 + L], lhsT=VTT.v[0:L, par, h * 128:(h + 1) * 128],
                                   rhs=SC.v[0:L, par, h * 64:h * 64 + L], start=False, stop=True)
                return [ins]
            tr.op("pe", fn_o, reads=["Sb", VTT.c(par), SC.c(par)] + qc, writes=[("ps", bo)])
            dve("tensor_tensor", ["S"] + [("DEC", h) for h in range(8)], ["S"], out=S[:], in0=S[:],
                in1=DEC[:, :, ci:ci + 1].broadcast_to([128, 8, 128]), op=ALU.mult)
            dve("tensor_tensor", ["S", ("ps", bsu), ("ps", bsu + 1)], ["S"], out=S[:], in0=S[:],
                in1=pst[bsu // 2][:, :].rearrange("p (h v) -> p h v", h=8, v=128), op=ALU.subtract)
            act(OT.v[:, :, t0:t0 + L], bank(bo).rearrange("p (h t) -> p h t", h=8, t=64)[:, :, 0:L], AF.Copy,
                [("ps", bo)], scell(OT, t0, L))
            if ci < nch - 1:
                act(Sb[:], S[:], AF.Copy, ["S"], ["Sb"])

        qcs = {0: part_a(0)}
        for ci in range(nch):
            if ci + 1 < nch:
                qcs[ci + 1] = part_a(ci + 1)
            part_b(ci, qcs[ci])
        sp_dma(hp[l].rearrange("h d v -> d h v"), S[:], ["S"], [("hp", l)], base="st", nsem=2)

        if has_sample:
            tS = 640
            parS = nch % 2
            dec_ap = DECS[:, :, :].unsqueeze(3).broadcast_to([128, 8, 16, 4])
            qcells = scell(QT, tS, 64)
            kc = [NK.c(h) for h in range(8)]
            vc = scell(VT, tS, 64)
            bs = alloc_bank()

            def fn_sc(e):
                ins = None
                for h in range(8):
                    ins = e.matmul(bank(bs)[0:64, h * 64:(h + 1) * 64], lhsT=NK.v[:, h, tS:tS + 64],
                                   rhs=QT.v[:, h, tS:tS + 64], start=True, stop=True)
                return [ins]
            tr.op("pe", fn_sc, reads=qcells + kc, writes=[("ps", bs)])
            dve("tensor_tensor", [("ps", bs), "maskc"], [SC.c(parS)], out=SC.v[0:64, parS, :], in0=bank(bs)[0:64, :],
                in1=maskc[:, 1, :], op=ALU.mult)
            dve("tensor_tensor", kc + [("DECS", h) for h in range(8)], [KH.c(parS)],
                out=KH.v[:, parS, :].rearrange("p (h q t) -> p h q t", h=8, q=16, t=4),
                in0=NK.v[:, :, tS:tS + 64].rearrange("p h (q t) -> p h q t", q=16, t=4), in1=dec_ap, op=ALU.mult)
            bk = alloc_bank()
            bv = alloc_bank()

            def fn_tk(e):
                ins = None
                for h in range(8):
                    ins = e.transpose(out=bank_bf(bk)[0:64, h * 128:(h + 1) * 128],
                                      in_=KH.v[:, parS, h * 64:(h + 1) * 64], identity=identb[:])
                return [ins]
            tr.op("pe", fn_tk, reads=[KH.c(parS), "identb"], writes=[("ps", bk)])

            def fn_tv(e):
                ins = None
                for h in range(8):
                    ins = e.transpose(out=bank_bf(bv)[0:64, h * 128:(h + 1) * 128],
                                      in_=VT.v[:, h, tS:tS + 64], identity=identb[:])
                return [ins]
            tr.op("pe", fn_tv, reads=vc + ["identb"], writes=[("ps", bv)])
            act(KT.v[0:64, parS, :], bank_bf(bk)[0:64, :], AF.Copy, [("ps", bk)], [KT.c(parS)])
            act(VTT.v[0:64, parS, :], bank_bf(bv)[0:64, :], AF.Copy, [("ps", bv)], [VTT.c(parS)])
            bo = alloc_bank()
            held.add(bo)

            def fn_o0(e):
                e.matmul(bank(bo), lhsT=zerob[:], rhs=xb[:, 0, 0:512], start=True, stop=False)
                ins = None
                for h in range(8):
                    ins = e.matmul(bank(bo)[:, h * 64:(h + 1) * 64], lhsT=VTT.v[0:64, parS, h * 128:(h + 1) * 128],
                                   rhs=SC.v[0:64, parS, h * 64:(h + 1) * 64], start=False, stop=False)
                return [ins]
            tr.op("pe", fn_o0, reads=["zerob", VTT.c(parS), SC.c(parS)] + xcells("xb", 0) + xcells("xb", 1),
                  writes=[("ps", bo)])
            for q in range(16):

                def fn_oq(e, q=q):
                    ins = None
                    for h in range(8):
                        ins = e.matmul(bank(bo)[:, h * 64 + q * 4:h * 64 + q * 4 + 4],
                                       lhsT=S0b.v[:, q % 2, h * 128:(h + 1) * 128], rhs=QTF[:, h, q * 4:q * 4 + 4],
                                       start=False, stop=(q == 15 and h == 7))
                    return [ins]
                tr.op("pe", fn_oq, reads=[S0b.c(q % 2)] + [("QTF", h) for h in range(8)], writes=[("ps", bo)])
                dve("tensor_scalar", [KT.c(parS), "rowm"], [KTM.c(0)], out=KTM.v[0:64, 0, :], in0=KT.v[0:64, parS, :],
                    scalar1=rowm[:, q:q + 1], scalar2=None, op0=ALU.mult)
                bsu = alloc_bank2()

                def fn_su(e, bsu=bsu):
                    ins = None
                    for h in range(8):
                        ins = e.matmul(pst[bsu // 2][:, h * 128:(h + 1) * 128],
                                       lhsT=KTM.v[0:64, 0, h * 128:(h + 1) * 128],
                                       rhs=VTT.v[0:64, parS, h * 128:(h + 1) * 128], start=True, stop=True)
                    return [ins]
                tr.op("pe", fn_su, reads=[KTM.c(0), VTT.c(parS)], writes=[("ps", bsu), ("ps", bsu + 1)])
                sov = SOb.v[:, q % 2, :].rearrange("p (h v) -> p h v", h=8, v=128)
                dve("tensor_tensor", [S0b.c(q % 2)] + [("DECS", h) for h in range(8)], [SOb.c(q % 2)], out=sov,
                    in0=S0b.v[:, q % 2, :].rearrange("p (h v) -> p h v", h=8, v=128),
                    in1=DECS[:, :, q:q + 1].broadcast_to([128, 8, 128]), op=ALU.mult)
                dve("tensor_tensor", [SOb.c(q % 2), ("ps", bsu), ("ps", bsu + 1)], [SOb.c(q % 2)], out=sov, in0=sov,
                    in1=pst[bsu // 2][:, :].rearrange("p (h v) -> p h v", h=8, v=128), op=ALU.subtract)
                if q + 2 < 16:
                    s0_load(q + 2)
                sp_dma(hs[l, q].rearrange("h d v -> d h v"), sov, [SOb.c(q % 2)], [], sem="so%d" % (q % 2))
            act(OT.v[:, :, tS:tS + 64], bank(bo).rearrange("p (h t) -> p h t", h=8, t=64), AF.Copy,
                [("ps", bo)], scell(OT, tS, 64))
            held.discard(bo)

        SQ2 = UB("SQ2", 23040, BF16, [4, 360])
        RS = UB("RS", 25920, F32, [4, 360])
        NRB = 4
        O2 = UB("O2", 34560, BF16, [8, GT])
        SGO = UB("SGO", 31680, F32, [2, 360])
        items = [(h, s) for h in range(8) for s in range(2)]

        def rms_sq(ii):
            h, s = items[ii]
            t0, n = subs[s]
            act(SQ2.v[:, ii % 4, 0:n], OT.v[:, h, t0:t0 + n], AF.Square, [OT.c(h, s)], [SQ2.c(ii % 4)])
        rms_sq(0)
        rms_sq(1)
        for ii, (h, s) in enumerate(items):
            t0, n = subs[s]
            p2 = ii % 4
            if ii + 2 < len(items):
                rms_sq(ii + 2)
            b = alloc_bank()
            mm_group(bank(b, n), [(ones128[:], SQ2.v[:, p2, 0:n])], ["ones128", SQ2.c(p2)], [("ps", b)])
            act(RS.v[:, p2, 0:n], bank(b, n), AF.Sqrt, [("ps", b), "epsr"], [RS.c(p2)], bias=epsr[:], scale=1.0)
            dve("reciprocal", [RS.c(p2)], [RS.c(p2)], out=RS.v[:, p2, 0:n], in_=RS.v[:, p2, 0:n])
            dve("tensor_tensor", [RS.c(p2), OT.c(h, s)], [OT.c(h, s)], out=OT.v[:, h, t0:t0 + n],
                in0=OT.v[:, h, t0:t0 + n], in1=RS.v[:, p2, 0:n], op=ALU.mult)
        sgi = [0]
        for hh in (0, 1):
            def cons_og(i, s, t0, n, b, hh=hh):
                h = 4 * hh + i
                p2 = sgi[0] % 2
                sgi[0] += 1
                act(SGO.v[:, p2, 0:n], bank(b, n), AF.Silu, [("ps", b)], [SGO.c(p2)])
                dve("scalar_tensor_tensor", [SGO.c(p2), OT.c(h, s), "normw"], [O2.c(h, s)],
                    out=O2.v[:, h, t0:t0 + n], in0=OT.v[:, h, t0:t0 + n], scalar=normw[:, l:l + 1],
                    in1=SGO.v[:, p2, 0:n], op0=ALU.mult, op1=ALU.mult)
            dense8([0, 128, 256, 384], subs, cons_og)

        TGA = UB("TGA", 46080, BF16, [8, GT])
        for hh in (0, 1):
            def cons_ga(i, s, t0, n, b, hh=hh):
                c = 4 * hh + i
                act(TGA.v[:, c, t0:t0 + n], bank(b, n), AF.Tanh, [("ps", b)], [TGA.c(c, s)], scale=0.5)
            dense8([0, 128, 256, 384], subs, cons_ga)
        MA = UB("MA", 0, F32, [8, GT])
        for hh in (0, 1):
            def cons_wa(i, s, t0, n, b, hh=hh):
                c = 4 * hh + i
                dve("scalar_tensor_tensor", [("ps", b), TGA.c(c, s)], [MA.c(c, s)], out=MA.v[:, c, t0:t0 + n],
                    in0=TGA.v[:, c, t0:t0 + n], scalar=1.0, in1=bank(b, n), op0=ALU.add, op1=ALU.mult)
            dense8_from(O2, lambda k, s: O2.c(k, s), subs, cons_wa)

        U2 = UB("U2", 23040, F32, [8, GT + 2])
        CV = UB("CV", 57664, F32, [8, GT])
        CGT = UB("CGT", 80704, F32, [2, 360])
        Gp = 640 if has_sample else G
        for c in range(8):
            pass
        dve("tensor_copy", ["ubuf"], [U2.c("halo")], out=U2.v[:, :, 0:2], in_=ubuf[:, l, :, :])
        cgi = [0]
        for j in range(4):
            def cons_ch(i, s, t0, n, b, j=j):
                c = 2 * j + (i % 2)
                if i < 2:
                    act(CGT.v[:, i, 0:n], bank(b, n), AF.Copy, [("ps", b)], [CGT.c(i)])
                else:
                    dve("tensor_tensor", [("ps", b), CGT.c(i - 2)], [U2.c(c, s)], out=U2.v[:, c, 2 + t0:2 + t0 + n],
                        in0=bank(b, n), in1=CGT.v[:, i - 2, 0:n], op=ALU.mult)
            dense8([0, 128, 256, 384], subs, cons_ch)
        for c in range(8):
            uc = [U2.c(c, 0), U2.c(c, 1), U2.c("halo")]
            dve("tensor_scalar", uc + ["convw"], [CV.c(c)], out=CV.v[:, c, 0:Gp], in0=U2.v[:, c, 0:Gp],
                scalar1=convw[:, l, 0, c:c + 1], scalar2=None, op0=ALU.mult)
            dve("scalar_tensor_tensor", uc + ["convw", CV.c(c)], [CV.c(c)], out=CV.v[:, c, 0:Gp],
                in0=U2.v[:, c, 1:Gp + 1], scalar=convw[:, l, 1, c:c + 1], in1=CV.v[:, c, 0:Gp],
                op0=ALU.mult, op1=ALU.add)
            dve("scalar_tensor_tensor", uc + ["convw", CV.c(c)], [CV.c(c)], out=CV.v[:, c, 0:Gp],
                in0=U2.v[:, c, 2:Gp + 2], scalar=convw[:, l, 2, c:c + 1], in1=CV.v[:, c, 0:Gp],
                op0=ALU.mult, op1=ALU.add)
        allu = [U2.c(c, s) for c in range(8) for s in range(2)] + [U2.c("halo")]
        allcv = [CV.c(c) for c in range(8)]
        dve("tensor_copy", allu, ["ubuf"], out=ubuf[:, l, :, :], in_=U2.v[:, :, Gp:Gp + 2])
        if has_sample:
            act(CPO[:, l, :, :], U2.v[:, :, Gp:Gp + 2], AF.Copy, allu, [("CPO", l)])
            dve("tensor_copy", ["CB"], ["US"], out=US[:, :, :, 0:2], in_=CB[:, l, :, :, :])
            dve("tensor_copy", allu + ["US"], ["US"], out=US[:, :, :, 2:6],
                in_=U2.v[:, :, 2 + 640:2 + 704].rearrange("p c (q t) -> p c q t", q=16, t=4))
            act(CSO[:, l, :, :, :], US[:, :, :, 4:6], AF.Copy, ["US"], [("CSO", l)])
            for jx in range(3):
                wbc = convw[:, l, jx, :].unsqueeze(2).unsqueeze(3).broadcast_to([128, 8, 16, 4])
                if jx == 0:
                    dve("tensor_tensor", ["US", "convw"], ["CVS"], out=CVS[:], in0=US[:, :, :, 0:4], in1=wbc,
                        op=ALU.mult)
                else:
                    dve("tensor_tensor", ["US", "convw"], ["CVS2"], out=CVS2[:], in0=US[:, :, :, jx:jx + 4], in1=wbc,
                        op=ALU.mult)
                    dve("tensor_tensor", ["CVS", "CVS2"], ["CVS"], out=CVS[:], in0=CVS[:], in1=CVS2[:], op=ALU.add)
            dve("tensor_copy", ["CVS"] + allcv, allcv, out=CV.v[:, :, 640:704],
                in_=CVS[:].rearrange("p c q t -> p c (q t)"))
        TGB = UB("TGB", 46144, BF16, [8, GT])
        for hh in (0, 1):
            def cons_gb(i, s, t0, n, b, hh=hh):
                c = 4 * hh + i
                act(TGB.v[:, c, t0:t0 + n], bank(b, n), AF.Tanh, [("ps", b)], [TGB.c(c, s)], scale=0.5)
            dense8([0, 128, 256, 384], subs, cons_gb)
        BC = UB("BC", 23040, BF16, [8, GT])
        for hh in (0, 1):
            def cons_bg(i, s, t0, n, b, hh=hh):
                c = 4 * hh + i
                dve("tensor_tensor", [("ps", b)] + allcv, [BC.c(c, s)], out=BC.v[:, c, t0:t0 + n], in0=bank(b, n),
                    in1=CV.v[:, c, t0:t0 + n], op=ALU.mult)
            dense8([0, 128, 256, 384], subs, cons_bg)
        M = UB("M", 34560, BF16, [8, GT])
        YTMP = UB("YTMP", 57664, F32, [2, 360])
        yi = [0]
        for hh in (0, 1):
            def cons_wb(i, s, t0, n, b, hh=hh):
                c = 4 * hh + i
                p2 = yi[0] % 2
                yi[0] += 1
                dve("scalar_tensor_tensor", [("ps", b), TGB.c(c, s)], [YTMP.c(p2)], out=YTMP.v[:, p2, 0:n],
                    in0=TGB.v[:, c, t0:t0 + n], scalar=1.0, in1=bank(b, n), op0=ALU.add, op1=ALU.mult)
                dve("tensor_tensor", [YTMP.c(p2), MA.c(c, s)], [M.c(c, s)], out=M.v[:, c, t0:t0 + n],
                    in0=YTMP.v[:, p2, 0:n], in1=MA.v[:, c, t0:t0 + n], op=ALU.add)
            dense8_from(BC, lambda k, s: BC.c(k, s), subs, cons_wb)
        def cons_wo(c, s, t0, n, b):
            dve("scalar_tensor_tensor", [("ps", b), ("xf", c, s)], [("xf", c, s)], out=xf[:, c, t0:t0 + n],
                in0=bank(b, n), scalar=CR, in1=xf[:, c, t0:t0 + n], op0=ALU.mult, op1=ALU.add)
        slA, wvA = next_block()
        slB, wvB = next_block()
        for s, (t0, n) in enumerate(subs):
            for hh, (sl, wv) in enumerate(((slA, wvA), (slB, wvB))):
                for i in range(4):
                    b = alloc_bank()
                    mm_group(bank(b, n), [(wv[:, k, i * 128:(i + 1) * 128], M.v[:, k, t0:t0 + n]) for k in range(8)],
                             [("w", sl)] + [M.c(k, s) for k in range(8)], [("ps", b)])
                    cons_wo(4 * hh + i, s, t0, n, b)
                if s == 1 and hh == 0:
                    ln_sub(l, 1, subs, 0, 46080, 57600, 69120)
        prefetch()
        ln_sub(l, 1, subs, 1, 46080, 57600, 69120)

    for g in range(NG):
        g0, G = GROUPS[g]
        n = G // 2
        subs = [(0, n), (n, G - n)]
        sp_dma(xf[:, :, 0:G], xT[:, :, g0:g0 + G], [], [("xf", c, s) for c in range(8) for s in range(2)])
        sp_dma(Et[:], cE[:, g, :], [], ["Et"])
        dve("tensor_scalar", ["Et"], ["nEt"], out=nEt[:], in0=Et[:], scalar1=-1.0, scalar2=1.0, op0=ALU.mult,
            op1=ALU.add)
        for s, (t0, nn) in enumerate(subs):
            act(xb[:, :, t0:t0 + nn], xf[:, :, t0:t0 + nn], AF.Copy, xcells("xf", s), xcells("xb", s))
        for l in range(NL):
            ffn(l, 0, subs)
            if DEBUG["stop"] == "ffn1":
                break
            mixer(g, l, subs)
            if DEBUG["stop"] == "mixer":
                break
            ffn(l, 1, subs)
        allxf = [("xf", c, s) for c in range(8) for s in range(2)]
        if g == 0:
            sp_dma(yT[:, :, 0:G - 16], xf[:, :, 16:G], allxf, [], base="yo", nsem=1)
        else:
            sp_dma(yT[:, :, g0 - 16:g0 - 16 + G], xf[:, :, 0:G], allxf, [], base="yo", nsem=1)
    if DEBUG["stop"] is None:
      sp_dma(cso[:, 0:NL], CSO[:, 0:NL], [("CSO", l) for l in range(NL)], [], base="yo", nsem=1)
      sp_dma(cpo[:, 0:NL], CPO[:, 0:NL], [("CPO", l) for l in range(NL)], [], base="yo", nsem=1)

    cum = {}
    for en in Tr.ENGS:
        c = 0
        arr = []
        for o in tr.ops[en]:
            if o["signal"]:
                c += 1
            arr.append(c)
        cum[en] = arr
    dsems = sorted(tr.dval.keys())
    semctx = {}
    for en in Tr.ENGS:
        cm = nc.semaphore("e_" + en)
        semctx[("e", en)] = cm.__enter__()
        ctxs.append(cm)
    for ds in dsems:
        cm = nc.semaphore("d_" + ds)
        semctx[("d", ds)] = cm.__enter__()
        ctxs.append(cm)

    def replay(en, e):
        my = semctx[("e", en)]
        for o in tr.ops[en]:
            for (k, v) in o["waits"]:
                if k[0] == "e":
                    e.wait_ge(semctx[k], cum[k[1]][v])
                else:
                    e.wait_ge(semctx[k], v)
            inss = o["fn"](e)
            if o["dma"] is not None:
                for ins in inss:
                    ins.then_inc(semctx[("d", o["dma"])], 16)
            elif o["signal"]:
                inss[-1].then_inc(my, 1)
        if en == "sp":
            for ds in dsems:
                e.wait_ge(semctx[("d", ds)], tr.dval[ds])

    with nc.Block() as block:
        @block.tensor
        def _(e):
            replay("pe", e)

        @block.scalar
        def _(e):
            replay("act", e)

        @block.vector
        def _(e):
            replay("dve", e)

        @block.gpsimd
        def _(e):
            replay("pool", e)

        @block.sync
        def _(e):
            replay("sp", e)
    for cm in reversed(ctxs):
        cm.__exit__(None, None, None)
    stats = {en: len(tr.ops[en]) for en in Tr.ENGS}
    return nc, stats


def host_consts():
    ident = np.eye(128, dtype=np.float32)
    s_ = np.arange(64)[:, None]
    t_ = np.arange(64)[None, :]
    mc = np.where(s_ <= t_, -1.0, 0.0).astype(np.float32)
    ms = np.where((s_ <= t_) & (s_ // 4 == t_ // 4), -1.0, 0.0).astype(np.float32)
    cmask = np.zeros((64, 2, 512), np.float32)
    cmask[:, 0, :] = np.tile(mc, (1, 8))
    cmask[:, 1, :] = np.tile(ms, (1, 8))
    crow = (np.arange(64)[:, None] // 4 == np.arange(16)[None, :]).astype(np.float32)
    E = np.zeros((3, GT), np.float32)
    E[0, 0] = 1.0
    E[0, 16::64] = 1.0
    E[1, 0:704:64] = 1.0
    E[2, 0:640:64] = 1.0
    E[2, 640:704:4] = 1.0
    cE = np.ascontiguousarray(np.broadcast_to(E[None], (128, 3, GT)))
    return ident, cmask, crow, cE


_CACHE = {}


def kernel(x_prompt, x_sample, state_hgrn, state_conv, meta_tokens, w_in, hgrn_lb_logits,
           hgrn_norm_w, conv_w, w_branch_a, w_branch_b, w_out, ffn1_w_gu, ffn1_w_down,
           ffn2_w_gu, ffn2_w_down, ln_g, ln_b):
    f32 = np.float32
    x_prompt = np.asarray(x_prompt, f32)
    x_sample = np.asarray(x_sample, f32)
    state_hgrn = np.asarray(state_hgrn, f32)
    state_conv = np.asarray(state_conv, f32)
    meta = np.asarray(meta_tokens, f32)
    nc, stats = build_program()
    ident, cmask, crow, cE = host_consts()

    def fm(a):
        a = np.asarray(a, f32)
        sh = a.shape[:-1]
        return np.ascontiguousarray(np.moveaxis(a.reshape(sh + (8, 128)), -1, 0))

    def dn_layout(w):
        w = np.asarray(w, f32).reshape(4, NJ, 128, 8, 128)
        return np.ascontiguousarray(w.transpose(0, 3, 2, 1, 4))

    shared = {
        "w_in": np.ascontiguousarray(np.asarray(w_in, f32)),
        "w_a": np.ascontiguousarray(np.asarray(w_branch_a, f32)),
        "w_b": np.ascontiguousarray(np.asarray(w_branch_b, f32)),
        "w_o": np.ascontiguousarray(np.asarray(w_out, f32)),
        "f1gu": np.ascontiguousarray(np.asarray(ffn1_w_gu, f32)),
        "f2gu": np.ascontiguousarray(np.asarray(ffn2_w_gu, f32)),
        "f1d": dn_layout(ffn1_w_down),
        "f2d": dn_layout(ffn2_w_down),
        "plng": fm(ln_g), "plnb": fm(ln_b), "plog": fm(hgrn_lb_logits), "pconv": fm(conv_w),
        "pnorm": np.ascontiguousarray(np.asarray(hgrn_norm_w, f32).T),
        "cident": ident, "cmask": cmask, "crow": crow, "cE": cE,
    }
    in_maps = []
    for c in range(8):
        toks = np.concatenate([meta, x_prompt[c], x_sample[16 * c:16 * c + 16].reshape(64, D)], axis=0)
        xT = np.ascontiguousarray(toks.reshape(NTOK, 8, 128).transpose(2, 1, 0))
        m = dict(shared)
        m["xT"] = xT
        m["s0"] = np.ascontiguousarray(state_hgrn[:, 16 * c:16 * c + 16])
        sc = state_conv[:, 16 * c:16 * c + 16]
        m["cb"] = np.ascontiguousarray(sc.reshape(4, 16, 2, 8, 128).transpose(4, 0, 3, 1, 2))
        in_maps.append(m)
    res = run_bass_kernel_spmd(nc, in_maps, core_ids=list(range(8)))
    y_prompt = np.zeros((8, 2048, D), f32)
    y_sample = np.zeros((128, 4, D), f32)
    nhp = np.zeros((4, 8, 8, 128, 128), f32)
    ncp = np.zeros((4, 8, 2, 1024), f32)
    nhs = np.zeros((4, 128, 8, 128, 128), f32)
    ncs = np.zeros((4, 128, 2, 1024), f32)
    for c in range(8):
        r = res.results[c]
        y = np.asarray(r["yT"]).transpose(2, 1, 0).reshape(NTOK - 16, D)
        y_prompt[c] = y[0:2048]
        y_sample[16 * c:16 * c + 16] = y[2048:].reshape(16, 4, D)
        nhp[:, c] = np.asarray(r["hp"])
        nhs[:, 16 * c:16 * c + 16] = np.asarray(r["hs"])
        cso = np.asarray(r["cso"])
        ncs[:, 16 * c:16 * c + 16] = cso.transpose(1, 3, 4, 2, 0).reshape(4, 16, 2, 1024)
        cpo = np.asarray(r["cpo"])
        ncp[:, c] = cpo.transpose(1, 3, 2, 0).reshape(4, 2, 1024)
    return (y_prompt, y_sample, nhp, ncp, nhs, ncs)
```

```python
import numpy as np
import concourse.bass as bass
import concourse.mybir as mybir
from concourse.bass_utils import run_bass_kernel_spmd

F32 = mybir.dt.float32
BF16 = mybir.dt.bfloat16
F32R = mybir.dt.float32r
AF = mybir.ActivationFunctionType
ALU = mybir.AluOpType

D = 1024
DFF = 2816
DEPTH = 4
NCK = 8
NJ = 22
NTOK = 2128
GROUPS = [(0, 720), (720, 704), (1424, 704)]
GT = 720
ALPHA = (2 * DEPTH) ** 0.25
CR = 0.5 / ALPHA
LN_EPS_P = 1e-5 / (ALPHA * ALPHA)
RMS_EPS = 1e-6
SLOT = 4096
NSLOT = 3
UBYTES = 90112

DEBUG = {"layers": DEPTH, "groups": 3, "stop": None}


class Tr:
    ENGS = ("pe", "act", "dve", "pool", "sp")

    def __init__(self):
        self.ops = {n: [] for n in self.ENGS}
        self.waited = {n: {} for n in self.ENGS}
        self.cells = {}
        self.dval = {}
        self.live = []
        self.ninst = 0

    def cell(self, key):
        c = self.cells.get(key)
        if c is None:
            pend = {}
            if isinstance(key, tuple) and isinstance(key[0], UInst):
                assert key[0].alive, ("access to retired buffer", key[0].name)
                pend = dict(key[0].pending)
            c = [{}, pend]
            self.cells[key] = c
        return c

    def op(self, eng, fn, reads=(), writes=(), dma_sem=None, ndma=1):
        deps = {}

        def merge(d):
            for k, v in d.items():
                if deps.get(k, -1) < v:
                    deps[k] = v
        for c in reads:
            merge(self.cell(c)[0])
        for c in writes:
            cc = self.cell(c)
            merge(cc[0])
            merge(cc[1])
        if dma_sem is not None:
            prev = self.dval.get(dma_sem, 0)
            if prev > 0:
                deps[("d", dma_sem)] = max(deps.get(("d", dma_sem), -1), prev)
        waits = []
        wd = self.waited[eng]
        for k, v in deps.items():
            if k == ("e", "pe") and eng == "pe":
                continue
            if wd.get(k, -1) >= v:
                continue
            wd[k] = v
            waits.append((k, v))
            if k[0] == "e":
                self.ops[k[1]][v]["signal"] = True
        idx = len(self.ops[eng])
        if dma_sem is not None:
            self.dval[dma_sem] = self.dval.get(dma_sem, 0) + 16 * ndma
            ev = (("d", dma_sem), self.dval[dma_sem])
        else:
            ev = (("e", eng), idx)
        self.ops[eng].append({"fn": fn, "waits": waits, "signal": False, "dma": dma_sem})
        for c in reads:
            r = self.cell(c)[1]
            if r.get(ev[0], -1) < ev[1]:
                r[ev[0]] = ev[1]
        for c in writes:
            self.cells[c] = [{ev[0]: ev[1]}, {}]
        return ev


class UInst:
    def __init__(self, tr, name, off, nbytes):
        self.name, self.off, self.nbytes = name, off, nbytes
        self.alive = True
        self.pending = {}
        keep = []
        for o in tr.live:
            if o.off < off + nbytes and off < o.off + o.nbytes:
                if o.alive:
                    o.alive = False
                    for k, c in list(tr.cells.items()):
                        if isinstance(k, tuple) and k[0] is o:
                            for d in (c[0], c[1]):
                                for kk, vv in d.items():
                                    if o.pending.get(kk, -1) < vv:
                                        o.pending[kk] = vv
                            del tr.cells[k]
                for kk, vv in o.pending.items():
                    if self.pending.get(kk, -1) < vv:
                        self.pending[kk] = vv
                if not (off <= o.off and o.off + o.nbytes <= off + nbytes):
                    keep.append(o)
            else:
                keep.append(o)
        keep.append(self)
        tr.live = keep


def group_chunks(g):
    if g == 0:
        return [(0, 16)] + [(16 + 64 * i, 64) for i in range(11)]
    if g == 1:
        return [(64 * i, 64) for i in range(11)]
    return [(64 * i, 64) for i in range(10)]


def build_program():
    nc = bass.Bass("TRN2", target_bir_lowering=False)
    NL = DEBUG["layers"]
    NG = DEBUG["groups"]

    def din(name, shape):
        return nc.dram_tensor(name, shape, F32, kind="ExternalInput").ap()

    def dout(name, shape):
        return nc.dram_tensor(name, shape, F32, kind="ExternalOutput").ap()

    xT = din("xT", [128, 8, NTOK])
    s0 = din("s0", [4, 16, 8, 128, 128])
    cb = din("cb", [128, 4, 8, 16, 2])
    w_in = din("w_in", [4, D, 9 * D])
    w_a = din("w_a", [4, D, D])
    w_b = din("w_b", [4, D, D])
    w_o = din("w_o", [4, D, D])
    fgu = [din("f1gu", [4, D, 2 * DFF]), din("f2gu", [4, D, 2 * DFF])]
    fdn = [din("f1d", [4, 8, 128, NJ, 128]), din("f2d", [4, 8, 128, NJ, 128])]
    plng = din("plng", [128, 4, 3, 8])
    plnb = din("plnb", [128, 4, 3, 8])
    plog = din("plog", [128, 4, 8])
    pconv = din("pconv", [128, 4, 3, 8])
    pnorm = din("pnorm", [128, 4])
    cident = din("cident", [128, 128])
    cmask = din("cmask", [64, 2, 512])
    crow = din("crow", [64, 16])
    cE = din("cE", [128, 3, GT])

    yT = dout("yT", [128, 8, NTOK - 16])
    hs = dout("hs", [4, 16, 8, 128, 128])
    hp = dout("hp", [4, 8, 128, 128])
    cso = dout("cso", [128, 4, 8, 16, 2])
    cpo = dout("cpo", [128, 4, 8, 2])

    tr = Tr()
    sb = {}
    ctxs = []

    def salloc(name, shape, dt):
        cm = nc.sbuf_tensor(name, shape, dt)
        t = cm.__enter__()
        ctxs.append(cm)
        sb[name] = t
        return t

    xf = salloc("xf", [128, 8, GT], F32)
    xb = salloc("xb", [128, 8, GT], BF16)
    S = salloc("S", [128, 8, 128], F32)
    Sb = salloc("Sb", [128, 8, 128], BF16)
    W = salloc("W", [128, NSLOT, SLOT], BF16)
    U = salloc("U", [128, UBYTES // 4], F32)
    identf = salloc("identf", [128, 128], F32)
    identb = salloc("identb", [128, 128], BF16)
    ones1k = salloc("ones1k", [128, 128], BF16)
    ones128 = salloc("ones128", [128, 128], BF16)
    zerob = salloc("zerob", [128, 128], BF16)
    lng = salloc("lng", [128, 4, 3, 8], F32)
    lnb = salloc("lnb", [128, 4, 3, 8], F32)
    lgt = salloc("lgt", [128, 4, 8], F32)
    lbt = salloc("lbt", [128, 4, 8], F32)
    c0t = salloc("c0t", [128, 4, 8], F32)
    c1t = salloc("c1t", [128, 4, 8], F32)
    sm8 = salloc("sm8", [128, 3, 8], F32)
    convw = salloc("convw", [128, 4, 3, 8], F32)
    normw = salloc("normw", [128, 4], F32)
    maskc = salloc("maskc", [64, 2, 512], F32)
    rowm = salloc("rowm", [64, 16], F32)
    Et = salloc("Et", [128, GT], F32)
    nEt = salloc("nEt", [128, GT], F32)
    ubuf = salloc("ubuf", [128, 4, 8, 2], F32)
    DEC = salloc("DEC", [128, 8, 12], F32)
    DECS = salloc("DECS", [128, 8, 16], F32)
    CB = salloc("CB", [128, 4, 8, 16, 2], F32)
    US = salloc("US", [128, 8, 16, 6], F32)
    CVS = salloc("CVS", [128, 8, 16, 4], F32)
    CVS2 = salloc("CVS2", [128, 8, 16, 4], F32)
    QTF = salloc("QTF", [128, 8, 64], F32)
    epsl = salloc("epsl", [128, 1], F32)
    epsr = salloc("epsr", [128, 1], F32)
    CSO = salloc("CSO", [128, 4, 8, 16, 2], F32)
    CPO = salloc("CPO", [128, 4, 8, 2], F32)

    pst = []
    for i in range(4):
        cm = nc.psum_tensor("ps%d" % i, [128, 1024], F32)
        pst.append(cm.__enter__())
        ctxs.append(cm)

    def bank(b, n=512):
        return pst[b // 2][:, (b % 2) * 512:(b % 2) * 512 + n]

    def bank_bf(b):
        return pst[b // 2][:, (b % 2) * 512:(b % 2) * 512 + 512].bitcast(BF16)

    bank_rr = [0]
    held = set()

    def alloc_bank():
        while True:
            b = bank_rr[0]
            bank_rr[0] = (b + 1) % 8
            if b not in held:
                return b

    def alloc_bank2():
        while True:
            if bank_rr[0] % 2:
                bank_rr[0] = (bank_rr[0] + 1) % 8
            b = bank_rr[0]
            bank_rr[0] = (b + 2) % 8
            if b not in held and (b + 1) not in held:
                return b

    def uview(off, dt, shape):
        esz = 2 if dt == BF16 else 4
        n = int(np.prod(shape))
        assert off % 4 == 0 and off + n * esz <= UBYTES, (off, shape)
        v = U[:, off // 4: off // 4 + (n * esz) // 4]
        if dt != F32:
            v = v.bitcast(dt)
        if len(shape) == 2:
            return v.rearrange("p (a b) -> p a b", a=shape[0], b=shape[1])
        if len(shape) == 3:
            return v.rearrange("p (a b c) -> p a b c", a=shape[0], b=shape[1], c=shape[2])
        return v

    class UB:
        def __init__(self, name, off, dt, shape):
            esz = 2 if dt == BF16 else 4
            self.inst = UInst(tr, name, off, int(np.prod(shape)) * esz)
            self.v = uview(off, dt, shape)

        def c(self, *key):
            return (self.inst,) + key

    wstate = {"n": 0}

    def load_block(parts, nk):
        i = wstate["n"]
        wstate["n"] += 1
        slot = i % NSLOT
        ncols = sum(p[2] for p in parts)
        assert nk * ncols <= SLOT
        wv = W[:, slot, 0:nk * ncols].rearrange("p (k c) -> p k c", k=nk, c=ncols)

        def fn(e, parts=parts, wv=wv):
            out = []
            for (src, co, ncp) in parts:
                sv = src if len(src.shape) == 3 else src.rearrange("(k p) c -> p k c", p=128)
                out.append(e.dma_start(out=wv[:, :, co:co + ncp], in_=sv))
            return out
        tr.op("pool", fn, reads=(), writes=[("w", slot)], dma_sem="w%d" % slot, ndma=len(parts))
        return slot, wv

    pending_blocks = []

    def block_sched2():
        for g in range(NG):
            for l in range(NL):
                for phase in ("ffn1", "mixer", "ffn2"):
                    if phase == "mixer":
                        wl = w_in[l]

                        def seg(sg, hh):
                            return [(wl[:, 1024 * sg + 512 * hh: 1024 * sg + 512 * hh + 512], 0, 512)]
                        for hh in (0, 1):
                            yield (seg(1, hh), 8)
                            yield (seg(2, hh), 8)
                            yield (seg(0, hh), 8)
                        for hh in (0, 1):
                            yield (seg(3, hh), 8)
                        for hh in (0, 1):
                            yield (seg(7, hh), 8)
                        for hh in (0, 1):
                            yield ([(w_a[l][:, 512 * hh: 512 * hh + 512], 0, 512)], 8)
                        for j in range(4):
                            yield ([(wl[:, 5120 + 256 * j: 5120 + 256 * j + 256], 0, 256),
                                    (wl[:, 6144 + 256 * j: 6144 + 256 * j + 256], 256, 256)], 8)
                        for hh in (0, 1):
                            yield (seg(8, hh), 8)
                        for hh in (0, 1):
                            yield (seg(4, hh), 8)
                        for hh in (0, 1):
                            yield ([(w_b[l][:, 512 * hh: 512 * hh + 512], 0, 512)], 8)
                        for hh in (0, 1):
                            yield ([(w_o[l][:, 512 * hh: 512 * hh + 512], 0, 512)], 8)
                    else:
                        fi = 0 if phase == "ffn1" else 1
                        gu = fgu[fi][l]
                        dn = fdn[fi][l]
                        for j in range(11):
                            yield ([(gu[:, 256 * j: 256 * j + 256], 0, 256),
                                    (gu[:, DFF + 256 * j: DFF + 256 * j + 256], 256, 256)], 8)
                        for rep in range(2):
                            for c in range(8):
                                yield ([(dn[c], 0, 128)], NJ)

    sched = block_sched2()
    prefetched = []

    def prefetch(maxn=99):
        k = 0
        while len(prefetched) < NSLOT and k < maxn:
            k += 1
            try:
                parts, nk = next(sched)
            except StopIteration:
                return
            prefetched.append(load_block(parts, nk))

    def next_block():
        if not prefetched:
            prefetch()
        return prefetched.pop(0)

    def mm_group(out_ap, pairs, reads, writes, first_start=True):
        def fn(e, out_ap=out_ap, pairs=pairs, first_start=first_start):
            ins = None
            n = len(pairs)
            for i, (l_, r_) in enumerate(pairs):
                ins = e.matmul(out_ap, lhsT=l_, rhs=r_, start=(first_start and i == 0), stop=(i == n - 1))
            return [ins]
        tr.op("pe", fn, reads=reads, writes=writes)

    def act(out, in_, func, reads, writes, bias=None, scale=None):
        kw = {}
        if bias is not None:
            kw["bias"] = bias
        if scale is not None:
            kw["scale"] = scale

        def fn(e):
            return [e.activation(out=out, in_=in_, func=func, **kw)]
        tr.op("act", fn, reads=reads, writes=writes)

    def dve(kind, reads, writes, eng="dve", **kw):
        def fn(e):
            return [getattr(e, kind)(**kw)]
        tr.op(eng, fn, reads=reads, writes=writes)

    def dma(eng, out, in_, reads, writes, sem):
        def fn(e):
            return [e.dma_start(out=out, in_=in_)]
        tr.op(eng, fn, reads=reads, writes=writes, dma_sem=sem)

    spsem = [0]

    def sp_dma(out, in_, reads, writes, nsem=4, base="io", sem=None):
        s = sem if sem is not None else "%s%d" % (base, spsem[0] % nsem)
        spsem[0] += 1
        dma("sp", out, in_, reads, writes, s)

    def load_const(t, src, key):
        sp_dma(t[:] if not isinstance(t, bass.AP) else t, src, [], [key])

    load_const(identf, cident, "identf")
    load_const(lng, plng, "lng")
    load_const(lnb, plnb, "lnb")
    load_const(lgt, plog, "lgt")
    load_const(convw, pconv, "convw")
    load_const(normw, pnorm, "normw")
    load_const(maskc, cmask, "maskc")
    load_const(rowm, crow, "rowm")
    load_const(CB, cb, "CB")
    dve("tensor_copy", ["identf"], ["identb"], out=identb[:], in_=identf[:])
    dve("memset", [], ["ones1k"], ap=ones1k[:], constant=1.0 / 1024.0)
    dve("memset", [], ["ones128"], ap=ones128[:], constant=1.0 / 128.0)
    dve("memset", [], ["zerob"], ap=zerob[:], constant=0.0)
    dve("memset", [], ["epsl"], ap=epsl[:], constant=LN_EPS_P)
    dve("memset", [], ["epsr"], ap=epsr[:], constant=RMS_EPS)
    dve("memset", [], ["ubuf"], ap=ubuf[:], constant=0.0)
    dve("tensor_tensor", ["lgt"], ["sm8"], out=sm8[:, 0, :], in0=lgt[:, 0, :], in1=lgt[:, 1, :], op=ALU.max)
    dve("tensor_tensor", ["lgt", "sm8"], ["sm8"], out=sm8[:, 0, :], in0=sm8[:, 0, :], in1=lgt[:, 2, :], op=ALU.max)
    dve("tensor_tensor", ["lgt", "sm8"], ["sm8"], out=sm8[:, 0, :], in0=sm8[:, 0, :], in1=lgt[:, 3, :], op=ALU.max)
    dve("tensor_tensor", ["lgt", "sm8"], ["lbt"], out=lbt[:], in0=lgt[:],
        in1=sm8[:, 0:1, :].broadcast_to([128, 4, 8]), op=ALU.subtract)
    act(lbt[:], lbt[:], AF.Exp, ["lbt"], ["lbt"])
    dve("tensor_tensor", ["lbt"], ["sm8"], out=sm8[:, 1, :], in0=lbt[:, 0, :], in1=lbt[:, 1, :], op=ALU.add)
    dve("tensor_tensor", ["lbt", "sm8"], ["sm8"], out=sm8[:, 1, :], in0=sm8[:, 1, :], in1=lbt[:, 2, :], op=ALU.add)
    dve("tensor_tensor", ["lbt", "sm8"], ["sm8"], out=sm8[:, 1, :], in0=sm8[:, 1, :], in1=lbt[:, 3, :], op=ALU.add)
    dve("reciprocal", ["sm8"], ["sm8"], out=sm8[:, 2, :], in_=sm8[:, 1, :])
    dve("tensor_tensor", ["lbt", "sm8"], ["lgt"], out=lgt[:], in0=lbt[:],
        in1=sm8[:, 2:3, :].broadcast_to([128, 4, 8]), op=ALU.mult)
    dve("memset", [], ["lbt"], ap=lbt[:, 0, :], constant=0.0)
    dve("tensor_copy", ["lgt", "lbt"], ["lbt"], out=lbt[:, 1, :], in_=lgt[:, 1, :])
    dve("tensor_tensor", ["lgt", "lbt"], ["lbt"], out=lbt[:, 2, :], in0=lbt[:, 1, :], in1=lgt[:, 2, :], op=ALU.add)
    dve("tensor_tensor", ["lgt", "lbt"], ["lbt"], out=lbt[:, 3, :], in0=lbt[:, 2, :], in1=lgt[:, 3, :], op=ALU.add)
    dve("tensor_scalar", ["lbt"], ["c1t"], out=c1t[:], in0=lbt[:], scalar1=-0.5, scalar2=0.5,
        op0=ALU.mult, op1=ALU.add)
    dve("tensor_tensor", ["lbt", "c1t"], ["c0t"], out=c0t[:], in0=lbt[:], in1=c1t[:], op=ALU.add)

    prefetch()

    def xcells(name, s):
        return [(name, c, s) for c in range(8)]

    def ln_sub(l, k, subs, s, off_sq, off_yt, off_st):
        nmax = max(n for _, n in subs)
        t0, n = subs[s]
        SQ = UB("SQ", off_sq, BF16, [16, nmax])
        YT = UB("YT", off_yt, F32, [8, nmax])
        ST = UB("ST", off_st, F32, [4, nmax])
        dve("tensor_copy", xcells("xf", s), [SQ.c(0)], out=SQ.v[:, 0:8, 0:n], in_=xf[:, :, t0:t0 + n])
        act(SQ.v[:, 8:16, 0:n], xf[:, :, t0:t0 + n], AF.Square, xcells("xf", s), [SQ.c(1)])
        bm = alloc_bank()
        be = alloc_bank()
        mm_group(bank(bm, n), [(ones1k[:], SQ.v[:, c, 0:n]) for c in range(8)],
                 ["ones1k", SQ.c(0)], [("ps", bm)])
        mm_group(bank(be, n), [(ones1k[:], SQ.v[:, 8 + c, 0:n]) for c in range(8)],
                 ["ones1k", SQ.c(1)], [("ps", be)])
        act(ST.v[:, 0, 0:n], bank(bm, n), AF.Identity, [("ps", bm)], [ST.c(0)])
        act(ST.v[:, 1, 0:n], bank(bm, n), AF.Square, [("ps", bm)], [ST.c(1)])
        dve("tensor_tensor", [("ps", be), ST.c(1)], [ST.c(2)], out=ST.v[:, 2, 0:n], in0=bank(be, n),
            in1=ST.v[:, 1, 0:n], op=ALU.subtract)
        act(ST.v[:, 2, 0:n], ST.v[:, 2, 0:n], AF.Sqrt, [ST.c(2), "epsl"], [ST.c(2)], bias=epsl[:], scale=1.0)
        dve("tensor_tensor", xcells("xf", s) + [ST.c(0)], [YT.c(0)], out=YT.v[:, :, 0:n],
            in0=xf[:, :, t0:t0 + n], in1=ST.v[:, 0:1, 0:n].broadcast_to([128, 8, n]), op=ALU.subtract)
        dve("reciprocal", [ST.c(2)], [ST.c(3)], out=ST.v[:, 3, 0:n], in_=ST.v[:, 2, 0:n])
        dve("tensor_tensor", [YT.c(0), ST.c(3)], [YT.c(0)], out=YT.v[:, :, 0:n],
            in0=YT.v[:, :, 0:n], in1=ST.v[:, 3:4, 0:n].broadcast_to([128, 8, n]), op=ALU.mult)
        for c in range(8):
            act(xb[:, c, t0:t0 + n], YT.v[:, c, 0:n], AF.Identity, [YT.c(0), "lng", "lnb"], [("xb", c, s)],
                scale=lng[:, l, k, c:c + 1], bias=lnb[:, l, k, c:c + 1])
        for c in range(8):
            if c % 4 == 0:
                act(xf[:, c, t0:t0 + n], YT.v[:, c, 0:n], AF.Identity, [YT.c(0), "lng", "lnb"], [("xf", c, s)],
                    scale=lng[:, l, k, c:c + 1], bias=lnb[:, l, k, c:c + 1])
            else:
                dve("tensor_scalar", [YT.c(0), "lng", "lnb"], [("xf", c, s)], out=xf[:, c, t0:t0 + n],
                    in0=YT.v[:, c, 0:n], scalar1=lng[:, l, k, c:c + 1], scalar2=lnb[:, l, k, c:c + 1],
                    op0=ALU.mult, op1=ALU.add)

    def ffn(l, fi, subs):
        H = UB("H", 0, BF16, [NJ, GT])
        SG = UB("SG", 60480, BF16, [4, 360])
        sgi = [0]

        def gu_groups(j, slot, wv, s):
            t0, n = subs[s]
            for jj in range(2):
                bg = alloc_bank()
                bu = alloc_bank()
                mm_group(bank(bg, n), [(wv[:, k, jj * 128:(jj + 1) * 128], xb[:, k, t0:t0 + n]) for k in range(8)],
                         [("w", slot)] + xcells("xb", s), [("ps", bg)])
                mm_group(bank(bu, n), [(wv[:, k, 256 + jj * 128:256 + (jj + 1) * 128], xb[:, k, t0:t0 + n])
                                       for k in range(8)],
                         [("w", slot)] + xcells("xb", s), [("ps", bu)])
                sgv = SG.v[:, sgi[0] % 4, 0:n]
                sgc = SG.c(sgi[0] % 4)
                sgi[0] += 1
                act(sgv, bank(bg, n), AF.Silu, [("ps", bg)], [sgc])
                dve("tensor_tensor", [sgc, ("ps", bu)], [H.c(2 * j + jj, s)], out=H.v[:, 2 * j + jj, t0:t0 + n],
                    in0=bank(bu, n), in1=sgv, op=ALU.mult)
        sl0, wv0 = next_block()
        sl1, wv1 = next_block()
        gu_groups(0, sl0, wv0, 0)
        gu_groups(1, sl1, wv1, 0)
        gu_groups(0, sl0, wv0, 1)
        prefetch(1)
        gu_groups(1, sl1, wv1, 1)
        prefetch(1)
        for j in range(2, 11):
            slot, wv = next_block()
            for s in range(2):
                gu_groups(j, slot, wv, s)
            prefetch()
        kk = 0 if fi == 0 else 2
        for s, (t0, n) in enumerate(subs):
            for c in range(8):
                slot, wv = next_block()
                b = alloc_bank()
                mm_group(bank(b, n), [(wv[:, k, 0:128], H.v[:, k, t0:t0 + n]) for k in range(NJ)],
                         [("w", slot)] + [H.c(k, s) for k in range(NJ)], [("ps", b)])
                dve("scalar_tensor_tensor", [("ps", b), ("xf", c, s)], [("xf", c, s)], out=xf[:, c, t0:t0 + n],
                    in0=bank(b, n), scalar=CR, in1=xf[:, c, t0:t0 + n], op0=ALU.mult, op1=ALU.add)
                prefetch()
                if s == 1 and c == 1:
                    ln_sub(l, kk, subs, 0, 31680, 43200, 54720)
        ln_sub(l, kk, subs, 1, 31680, 43200, 54720)

    def dense8(out_chunks, subs, consumer):
        slot, wv = next_block()
        for s, (t0, n) in enumerate(subs):
            for i, co in enumerate(out_chunks):
                b = alloc_bank()
                mm_group(bank(b, n), [(wv[:, k, co:co + 128], xb[:, k, t0:t0 + n]) for k in range(8)],
                         [("w", slot)] + xcells("xb", s), [("ps", b)])
                consumer(i, s, t0, n, b)
        prefetch()

    def dense8_pair(subs, consA, consB):
        slA, wvA = next_block()
        slB, wvB = next_block()
        for s, (sl, wv, cons) in ((0, (slA, wvA, consA)), (0, (slB, wvB, consB)), (1, (slA, wvA, consA)),
                                  (1, (slB, wvB, consB))):
            t0, n = subs[s]
            for i in range(4):
                b = alloc_bank()
                mm_group(bank(b, n), [(wv[:, k, i * 128:(i + 1) * 128], xb[:, k, t0:t0 + n]) for k in range(8)],
                         [("w", sl)] + xcells("xb", s), [("ps", b)])
                cons(i, s, t0, n, b)
            if s == 1:
                prefetch(1)

    def dense8_from(src, srccells, subs, consumer):
        slot, wv = next_block()
        for s, (t0, n) in enumerate(subs):
            for i in range(4):
                b = alloc_bank()
                mm_group(bank(b, n), [(wv[:, k, i * 128:(i + 1) * 128], src.v[:, k, t0:t0 + n]) for k in range(8)],
                         [("w", slot)] + [srccells(k, s) for k in range(8)], [("ps", b)])
                consumer(i, s, t0, n, b)
        prefetch()

    def mixer(g, l, subs):
        G = GROUPS[g][1]
        chunks = group_chunks(g)
        has_sample = (g == 2)
        nch = len(chunks)
        Fb = UB("F", 0, F32, [4, GT])
        Pb = UB("P", 11520, F32, [4, GT])
        QT = UB("QT", 23040, BF16, [8, GT])
        NK = UB("NK", 34560, BF16, [8, GT])
        VT = UB("VT", 46080, BF16, [8, GT])
        A_ = UB("A", 57600, F32, [2, GT])
        B_ = UB("B", 63360, F32, [2, GT])
        R_ = UB("R", 69120, F32, [2, GT])
        QS = UB("QS", 74880, F32, [2, 360])
        if g == 0:
            dve("memset", [], ["S"], ap=S[:], constant=0.0)
        else:
            sp_dma(S[:], hp[l].rearrange("h d v -> d h v"), [("hp", l)], ["S"], base="st", nsem=2)
        act(Sb[:], S[:], AF.Copy, ["S"], ["Sb"])
        qsi = [0]
        vtog = [0]
        for hh in (0, 1):
            def cons_f(i, s, t0, n, b, hh=hh):
                h = 4 * hh + i
                act(Fb.v[:, i, t0:t0 + n], bank(b, n), AF.Tanh, [("ps", b)], [Fb.c(i, s)], scale=0.5)
                act(Fb.v[:, i, t0:t0 + n], Fb.v[:, i, t0:t0 + n], AF.Identity, [Fb.c(i, s), "c0t", "c1t"], [Fb.c(i, s)],
                    scale=c1t[:, l, h:h + 1], bias=c0t[:, l, h:h + 1])

            def cons_v(i, s, t0, n, b, hh=hh):
                h = 4 * hh + i
                act(VT.v[:, h, t0:t0 + n], bank(b, n), AF.Copy, [("ps", b)], [VT.c(h, s)])
            if hh == 0:
                dense8_pair(subs, cons_f, cons_v)
            else:
                dense8([0, 128, 256, 384], subs, cons_f)
            for i0 in (0, 2):
                pair = (i0, i0 + 1)

                def fcs(i):
                    return [Fb.c(i, 0), Fb.c(i, 1)]
                for i in pair:
                    dve("tensor_tensor", fcs(i) + ["nEt"], [A_.c(i % 2)], out=A_.v[:, i % 2, 0:G], in0=Fb.v[:, i, 0:G],
                        in1=nEt[:, 0:G], op=ALU.mult)
                    dve("tensor_tensor", fcs(i) + ["Et"], [B_.c(i % 2)], out=B_.v[:, i % 2, 0:G], in0=Fb.v[:, i, 0:G],
                        in1=Et[:, 0:G], op=ALU.mult)
                for i in pair:
                    dve("tensor_tensor_scan", [A_.c(i % 2), B_.c(i % 2)], [Pb.c(i)], out=Pb.v[:, i, 0:G],
                        data0=A_.v[:, i % 2, 0:G], data1=B_.v[:, i % 2, 0:G], initial=0.0, op0=ALU.mult, op1=ALU.add)
                for i in pair:
                    dve("reciprocal", [Pb.c(i)], [R_.c(i % 2)], out=R_.v[:, i % 2, 0:G], in_=Pb.v[:, i, 0:G])
                for i in pair:
                    h = 4 * hh + i
                    dve("scalar_tensor_tensor", fcs(i) + [R_.c(i % 2)], [NK.c(h)], out=NK.v[:, h, 0:G],
                        in0=Fb.v[:, i, 0:G], scalar=1.0, in1=R_.v[:, i % 2, 0:G], op0=ALU.subtract, op1=ALU.mult)
                for i in pair:
                    h = 4 * hh + i
                    e0 = chunks[0][1] - 1
                    act(DEC[:, h, 0:nch], Pb.v[:, i, e0:e0 + 64 * (nch - 1) + 1:64], AF.Copy, [Pb.c(i)], [("DEC", h)])
                    if has_sample:
                        act(DECS[:, h, :], Pb.v[:, i, 643:704:4], AF.Copy, [Pb.c(i)], [("DECS", h)])
            if hh == 1:
                dense8([0, 128, 256, 384], subs, cons_v)

            def cons_q(i, s, t0, n, b, hh=hh):
                h = 4 * hh + i
                qv = QS.v[:, qsi[0] % 2, 0:n]
                qc = QS.c(qsi[0] % 2)
                qsi[0] += 1
                act(qv, bank(b, n), AF.Silu, [("ps", b)], [qc])
                dve("tensor_tensor", [qc, Pb.c(i)], [QT.c(h, s)], out=QT.v[:, h, t0:t0 + n], in0=qv,
                    in1=Pb.v[:, i, t0:t0 + n], op=ALU.mult)
                if has_sample and s == 1:
                    dve("tensor_tensor", [qc, Pb.c(i)], [("QTF", h)], out=QTF[:, h, :], in0=qv[:, n - 64:n],
                        in1=Pb.v[:, i, 640:704], op=ALU.mult)
            dense8([0, 128, 256, 384], subs, cons_q)
        OT = UB("OT", 0, F32, [8, GT])
        SC = UB("SC", 57600, BF16, [2, 512])
        KT = UB("KT", 59648, BF16, [2, 1024])
        VTT = UB("VTT", 63744, BF16, [2, 1024])
        KH = UB("KH", 67840, BF16, [2, 512])
        n2 = subs[0][1]

        def scell(buf, t0, L):
            ss = set()
            for (s, (a, n)) in enumerate(subs):
                if t0 < a + n and a < t0 + L:
                    ss.add(s)
            return [buf.c(h, s) for h in range(8) for s in ss]

        def chunk_common(ci, t0, L, mask_idx, dec_ap, dec_cells, par):
            qc = scell(QT, t0, L)
            kc = [NK.c(h) for h in range(8)]
            vc = scell(VT, t0, L)
            bs = alloc_bank()
            def fn_sc(e):
                ins = None
                for h in range(8):
                    ins = e.matmul(bank(bs)[0:L, h * 64:h * 64 + L], lhsT=NK.v[:, h, t0:t0 + L],
                                   rhs=QT.v[:, h, t0:t0 + L], start=True, stop=True)
                return [ins]
            tr.op("pe", fn_sc, reads=qc + kc, writes=[("ps", bs)])
            scv = SC.v[0:L, par, :].rearrange("p (h t) -> p h t", h=8, t=64)[:, :, 0:L]
            dve("tensor_tensor", [("ps", bs), "maskc"], [SC.c(par)], out=scv,
                in0=bank(bs)[0:L, :].rearrange("p (h t) -> p h t", h=8, t=64)[:, :, 0:L],
                in1=maskc[0:L, mask_idx, :].rearrange("p (h t) -> p h t", h=8, t=64)[:, :, 0:L], op=ALU.mult)
            khv = KH.v[:, par, :].rearrange("p (h t) -> p h t", h=8, t=64)[:, :, 0:L]
            dve("tensor_tensor", kc + dec_cells, [KH.c(par)], out=khv, in0=NK.v[:, :, t0:t0 + L], in1=dec_ap,
                op=ALU.mult)
            bk = alloc_bank()
            bv = alloc_bank()

            def fn_tk(e):
                ins = None
                for h in range(8):
                    ins = e.transpose(out=bank_bf(bk)[0:L, h * 128:(h + 1) * 128],
                                      in_=KH.v[:, par, h * 64:h * 64 + L], identity=identb[:])
                return [ins]
            tr.op("pe", fn_tk, reads=[KH.c(par), "identb"], writes=[("ps", bk)])

            def fn_tv(e):
                ins = None
                for h in range(8):
                    ins = e.transpose(out=bank_bf(bv)[0:L, h * 128:(h + 1) * 128],
                                      in_=VT.v[:, h, t0:t0 + L], identity=identb[:])
                return [ins]
            tr.op("pe", fn_tv, reads=vc + ["identb"], writes=[("ps", bv)])
            act(KT.v[0:L, par, :], bank_bf(bk)[0:L, :], AF.Copy, [("ps", bk)], [KT.c(par)])
            act(VTT.v[0:L, par, :], bank_bf(bv)[0:L, :], AF.Copy, [("ps", bv)], [VTT.c(par)])
            return qc

        if has_sample:
            S0b = UB("S0b", 69888, F32, [2, 1024])
            SOb = UB("SOb", 78080, F32, [2, 1024])
            KTM = UB("KTM", 86272, BF16, [1, 1024])

            def s0_load(q):
                sp_dma(S0b.v[:, q % 2, :].rearrange("p (h v) -> p h v", h=8, v=128),
                       s0[l, q].rearrange("h d v -> d h v"), [], [S0b.c(q % 2)], sem="si%d" % (q % 2))
            s0_load(0)
            s0_load(1)

        def part_a(ci):
            t0, L = chunks[ci]
            dec_ap = DEC[:, :, ci:ci + 1].broadcast_to([128, 8, L])
            return chunk_common(ci, t0, L, 0, dec_ap, [("DEC", h) for h in range(8)], ci % 2)

        def part_b(ci, qc):
            t0, L = chunks[ci]
            par = ci % 2
            bsu = alloc_bank2()

            def fn_su(e, L=L, par=par, bsu=bsu):
                ins = None
                for h in range(8):
                    ins = e.matmul(pst[bsu // 2][:, h * 128:(h + 1) * 128], lhsT=KT.v[0:L, par, h * 128:(h + 1) * 128],
                                   rhs=VTT.v[0:L, par, h * 128:(h + 1) * 128], start=True, stop=True)
                return [ins]
            tr.op("pe", fn_su, reads=[KT.c(par), VTT.c(par)], writes=[("ps", bsu), ("ps", bsu + 1)])
            bo = alloc_bank()

            def fn_o(e, t0=t0, L=L, par=par, bo=bo):
                ins = None
                for h in range(8):
                    e.matmul(bank(bo)[:, h * 64:h * 64 + L], lhsT=Sb[:, h, :], rhs=QT.v[:, h, t0:t0 + L],
                             start=True, stop=False)
                    ins = e.matmul(bank(bo)[:, h * 64:h * 64 + L], lhsT=VTT.v[0:L, par, h * 128:(h + 1) * 128],
                                   rhs=SC.v[0:L, par, h * 64:h * 64 + L], start=False, stop=True)
                return [ins]
            tr.op("pe", fn_o, reads=["Sb", VTT.c(par), SC.c(par)] + qc, writes=[("ps", bo)])
            dve("tensor_tensor", ["S"] + [("DEC", h) for h in range(8)], ["S"], out=S[:], in0=S[:],
                in1=DEC[:, :, ci:ci + 1].broadcast_to([128, 8, 128]), op=ALU.mult)
            dve("tensor_tensor", ["S", ("ps", bsu), ("ps", bsu + 1)], ["S"], out=S[:], in0=S[:],
                in1=pst[bsu // 2][:, :].rearrange("p (h v) -> p h v", h=8, v=128), op=ALU.subtract)
            act(OT.v[:, :, t0:t0 + L], bank(bo).rearrange("p (h t) -> p h t", h=8, t=64)[:, :, 0:L], AF.Copy,
                [("ps", bo)], scell(OT, t0, L))
            if ci < nch - 1:
                act(Sb[:], S[:], AF.Copy, ["S"], ["Sb"])

        qcs = {0: part_a(0)}
        for ci in range(nch):
            if ci + 1 < nch:
                qcs[ci + 1] = part_a(ci + 1)
            part_b(ci, qcs[ci])
        sp_dma(hp[l].rearrange("h d v -> d h v"), S[:], ["S"], [("hp", l)], base="st", nsem=2)

        if has_sample:
            tS = 640
            parS = nch % 2
            dec_ap = DECS[:, :, :].unsqueeze(3).broadcast_to([128, 8, 16, 4])
            qcells = scell(QT, tS, 64)
            kc = [NK.c(h) for h in range(8)]
            vc = scell(VT, tS, 64)
            bs = alloc_bank()

            def fn_sc(e):
                ins = None
                for h in range(8):
                    ins = e.matmul(bank(bs)[0:64, h * 64:(h + 1) * 64], lhsT=NK.v[:, h, tS:tS + 64],
                                   rhs=QT.v[:, h, tS:tS + 64], start=True, stop=True)
                return [ins]
            tr.op("pe", fn_sc, reads=qcells + kc, writes=[("ps", bs)])
            dve("tensor_tensor", [("ps", bs), "maskc"], [SC.c(parS)], out=SC.v[0:64, parS, :], in0=bank(bs)[0:64, :],
                in1=maskc[:, 1, :], op=ALU.mult)
            dve("tensor_tensor", kc + [("DECS", h) for h in range(8)], [KH.c(parS)],
                out=KH.v[:, parS, :].rearrange("p (h q t) -> p h q t", h=8, q=16, t=4),
                in0=NK.v[:, :, tS:tS + 64].rearrange("p h (q t) -> p h q t", q=16, t=4), in1=dec_ap, op=ALU.mult)
            bk = alloc_bank()
            bv = alloc_bank()

            def fn_tk(e):
                ins = None
                for h in range(8):
                    ins = e.transpose(out=bank_bf(bk)[0:64, h * 128:(h + 1) * 128],
                                      in_=KH.v[:, parS, h * 64:(h + 1) * 64], identity=identb[:])
                return [ins]
            tr.op("pe", fn_tk, reads=[KH.c(parS), "identb"], writes=[("ps", bk)])

            def fn_tv(e):
                ins = None
                for h in range(8):
                    ins = e.transpose(out=bank_bf(bv)[0:64, h * 128:(h + 1) * 128],
                                      in_=VT.v[:, h, tS:tS + 64], identity=identb[:])
                return [ins]
            tr.op("pe", fn_tv, reads=vc + ["identb"], writes=[("ps", bv)])
            act(KT.v[0:64, parS, :], bank_bf(bk)[0:64, :], AF.Copy, [("ps", bk)], [KT.c(parS)])
            act(VTT.v[0:64, parS, :], bank_bf(bv)[0:64, :], AF.Copy, [("ps", bv)], [VTT.c(parS)])
            bo = alloc_bank()
            held.add(bo)

            def fn_o0(e):
                e.matmul(bank(bo), lhsT=zerob[:], rhs=xb[:, 0, 0:512], start=True, stop=False)
                ins = None
                for h in range(8):
                    ins = e.matmul(bank(bo)[:, h * 64:(h + 1) * 64], lhsT=VTT.v[0:64, parS, h * 128:(h + 1) * 128],
                                   rhs=SC.v[0:64, parS, h * 64:(h + 1) * 64], start=False, stop=False)
                return [ins]
            tr.op("pe", fn_o0, reads=["zerob", VTT.c(parS), SC.c(parS)] + xcells("xb", 0) + xcells("xb", 1),
                  writes=[("ps", bo)])
            for q in range(16):

                def fn_oq(e, q=q):
                    ins = None
                    for h in range(8):
                        ins = e.matmul(bank(bo)[:, h * 64 + q * 4:h * 64 + q * 4 + 4],
                                       lhsT=S0b.v[:, q % 2, h * 128:(h + 1) * 128], rhs=QTF[:, h, q * 4:q * 4 + 4],
                                       start=False, stop=(q == 15 and h == 7))
                    return [ins]
                tr.op("pe", fn_oq, reads=[S0b.c(q % 2)] + [("QTF", h) for h in range(8)], writes=[("ps", bo)])
                dve("tensor_scalar", [KT.c(parS), "rowm"], [KTM.c(0)], out=KTM.v[0:64, 0, :], in0=KT.v[0:64, parS, :],
                    scalar1=rowm[:, q:q + 1], scalar2=None, op0=ALU.mult)
                bsu = alloc_bank2()

                def fn_su(e, bsu=bsu):
                    ins = None
                    for h in range(8):
                        ins = e.matmul(pst[bsu // 2][:, h * 128:(h + 1) * 128],
                                       lhsT=KTM.v[0:64, 0, h * 128:(h + 1) * 128],
                                       rhs=VTT.v[0:64, parS, h * 128:(h + 1) * 128], start=True, stop=True)
                    return [ins]
                tr.op("pe", fn_su, reads=[KTM.c(0), VTT.c(parS)], writes=[("ps", bsu), ("ps", bsu + 1)])
                sov = SOb.v[:, q % 2, :].rearrange("p (h v) -> p h v", h=8, v=128)
                dve("tensor_tensor", [S0b.c(q % 2)] + [("DECS", h) for h in range(8)], [SOb.c(q % 2)], out=sov,
                    in0=S0b.v[:, q % 2, :].rearrange("p (h v) -> p h v", h=8, v=128),
                    in1=DECS[:, :, q:q + 1].broadcast_to([128, 8, 128]), op=ALU.mult)
                dve("tensor_tensor", [SOb.c(q % 2), ("ps", bsu), ("ps", bsu + 1)], [SOb.c(q % 2)], out=sov, in0=sov,
                    in1=pst[bsu // 2][:, :].rearrange("p (h v) -> p h v", h=8, v=128), op=ALU.subtract)
                if q + 2 < 16:
                    s0_load(q + 2)
                sp_dma(hs[l, q].rearrange("h d v -> d h v"), sov, [SOb.c(q % 2)], [], sem="so%d" % (q % 2))
            act(OT.v[:, :, tS:tS + 64], bank(bo).rearrange("p (h t) -> p h t", h=8, t=64), AF.Copy,
                [("ps", bo)], scell(OT, tS, 64))
            held.discard(bo)

        SQ2 = UB("SQ2", 23040, BF16, [4, 360])
        RS = UB("RS", 25920, F32, [4, 360])
        NRB = 4
        O2 = UB("O2", 34560, BF16, [8, GT])
        SGO = UB("SGO", 31680, F32, [2, 360])
        items = [(h, s) for h in range(8) for s in range(2)]

        def rms_sq(ii):
            h, s = items[ii]
            t0, n = subs[s]
            act(SQ2.v[:, ii % 4, 0:n], OT.v[:, h, t0:t0 + n], AF.Square, [OT.c(h, s)], [SQ2.c(ii % 4)])
        rms_sq(0)
        rms_sq(1)
        for ii, (h, s) in enumerate(items):
            t0, n = subs[s]
            p2 = ii % 4
            if ii + 2 < len(items):
                rms_sq(ii + 2)
            b = alloc_bank()
            mm_group(bank(b, n), [(ones128[:], SQ2.v[:, p2, 0:n])], ["ones128", SQ2.c(p2)], [("ps", b)])
            act(RS.v[:, p2, 0:n], bank(b, n), AF.Sqrt, [("ps", b), "epsr"], [RS.c(p2)], bias=epsr[:], scale=1.0)
            dve("reciprocal", [RS.c(p2)], [RS.c(p2)], out=RS.v[:, p2, 0:n], in_=RS.v[:, p2, 0:n])
            dve("tensor_tensor", [RS.c(p2), OT.c(h, s)], [OT.c(h, s)], out=OT.v[:, h, t0:t0 + n],
                in0=OT.v[:, h, t0:t0 + n], in1=RS.v[:, p2, 0:n], op=ALU.mult)
        sgi = [0]
        for hh in (0, 1):
            def cons_og(i, s, t0, n, b, hh=hh):
                h = 4 * hh + i
                p2 = sgi[0] % 2
                sgi[0] += 1
                act(SGO.v[:, p2, 0:n], bank(b, n), AF.Silu, [("ps", b)], [SGO.c(p2)])
                dve("scalar_tensor_tensor", [SGO.c(p2), OT.c(h, s), "normw"], [O2.c(h, s)],
                    out=O2.v[:, h, t0:t0 + n], in0=OT.v[:, h, t0:t0 + n], scalar=normw[:, l:l + 1],
                    in1=SGO.v[:, p2, 0:n], op0=ALU.mult, op1=ALU.mult)
            dense8([0, 128, 256, 384], subs, cons_og)

        TGA = UB("TGA", 46080, BF16, [8, GT])
        for hh in (0, 1):
            def cons_ga(i, s, t0, n, b, hh=hh):
                c = 4 * hh + i
                act(TGA.v[:, c, t0:t0 + n], bank(b, n), AF.Tanh, [("ps", b)], [TGA.c(c, s)], scale=0.5)
            dense8([0, 128, 256, 384], subs, cons_ga)
        MA = UB("MA", 0, F32, [8, GT])
        for hh in (0, 1):
            def cons_wa(i, s, t0, n, b, hh=hh):
                c = 4 * hh + i
                dve("scalar_tensor_tensor", [("ps", b), TGA.c(c, s)], [MA.c(c, s)], out=MA.v[:, c, t0:t0 + n],
                    in0=TGA.v[:, c, t0:t0 + n], scalar=1.0, in1=bank(b, n), op0=ALU.add, op1=ALU.mult)
            dense8_from(O2, lambda k, s: O2.c(k, s), subs, cons_wa)

        U2 = UB("U2", 23040, F32, [8, GT + 2])
        CV = UB("CV", 57664, F32, [8, GT])
        CGT = UB("CGT", 80704, F32, [2, 360])
        Gp = 640 if has_sample else G
        for c in range(8):
            pass
        dve("tensor_copy", ["ubuf"], [U2.c("halo")], out=U2.v[:, :, 0:2], in_=ubuf[:, l, :, :])
        cgi = [0]
        for j in range(4):
            def cons_ch(i, s, t0, n, b, j=j):
                c = 2 * j + (i % 2)
                if i < 2:
                    act(CGT.v[:, i, 0:n], bank(b, n), AF.Copy, [("ps", b)], [CGT.c(i)])
                else:
                    dve("tensor_tensor", [("ps", b), CGT.c(i - 2)], [U2.c(c, s)], out=U2.v[:, c, 2 + t0:2 + t0 + n],
                        in0=bank(b, n), in1=CGT.v[:, i - 2, 0:n], op=ALU.mult)
            dense8([0, 128, 256, 384], subs, cons_ch)
        for c in range(8):
            uc = [U2.c(c, 0), U2.c(c, 1), U2.c("halo")]
            dve("tensor_scalar", uc + ["convw"], [CV.c(c)], out=CV.v[:, c, 0:Gp], in0=U2.v[:, c, 0:Gp],
                scalar1=convw[:, l, 0, c:c + 1], scalar2=None, op0=ALU.mult)
            dve("scalar_tensor_tensor", uc + ["convw", CV.c(c)], [CV.c(c)], out=CV.v[:, c, 0:Gp],
                in0=U2.v[:, c, 1:Gp + 1], scalar=convw[:, l, 1, c:c + 1], in1=CV.v[:, c, 0:Gp],
                op0=ALU.mult, op1=ALU.add)
            dve("scalar_tensor_tensor", uc + ["convw", CV.c(c)], [CV.c(c)], out=CV.v[:, c, 0:Gp],
                in0=U2.v[:, c, 2:Gp + 2], scalar=convw[:, l, 2, c:c + 1], in1=CV.v[:, c, 0:Gp],
                op0=ALU.mult, op1=ALU.add)
        allu = [U2.c(c, s) for c in range(8) for s in range(2)] + [U2.c("halo")]
        allcv = [CV.c(c) for c in range(8)]
        dve("tensor_copy", allu, ["ubuf"], out=ubuf[:, l, :, :], in_=U2.v[:, :, Gp:Gp + 2])
        if has_sample:
            act(CPO[:, l, :, :], U2.v[:, :, Gp:Gp + 2], AF.Copy, allu, [("CPO", l)])
            dve("tensor_copy", ["CB"], ["US"], out=US[:, :, :, 0:2], in_=CB[:, l, :, :, :])
            dve("tensor_copy", allu + ["US"], ["US"], out=US[:, :, :, 2:6],
                in_=U2.v[:, :, 2 + 640:2 + 704].rearrange("p c (q t) -> p c q t", q=16, t=4))
            act(CSO[:, l, :, :, :], US[:, :, :, 4:6], AF.Copy, ["US"], [("CSO", l)])
            for jx in range(3):
                wbc = convw[:, l, jx, :].unsqueeze(2).unsqueeze(3).broadcast_to([128, 8, 16, 4])
                if jx == 0:
                    dve("tensor_tensor", ["US", "convw"], ["CVS"], out=CVS[:], in0=US[:, :, :, 0:4], in1=wbc,
                        op=ALU.mult)
                else:
                    dve("tensor_tensor", ["US", "convw"], ["CVS2"], out=CVS2[:], in0=US[:, :, :, jx:jx + 4], in1=wbc,
                        op=ALU.mult)
                    dve("tensor_tensor", ["CVS", "CVS2"], ["CVS"], out=CVS[:], in0=CVS[:], in1=CVS2[:], op=ALU.add)
            dve("tensor_copy", ["CVS"] + allcv, allcv, out=CV.v[:, :, 640:704],
                in_=CVS[:].rearrange("p c q t -> p c (q t)"))
        TGB = UB("TGB", 46144, BF16, [8, GT])
        for hh in (0, 1):
            def cons_gb(i, s, t0, n, b, hh=hh):
                c = 4 * hh + i
                act(TGB.v[:, c, t0:t0 + n], bank(b, n), AF.Tanh, [("ps", b)], [TGB.c(c, s)], scale=0.5)
            dense8([0, 128, 256, 384], subs, cons_gb)
        BC = UB("BC", 23040, BF16, [8, GT])
        for hh in (0, 1):
            def cons_bg(i, s, t0, n, b, hh=hh):
                c = 4 * hh + i
                dve("tensor_tensor", [("ps", b)] + allcv, [BC.c(c, s)], out=BC.v[:, c, t0:t0 + n], in0=bank(b, n),
                    in1=CV.v[:, c, t0:t0 + n], op=ALU.mult)
            dense8([0, 128, 256, 384], subs, cons_bg)
        M = UB("M", 34560, BF16, [8, GT])
        YTMP = UB("YTMP", 57664, F32, [2, 360])
        yi = [0]
        for hh in (0, 1):
            def cons_wb(i, s, t0, n, b, hh=hh):
                c = 4 * hh + i
                p2 = yi[0] % 2
                yi[0] += 1
                dve("scalar_tensor_tensor", [("ps", b), TGB.c(c, s)], [YTMP.c(p2)], out=YTMP.v[:, p2, 0:n],
                    in0=TGB.v[:, c, t0:t0 + n], scalar=1.0, in1=bank(b, n), op0=ALU.add, op1=ALU.mult)
                dve("tensor_tensor", [YTMP.c(p2), MA.c(c, s)], [M.c(c, s)], out=M.v[:, c, t0:t0 + n],
                    in0=YTMP.v[:, p2, 0:n], in1=MA.v[:, c, t0:t0 + n], op=ALU.add)
            dense8_from(BC, lambda k, s: BC.c(k, s), subs, cons_wb)
        def cons_wo(c, s, t0, n, b):
            dve("scalar_tensor_tensor", [("ps", b), ("xf", c, s)], [("xf", c, s)], out=xf[:, c, t0:t0 + n],
                in0=bank(b, n), scalar=CR, in1=xf[:, c, t0:t0 + n], op0=ALU.mult, op1=ALU.add)
        slA, wvA = next_block()
        slB, wvB = next_block()
        for s, (t0, n) in enumerate(subs):
            for hh, (sl, wv) in enumerate(((slA, wvA), (slB, wvB))):
                for i in range(4):
                    b = alloc_bank()
                    mm_group(bank(b, n), [(wv[:, k, i * 128:(i + 1) * 128], M.v[:, k, t0:t0 + n]) for k in range(8)],
                             [("w", sl)] + [M.c(k, s) for k in range(8)], [("ps", b)])
                    cons_wo(4 * hh + i, s, t0, n, b)
                if s == 1 and hh == 0:
                    ln_sub(l, 1, subs, 0, 46080, 57600, 69120)
        prefetch()
        ln_sub(l, 1, subs, 1, 46080, 57600, 69120)

    for g in range(NG):
        g0, G = GROUPS[g]
        n = G // 2
        subs = [(0, n), (n, G - n)]
        sp_dma(xf[:, :, 0:G], xT[:, :, g0:g0 + G], [], [("xf", c, s) for c in range(8) for s in range(2)])
        sp_dma(Et[:], cE[:, g, :], [], ["Et"])
        dve("tensor_scalar", ["Et"], ["nEt"], out=nEt[:], in0=Et[:], scalar1=-1.0, scalar2=1.0, op0=ALU.mult,
            op1=ALU.add)
        for s, (t0, nn) in enumerate(subs):
            act(xb[:, :, t0:t0 + nn], xf[:, :, t0:t0 + nn], AF.Copy, xcells("xf", s), xcells("xb", s))
        for l in range(NL):
            ffn(l, 0, subs)
            if DEBUG["stop"] == "ffn1":
                break
            mixer(g, l, subs)
            if DEBUG["stop"] == "mixer":
                break
            ffn(l, 1, subs)
        allxf = [("xf", c, s) for c in range(8) for s in range(2)]
        if g == 0:
            sp_dma(yT[:, :, 0:G - 16], xf[:, :, 16:G], allxf, [], base="yo", nsem=1)
        else:
            sp_dma(yT[:, :, g0 - 16:g0 - 16 + G], xf[:, :, 0:G], allxf, [], base="yo", nsem=1)
    if DEBUG["stop"] is None:
      sp_dma(cso[:, 0:NL], CSO[:, 0:NL], [("CSO", l) for l in range(NL)], [], base="yo", nsem=1)
      sp_dma(cpo[:, 0:NL], CPO[:, 0:NL], [("CPO", l) for l in range(NL)], [], base="yo", nsem=1)

    cum = {}
    for en in Tr.ENGS:
        c = 0
        arr = []
        for o in tr.ops[en]:
            if o["signal"]:
                c += 1
            arr.append(c)
        cum[en] = arr
    dsems = sorted(tr.dval.keys())
    semctx = {}
    for en in Tr.ENGS:
        cm = nc.semaphore("e_" + en)
        semctx[("e", en)] = cm.__enter__()
        ctxs.append(cm)
    for ds in dsems:
        cm = nc.semaphore("d_" + ds)
        semctx[("d", ds)] = cm.__enter__()
        ctxs.append(cm)

    def replay(en, e):
        my = semctx[("e", en)]
        for o in tr.ops[en]:
            for (k, v) in o["waits"]:
                if k[0] == "e":
                    e.wait_ge(semctx[k], cum[k[1]][v])
                else:
                    e.wait_ge(semctx[k], v)
            inss = o["fn"](e)
            if o["dma"] is not None:
                for ins in inss:
                    ins.then_inc(semctx[("d", o["dma"])], 16)
            elif o["signal"]:
                inss[-1].then_inc(my, 1)
        if en == "sp":
            for ds in dsems:
                e.wait_ge(semctx[("d", ds)], tr.dval[ds])

    with nc.Block() as block:
        @block.tensor
        def _(e):
            replay("pe", e)

        @block.scalar
        def _(e):
            replay("act", e)

        @block.vector
        def _(e):
            replay("dve", e)

        @block.gpsimd
        def _(e):
            replay("pool", e)

        @block.sync
        def _(e):
            replay("sp", e)
    for cm in reversed(ctxs):
        cm.__exit__(None, None, None)
    stats = {en: len(tr.ops[en]) for en in Tr.ENGS}
    return nc, stats


def host_consts():
    ident = np.eye(128, dtype=np.float32)
    s_ = np.arange(64)[:, None]
    t_ = np.arange(64)[None, :]
    mc = np.where(s_ <= t_, -1.0, 0.0).astype(np.float32)
    ms = np.where((s_ <= t_) & (s_ // 4 == t_ // 4), -1.0, 0.0).astype(np.float32)
    cmask = np.zeros((64, 2, 512), np.float32)
    cmask[:, 0, :] = np.tile(mc, (1, 8))
    cmask[:, 1, :] = np.tile(ms, (1, 8))
    crow = (np.arange(64)[:, None] // 4 == np.arange(16)[None, :]).astype(np.float32)
    E = np.zeros((3, GT), np.float32)
    E[0, 0] = 1.0
    E[0, 16::64] = 1.0
    E[1, 0:704:64] = 1.0
    E[2, 0:640:64] = 1.0
    E[2, 640:704:4] = 1.0
    cE = np.ascontiguousarray(np.broadcast_to(E[None], (128, 3, GT)))
    return ident, cmask, crow, cE


_CACHE = {}


def kernel(x_prompt, x_sample, state_hgrn, state_conv, meta_tokens, w_in, hgrn_lb_logits,
           hgrn_norm_w, conv_w, w_branch_a, w_branch_b, w_out, ffn1_w_gu, ffn1_w_down,
           ffn2_w_gu, ffn2_w_down, ln_g, ln_b):
    f32 = np.float32
    x_prompt = np.asarray(x_prompt, f32)
    x_sample = np.asarray(x_sample, f32)
    state_hgrn = np.asarray(state_hgrn, f32)
    state_conv = np.asarray(state_conv, f32)
    meta = np.asarray(meta_tokens, f32)
    nc, stats = build_program()
    ident, cmask, crow, cE = host_consts()

    def fm(a):
        a = np.asarray(a, f32)
        sh = a.shape[:-1]
        return np.ascontiguousarray(np.moveaxis(a.reshape(sh + (8, 128)), -1, 0))

    def dn_layout(w):
        w = np.asarray(w, f32).reshape(4, NJ, 128, 8, 128)
        return np.ascontiguousarray(w.transpose(0, 3, 2, 1, 4))

    shared = {
        "w_in": np.ascontiguousarray(np.asarray(w_in, f32)),
        "w_a": np.ascontiguousarray(np.asarray(w_branch_a, f32)),
        "w_b": np.ascontiguousarray(np.asarray(w_branch_b, f32)),
        "w_o": np.ascontiguousarray(np.asarray(w_out, f32)),
        "f1gu": np.ascontiguousarray(np.asarray(ffn1_w_gu, f32)),
        "f2gu": np.ascontiguousarray(np.asarray(ffn2_w_gu, f32)),
        "f1d": dn_layout(ffn1_w_down),
        "f2d": dn_layout(ffn2_w_down),
        "plng": fm(ln_g), "plnb": fm(ln_b), "plog": fm(hgrn_lb_logits), "pconv": fm(conv_w),
        "pnorm": np.ascontiguousarray(np.asarray(hgrn_norm_w, f32).T),
        "cident": ident, "cmask": cmask, "crow": crow, "cE": cE,
    }
    in_maps = []
    for c in range(8):
        toks = np.concatenate([meta, x_prompt[c], x_sample[16 * c:16 * c + 16].reshape(64, D)], axis=0)
        xT = np.ascontiguousarray(toks.reshape(NTOK, 8, 128).transpose(2, 1, 0))
        m = dict(shared)
        m["xT"] = xT
        m["s0"] = np.ascontiguousarray(state_hgrn[:, 16 * c:16 * c + 16])
        sc = state_conv[:, 16 * c:16 * c + 16]
        m["cb"] = np.ascontiguousarray(sc.reshape(4, 16, 2, 8, 128).transpose(4, 0, 3, 1, 2))
        in_maps.append(m)
    res = run_bass_kernel_spmd(nc, in_maps, core_ids=list(range(8)))
    y_prompt = np.zeros((8, 2048, D), f32)
    y_sample = np.zeros((128, 4, D), f32)
    nhp = np.zeros((4, 8, 8, 128, 128), f32)
    ncp = np.zeros((4, 8, 2, 1024), f32)
    nhs = np.zeros((4, 128, 8, 128, 128), f32)
    ncs = np.zeros((4, 128, 2, 1024), f32)
    for c in range(8):
        r = res.results[c]
        y = np.asarray(r["yT"]).transpose(2, 1, 0).reshape(NTOK - 16, D)
        y_prompt[c] = y[0:2048]
        y_sample[16 * c:16 * c + 16] = y[2048:].reshape(16, 4, D)
        nhp[:, c] = np.asarray(r["hp"])
        nhs[:, 16 * c:16 * c + 16] = np.asarray(r["hs"])
        cso = np.asarray(r["cso"])
        ncs[:, 16 * c:16 * c + 16] = cso.transpose(1, 3, 4, 2, 0).reshape(4, 16, 2, 1024)
        cpo = np.asarray(r["cpo"])
        ncp[:, c] = cpo.transpose(1, 3, 2, 0).reshape(4, 2, 1024)
    return (y_prompt, y_sample, nhp, ncp, nhs, ncs)
```
